# Optimizing a Trainium2 kernel written in Bass

```python
import math
import jax, jax.numpy as jnp
from jax import lax
import numpy as np

D_MODEL = 2048
BATCH = 4
SEQ = 4096
DEPTH = 1

CONV_WIDTH = D_MODEL // 2
SSM_WIDTH = D_MODEL // 2
CONV_KERNEL = 31
SSM_GROUP = 16
SSM_GROUPS = SSM_WIDTH // SSM_GROUP
SSM_STATE = 64
D_FF = 4 * D_MODEL
N_MOD = 6
IN_COLS = 2 * CONV_WIDTH + SSM_WIDTH + 2 * D_MODEL
EPS = 1e-6
DT_MIN = 1e-3
DT_MAX = 1e-1

kernel_name = "hybrid_conformer_s5_gated_block"


def rmsnorm(x, g):
    xf = x.astype(jnp.float32)
    y = xf * lax.rsqrt(jnp.mean(jnp.square(xf), axis=-1, keepdims=True) + EPS)
    return (y * g.astype(jnp.float32)).astype(x.dtype)


def layernorm(x, g, b):
    xf = x.astype(jnp.float32)
    mu = jnp.mean(xf, axis=-1, keepdims=True)
    var = jnp.mean(jnp.square(xf - mu), axis=-1, keepdims=True)
    y = (xf - mu) * lax.rsqrt(var + EPS)
    return (y * g.astype(jnp.float32) + b.astype(jnp.float32)).astype(x.dtype)


def conformer_conv(v_glu, w_dw, b_dw, ln_g, ln_b, w_conv_out):
    a, gate = jnp.split(v_glu, 2, axis=-1)
    v = a * jax.nn.sigmoid(gate)
    v = lax.conv_general_dilated(
        v, w_dw[:, None, :], window_strides=(1,), padding=[(CONV_KERNEL - 1, 0)],
        dimension_numbers=("NWC", "WIO", "NWC"), feature_group_count=CONV_WIDTH,
    ) + b_dw
    v = jax.nn.silu(layernorm(v, ln_g, ln_b))
    return v @ w_conv_out


def s5_ssm(u, a_re, a_im, log_dt, b_re, b_im, c_re, c_im, d_skip, w_glu_a, w_glu_b):
    bsz, seq, _ = u.shape
    uf = u.astype(jnp.float32)
    ug = uf.reshape(bsz, seq, SSM_GROUPS, SSM_GROUP)
    lam = lax.complex(a_re.astype(jnp.float32), a_im.astype(jnp.float32))
    dt = jnp.exp(log_dt.astype(jnp.float32))[:, None]
    lam_bar = jnp.exp(lam * dt)
    b_c = lax.complex(b_re.astype(jnp.float32), b_im.astype(jnp.float32))
    b_bar = ((lam_bar - 1.0) / lam)[..., None] * b_c
    bu = jnp.einsum("blgh,gph->blgp", ug.astype(b_bar.dtype), b_bar)
    a_seq = jnp.broadcast_to(lam_bar, bu.shape)

    def combine(e1, e2):
        a1, s1 = e1
        a2, s2 = e2
        return a1 * a2, a2 * s1 + s2

    _, states = lax.associative_scan(combine, (a_seq, bu), axis=1)
    c_c = lax.complex(c_re.astype(jnp.float32), c_im.astype(jnp.float32))
    y = jnp.real(jnp.einsum("blgp,ghp->blgh", states, c_c))
    y = y.reshape(bsz, seq, SSM_WIDTH) + d_skip.astype(jnp.float32) * uf
    y = jax.nn.gelu(y).astype(u.dtype)
    return (y @ w_glu_a) * jax.nn.sigmoid(y @ w_glu_b)


def setup_inputs(seed: int = 0) -> dict:
    key = jax.random.key(seed)
    ks = jax.random.split(key, 32)
    f32 = jnp.float32

    def nrm(k, shape, scale):
        return jax.random.normal(k, shape, f32) * scale

    x = jax.random.normal(ks[0], (BATCH, SEQ, D_MODEL), f32)
    c = jax.random.normal(ks[1], (BATCH, D_MODEL), f32)
    w_ada = nrm(ks[2], (DEPTH, D_MODEL, N_MOD * D_MODEL), 0.5 * D_MODEL ** -0.5)
    b_ada = nrm(ks[3], (DEPTH, N_MOD * D_MODEL), 0.02)
    norm1_g = 1.0 + nrm(ks[4], (DEPTH, D_MODEL), 0.02)
    w_in = nrm(ks[5], (DEPTH, D_MODEL, IN_COLS), D_MODEL ** -0.5)
    w_dw = nrm(ks[6], (DEPTH, CONV_KERNEL, CONV_WIDTH), CONV_KERNEL ** -0.5)
    b_dw = nrm(ks[7], (DEPTH, CONV_WIDTH), 0.02)
    ln_g = 1.0 + nrm(ks[8], (DEPTH, CONV_WIDTH), 0.02)
    ln_b = nrm(ks[9], (DEPTH, CONV_WIDTH), 0.02)
    w_conv_out = nrm(ks[10], (DEPTH, CONV_WIDTH, D_MODEL), CONV_WIDTH ** -0.5)
    n_idx = jnp.arange(SSM_STATE, dtype=f32)
    a_re = -0.5 + nrm(ks[11], (DEPTH, SSM_GROUPS, SSM_STATE), 0.01)
    a_im = math.pi * n_idx + nrm(ks[12], (DEPTH, SSM_GROUPS, SSM_STATE), 0.01)
    log_dt = jax.random.uniform(ks[13], (DEPTH, SSM_GROUPS), f32, math.log(DT_MIN), math.log(DT_MAX))
    b_scale = (2.0 * SSM_GROUP) ** -0.5
    b_re = nrm(ks[14], (DEPTH, SSM_GROUPS, SSM_STATE, SSM_GROUP), b_scale)
    b_im = nrm(ks[15], (DEPTH, SSM_GROUPS, SSM_STATE, SSM_GROUP), b_scale)
    c_scale = (2.0 * SSM_STATE) ** -0.5
    c_re = nrm(ks[16], (DEPTH, SSM_GROUPS, SSM_GROUP, SSM_STATE), c_scale)
    c_im = nrm(ks[17], (DEPTH, SSM_GROUPS, SSM_GROUP, SSM_STATE), c_scale)
    d_skip = nrm(ks[18], (DEPTH, SSM_WIDTH), 1.0)
    w_glu_a = nrm(ks[19], (DEPTH, SSM_WIDTH, D_MODEL), SSM_WIDTH ** -0.5)
    w_glu_b = nrm(ks[20], (DEPTH, SSM_WIDTH, D_MODEL), SSM_WIDTH ** -0.5)
    w_out = nrm(ks[21], (DEPTH, D_MODEL, D_MODEL), D_MODEL ** -0.5)
    norm2_g = 1.0 + nrm(ks[22], (DEPTH, D_MODEL), 0.02)
    w_ff1 = nrm(ks[23], (DEPTH, D_MODEL, D_FF), D_MODEL ** -0.5)
    w_ff2 = nrm(ks[24], (DEPTH, D_FF, D_MODEL), D_FF ** -0.5)
    final_g = 1.0 + nrm(ks[25], (D_MODEL,), 0.02)
    return {
        "x": x, "c": c, "w_ada": w_ada, "b_ada": b_ada, "norm1_g": norm1_g, "w_in": w_in,
        "w_dw": w_dw, "b_dw": b_dw, "ln_g": ln_g, "ln_b": ln_b, "w_conv_out": w_conv_out,
        "a_re": a_re, "a_im": a_im, "log_dt": log_dt, "b_re": b_re, "b_im": b_im,
        "c_re": c_re, "c_im": c_im, "d_skip": d_skip, "w_glu_a": w_glu_a, "w_glu_b": w_glu_b,
        "w_out": w_out, "norm2_g": norm2_g, "w_ff1": w_ff1, "w_ff2": w_ff2, "final_g": final_g,
    }


def reference(x, c, w_ada, b_ada, norm1_g, w_in, w_dw, b_dw, ln_g, ln_b, w_conv_out,
              a_re, a_im, log_dt, b_re, b_im, c_re, c_im, d_skip, w_glu_a, w_glu_b,
              w_out, norm2_g, w_ff1, w_ff2, final_g):
    h = x
    c_act = jax.nn.silu(c)
    split_pts = [2 * CONV_WIDTH, 2 * CONV_WIDTH + SSM_WIDTH, 2 * CONV_WIDTH + SSM_WIDTH + D_MODEL]
    for l in range(DEPTH):
        mod = c_act @ w_ada[l] + b_ada[l]
        shift1, scale1, gate1, shift2, scale2, gate2 = [m[:, None, :] for m in jnp.split(mod, N_MOD, axis=-1)]

        u = rmsnorm(h, norm1_g[l]) * (1.0 + scale1) + shift1
        proj = u @ w_in[l]
        v_conv, v_ssm, g_conv, g_ssm = jnp.split(proj, split_pts, axis=-1)
        y_conv = conformer_conv(v_conv, w_dw[l], b_dw[l], ln_g[l], ln_b[l], w_conv_out[l])
        y_ssm = s5_ssm(v_ssm, a_re[l], a_im[l], log_dt[l], b_re[l], b_im[l], c_re[l], c_im[l],
                       d_skip[l], w_glu_a[l], w_glu_b[l])
        merged = jax.nn.sigmoid(g_conv) * y_conv + jax.nn.sigmoid(g_ssm) * y_ssm
        h = h + gate1 * (merged @ w_out[l])

        z = rmsnorm(h, norm2_g[l]) * (1.0 + scale2) + shift2
        ff = jnp.square(jax.nn.relu(z @ w_ff1[l])) @ w_ff2[l]
        h = h + gate2 * ff
    return rmsnorm(h, final_g)
```

```python
import math
import numpy as np
from contextlib import ExitStack
import concourse.bass as bass
import concourse.mybir as mybir
from concourse.bass_utils import run_bass_kernel_spmd

F32, BF16 = mybir.dt.float32, mybir.dt.bfloat16
AF = mybir.ActivationFunctionType
ALU = mybir.AluOpType

NCORE = 8
D = 2048
NTOK = 2048
NT = 512
NTILE = NTOK // NT
SUB = 128
NSUB = NT // SUB
MC = SUB // 4
NCH = MC // 4
NSLOT = 2
EPS = 1e-6
PI = math.pi

PC = {}
_off = 0
for _n, _w in [("cvec", 16), ("bada", 96), ("n1g", 16), ("n2g", 16), ("fg", 16), ("wdw", 248),
               ("bdw", 8), ("lng", 8), ("lnb", 8), ("dsk", 8), ("flag", 1), ("mB", 2), ("mC", 2),
               ("ident", 128)]:
    PC[_n] = (_off, _w)
    _off += _w
NPAR = _off


class Op:
    __slots__ = ("eng", "fn", "deps", "needs", "val", "dsem", "dval")


class Prog:
    ENG = ("pe", "act", "dve", "pool", "sp")

    def __init__(self):
        self.ops = {e: [] for e in self.ENG}
        self.lastw = {}
        self.readers = {}
        self.dmacount = {}

    def add(self, eng, fn, reads=(), writes=(), dsem=None):
        op = Op()
        op.eng, op.fn, op.needs, op.val, op.dsem, op.dval = eng, fn, False, None, dsem, None
        if dsem is not None:
            self.dmacount[id(dsem)] = self.dmacount.get(id(dsem), 0) + 16
            op.dval = self.dmacount[id(dsem)]
        deps, seen = [], set()
        cand = []
        for k in reads:
            w = self.lastw.get(k)
            if w is not None:
                cand.append(w)
        for k in writes:
            w = self.lastw.get(k)
            if w is not None:
                cand.append(w)
            cand.extend(self.readers.get(k, ()))
        for d in cand:
            if id(d) in seen:
                continue
            seen.add(id(d))
            if d.dsem is None and d.eng == eng and eng == "pe":
                continue
            deps.append(d)
        op.deps = deps
        for k in reads:
            self.readers.setdefault(k, []).append(op)
        for k in writes:
            self.lastw[k] = op
            self.readers[k] = []
        self.ops[eng].append(op)
        return op

    def emit(self, block, sems):
        for e in self.ENG:
            for op in self.ops[e]:
                for d in op.deps:
                    if d.dsem is None:
                        d.needs = True
        for e in self.ENG:
            n = 0
            for op in self.ops[e]:
                if op.dsem is None and op.needs:
                    n += 1
                    op.val = n

        def run(engname, eobj):
            known = {}
            for op in self.ops[engname]:
                need = {}
                for d in op.deps:
                    if d.dsem is not None:
                        sem, val = d.dsem, d.dval
                    else:
                        sem, val = sems[d.eng], d.val
                    if need.get(id(sem), (None, 0))[1] < val:
                        need[id(sem)] = (sem, val)
                for sid, (sem, val) in need.items():
                    if known.get(sid, 0) >= val:
                        continue
                    eobj.wait_ge(sem, val)
                    known[sid] = val
                ins = op.fn(eobj)
                if ins is None:
                    continue
                if op.dsem is not None:
                    ins.then_inc(op.dsem, 16)
                elif op.needs:
                    ins.then_inc(sems[engname], 1)

        block.tensor(lambda e: run("pe", e))
        block.scalar(lambda e: run("act", e))
        block.vector(lambda e: run("dve", e))
        block.gpsimd(lambda e: run("pool", e))
        block.sync(lambda e: run("sp", e))


def build():
    nc = bass.Bass("TRN2", target_bir_lowering=False)

    def din(name, shape):
        return nc.dram_tensor(name, list(shape), F32, kind="ExternalInput").ap()

    xT = din("xT", [D, NTOK])
    xpT = din("xpT", [D, NTOK])
    par_d = din("par", [128, NPAR])
    ssmB_d = din("ssmB", [128, 5, 512])
    ssmC_d = din("ssmC", [128, 7, 512])
    w_ada = din("w_ada", [D, 6 * D])
    w_in = din("w_in", [D, 7168])
    w_co = din("w_conv_out", [1024, D])
    w_ga = din("w_glu_a", [1024, D])
    w_gb = din("w_glu_b", [1024, D])
    w_out = din("w_out", [D, D])
    w_ff1 = din("w_ff1", [D, 4 * D])
    w_ff2 = din("w_ff2", [4 * D, D])
    outT = nc.dram_tensor("outT", [D, NTOK], F32, kind="ExternalOutput").ap()
    WSRC = {"w_in": w_in, "w_co": w_co, "w_ga": w_ga, "w_gb": w_gb, "w_out": w_out, "w_ff1": w_ff1}
    WBF = {k: nc.dram_tensor("bf_" + k, list(a.shape), BF16, kind="Internal").ap() for k, a in WSRC.items()}
    WNAME = {id(a): k for k, a in WSRC.items()}

    es = ExitStack()
    with es:
        def sb(name, shape, dt=F32):
            return es.enter_context(nc.sbuf_tensor(name, list(shape), dt))

        par = sb("par_sb", [128, NPAR])
        modv = sb("modv", [128, 96])
        g1s = sb("g1s", [128, 16])
        g2s = sb("g2s", [128, 16])
        W2 = sb("W2", [128, 8, 4, 2, 128], BF16)
        W4 = sb("W4", [128, 32, 4, 2, 32], BF16)
        IT = sb("IT", [128, 8, 4, 128], BF16)
        CA = sb("CA", [128, 7, 32, 2])
        CB = sb("CB", [128, 7, 32, 2])
        cab = sb("cab", [128, 16], BF16)
        onesb = sb("onesb", [128, 128], BF16)
        onesf = sb("onesf", [128, 128])
        zerob = sb("zerob", [128, 128], BF16)
        wslot = [sb(f"wslot{i}", [128, 8192], BF16) for i in range(NSLOT)]
        ps = es.enter_context(nc.psum_tensor("ps", [128, 8, 512], F32))

        def pcol(name, a=0, b=None):
            o, w = PC[name]
            b = w if b is None else b
            return par[:, o + a:o + b]

        ring = {"i": 0}
        bankc = {"i": 0}
        CVPARTS = {"w_in": 7, "w_co": 1, "w_ga": 1, "w_gb": 1, "w_out": 2, "w_ff1": 8}
        cvsem = {nm: [es.enter_context(nc.semaphore(f"cv_{nm}_{i}")) for i in range(n)] for nm, n in CVPARTS.items()}

        def cvt_piece(prog_, nm, i):
            srcv = WSRC[nm].rearrange("k (a c) -> (k a) c", c=1024)
            dstv = WBF[nm].rearrange("k (a c) -> (k a) c", c=1024)
            step = srcv.shape[0] // CVPARTS[nm]
            a, b = i * step, (i + 1) * step
            prog_.add("pool", lambda e: e.dma_start(out=dstv[a:b, :], in_=srcv[a:b, :]), [], [("cvt", nm, i)], dsem=cvsem[nm][i])

        def bank():
            b = bankc["i"] % 8
            bankc["i"] += 1
            return b

        with ExitStack() as e1:
            def sb1(name, shape, dt=F32):
                return e1.enter_context(nc.sbuf_tensor(name, list(shape), dt))

            def sem1(name):
                return e1.enter_context(nc.semaphore(name))

            prog = Prog()
            sems = {k: sem1("s1_" + k) for k in ("pe", "act", "dve", "pool")}
            wsem = [sem1(f"s1_w{i}") for i in range(NSLOT)]
            ldsem = [sem1(f"s1_ld{i}") for i in range(3)]
            sB = sb1("sB", [128, 5, 512])
            sC = sb1("sC", [128, 7, 512])
            tmps = {}

            def T(name):
                if name not in tmps:
                    tmps[name] = sb1("t_" + name, [128, 512])
                return tmps[name]

            def ap_of(x):
                return T(x)[:] if isinstance(x, str) else x[0]

            def key_of(x):
                return ("t", x) if isinstance(x, str) else x[1]

            def tt(out, a, b, op, eng="dve"):
                o, x, y = ap_of(out), ap_of(a), ap_of(b)
                prog.add(eng, lambda e: e.tensor_tensor(out=o, in0=x, in1=y, op=op),
                         [key_of(a), key_of(b)], [key_of(out)])

            def ts(out, a, s1, s2, op0, op1=None, eng="dve", extra=()):
                o, x = ap_of(out), ap_of(a)
                if op1 is None:
                    prog.add(eng, lambda e: e.tensor_scalar(out=o, in0=x, scalar1=s1, scalar2=None, op0=op0),
                             [key_of(a)] + list(extra), [key_of(out)])
                else:
                    prog.add(eng, lambda e: e.tensor_scalar(out=o, in0=x, scalar1=s1, scalar2=s2, op0=op0, op1=op1),
                             [key_of(a)] + list(extra), [key_of(out)])

            def act(out, a, func, scale=1.0, bias=0.0, extra=()):
                o, x = ap_of(out), ap_of(a)
                prog.add("act", lambda e: e.activation(out=o, in_=x, func=func, bias=bias, scale=scale),
                         [key_of(a)] + list(extra), [key_of(out)])

            prog.add("sp", lambda e: e.dma_start(out=par[:], in_=par_d), [], ["par"], dsem=ldsem[0])
            prog.add("sp", lambda e: e.dma_start(out=sB[:], in_=ssmB_d), [], ["sB"], dsem=ldsem[1])
            prog.add("sp", lambda e: e.dma_start(out=sC[:], in_=ssmC_d), [], ["sC"], dsem=ldsem[2])
            prog.add("dve", lambda e: e.memset(onesb[:], 1.0), [], ["onesb"])
            prog.add("dve", lambda e: e.memset(onesf[:], 1.0), [], ["onesf"])
            prog.add("dve", lambda e: e.memset(zerob[:], 0.0), [], ["zerob"])

            prog.add("act", lambda e: e.activation(out=cab[:], in_=pcol("cvec"), func=AF.Silu),
                     ["par"], ["cab"])
            w_ada_r = w_ada.rearrange("(kt p) c -> p kt c", p=128)
            mb = bank()
            for ch in range(8):
                s = ring["i"] % NSLOT
                ring["i"] += 1
                wv = wslot[s][:, 0:8192].rearrange("p (k c) -> p k c", k=16)
                src = w_ada_r[:, :, ch * 512:(ch + 1) * 512]
                prog.add("pool", lambda e, wv=wv, src=src: e.dma_start(out=wv, in_=src),
                         [], [("w", s)], dsem=wsem[s])
                for ti in range(4):
                    j = ch * 4 + ti

                    def fn(e, wv=wv, ti=ti, j=j):
                        ins = None
                        for kt in range(16):
                            ins = e.matmul(ps[:, mb, j:j + 1], lhsT=wv[:, kt, ti * 128:(ti + 1) * 128],
                                           rhs=cab[:, kt:kt + 1], start=(kt == 0), stop=(kt == 15))
                        return ins
                    prog.add("pe", fn, [("w", s), "cab"], [("ps", mb)])
            for i_ in range(CVPARTS["w_in"]):
                cvt_piece(prog, "w_in", i_)
            prog.add("dve", lambda e: e.tensor_tensor(out=modv[:, 0:32], in0=ps[:, mb, 0:32], in1=pcol("bada", 0, 32), op=ALU.add),
                     [("ps", mb), "par"], ["modv"])
            prog.add("dve", lambda e: e.scalar_tensor_tensor(out=g1s[:], in0=modv[:, 16:32], scalar=1.0, in1=pcol("n1g"),
                                                             op0=ALU.add, op1=ALU.mult), ["modv", "par"], ["g1s"])

            def cmul(outr, outi, ar, ai, br, bi):
                tt("c1", ar, br, ALU.mult)
                tt("c2", ai, bi, ALU.mult)
                tt(outr, "c1", "c2", ALU.subtract)
                tt("c1", ar, bi, ALU.mult)
                tt("c2", ai, br, ALU.mult)
                tt(outi, "c1", "c2", ALU.add)

            def lam_tables(are, aim, ldt):
                act("s0", ldt, AF.Exp)
                tt("s1", are, "s0", ALU.mult)
                act("s1", "s1", AF.Exp)
                tt("s2", aim, "s0", ALU.mult)
                for outn, shift in (("li", 0.0), ("lr", PI / 2)):
                    ts("s3", "s2", shift, None, ALU.add)
                    ts("s4", "s2", shift, None, ALU.add)
                    for k in range(8):
                        thr = (2 * k + 1) * PI
                        ts("s5", "s4", thr, -2 * PI, ALU.is_gt, ALU.mult)
                        tt("s3", "s3", "s5", ALU.add)
                    act("s3", "s3", AF.Sin)
                    tt(outn, "s1", "s3", ALU.mult)
                tt("s0", are, are, ALU.mult)
                tt("s1", aim, aim, ALU.mult)
                tt("s0", "s0", "s1", ALU.add)
                o, x = T("s1")[:], T("s0")[:]
                prog.add("dve", lambda e: e.reciprocal(out=o, in_=x), [("t", "s0")], [("t", "s1")])
                ts("s0", "lr", -1.0, None, ALU.add)
                tt("s2", "s0", are, ALU.mult)
                tt("s3", "li", aim, ALU.mult)
                tt("s2", "s2", "s3", ALU.add)
                tt("fr", "s2", "s1", ALU.mult)
                tt("s2", "li", are, ALU.mult)
                tt("s3", "s0", aim, ALU.mult)
                tt("s2", "s2", "s3", ALU.subtract)
                tt("fi", "s2", "s1", ALU.mult)

            def sBk(i):
                return (sB[:, i, :], "sB")

            def sCk(i):
                return (sC[:, i, :], "sC")

            lam_tables(sBk(0), sBk(1), sBk(2))
            cmul("br", "bi", "fr", "fi", sBk(3), sBk(4))
            cmul("l2r", "l2i", "lr", "li", "lr", "li")
            cmul("l3r", "l3i", "l2r", "l2i", "lr", "li")
            for b in range(4):
                if b == 3:
                    vr, vi = "br", "bi"
                else:
                    pw = {0: ("l3r", "l3i"), 1: ("l2r", "l2i"), 2: ("lr", "li")}[b]
                    cmul("wr", "wi", pw[0], pw[1], "br", "bi")
                    vr, vi = "wr", "wi"
                for plane, vn in ((0, vr), (1, vi)):
                    for j in range(2):
                        o = W2[:, :, b, plane, j * 64:(j + 1) * 64]
                        x = T(vn)[:].rearrange("p (c q) -> p c q", c=8)
                        m = pcol("mB", j, j + 1)
                        prog.add("dve", lambda e, o=o, x=x, m=m: e.tensor_scalar(out=o, in0=x, scalar1=m, scalar2=None,
                                                                                  op0=ALU.mult),
                                 [("t", vn), "par"], ["W2"])

            lam_tables(sCk(0), sCk(1), sCk(2))
            cmul("l2r", "l2i", "lr", "li", "lr", "li")
            cmul("l3r", "l3i", "l2r", "l2i", "lr", "li")
            cmul("l4r", "l4i", "l2r", "l2i", "l2r", "l2i")
            cmul("l8r", "l8i", "l4r", "l4i", "l4r", "l4i")
            cmul("l12r", "l12i", "l8r", "l8i", "l4r", "l4i")
            cmul("l16r", "l16i", "l8r", "l8i", "l8r", "l8i")
            cmul("l32r", "l32i", "l16r", "l16i", "l16r", "l16i")
            cmul("l64r", "l64i", "l32r", "l32i", "l32r", "l32i")
            cmul("l128r", "l128i", "l64r", "l64i", "l64r", "l64i")
            for i, nm in enumerate(("l4", "l8", "l12", "l16", "l32", "l64", "l128")):
                xr = T(nm + "r")[:].rearrange("p (a h) -> p a h", h=16)[:, :, 0]
                xi = T(nm + "i")[:].rearrange("p (a h) -> p a h", h=16)[:, :, 0]
                for o, x, sg, kn in ((CA[:, i, :, 0], xr, 1.0, nm + "r"), (CA[:, i, :, 1], xr, 1.0, nm + "r"),
                                     (CB[:, i, :, 0], xi, -1.0, nm + "i"), (CB[:, i, :, 1], xi, 1.0, nm + "i")):
                    prog.add("dve", lambda e, o=o, x=x, sg=sg: e.tensor_scalar(out=o, in0=x, scalar1=sg, scalar2=None,
                                                                              op0=ALU.mult), [("t", kn)], ["LC"])
            pws = [("lr", "li"), ("l2r", "l2i"), ("l3r", "l3i"), ("l4r", "l4i")]
            for b in range(4):
                cmul("wr", "wi", pws[b][0], pws[b][1], sCk(3), sCk(4))
                for plane, vn, sgn in ((0, "wr", 1.0), (1, "wi", -1.0)):
                    for j in range(2):
                        o = W4[:, :, b, plane, j * 16:(j + 1) * 16]
                        x = T(vn)[:].rearrange("p (a h) -> p a h", h=16)
                        m = pcol("mC", j, j + 1)
                        prog.add("dve", lambda e, o=o, x=x, m=m, sgn=sgn: e.tensor_scalar(
                            out=o, in0=x, scalar1=m, scalar2=sgn, op0=ALU.mult, op1=ALU.mult),
                            [("t", vn), "par"], ["W4"])
            cmul("br", "bi", "fr", "fi", sCk(5), sCk(6))
            Lm = sb1("Lm", [128, 2, 32, 32])
            Rm = sb1("Rm", [128, 2, 32, 32])
            ITf = sb1("ITf", [128, 8, 128])
            for plane, (src, sgn) in enumerate(((sCk(3), 1.0), (sCk(4), -1.0))):
                for j in range(2):
                    o = Rm[:, plane, :, j * 16:(j + 1) * 16]
                    x = src[0].rearrange("p (a h) -> p a h", h=16)
                    m = pcol("mC", j, j + 1)
                    prog.add("dve", lambda e, o=o, x=x, m=m, sgn=sgn: e.tensor_scalar(
                        out=o, in0=x, scalar1=m, scalar2=sgn, op0=ALU.mult, op1=ALU.mult), ["sC", "par"], ["Rm"])
            prog.add("dve", lambda e: e.memset(ITf[:], 0.0), [], ["ITf"])
            prog.add("dve", lambda e: e.memset(IT[:], 0.0), [], ["IT"])
            ident = pcol("ident")
            for lag in range(4):
                if lag == 0:
                    vr, vi = "br", "bi"
                else:
                    cmul("wr", "wi", pws[lag - 1][0], pws[lag - 1][1], "br", "bi")
                    vr, vi = "wr", "wi"
                for plane, vn in ((0, vr), (1, vi)):
                    for j in range(2):
                        o = Lm[:, plane, :, j * 16:(j + 1) * 16]
                        x = T(vn)[:].rearrange("p (a h) -> p a h", h=16)
                        m = pcol("mC", j, j + 1)
                        prog.add("dve", lambda e, o=o, x=x, m=m: e.tensor_scalar(out=o, in0=x, scalar1=m, scalar2=None,
                                                                                  op0=ALU.mult),
                                 [("t", vn), "par"], ["Lm"])
                bk = bank()
                for pt in range(32):
                    ct, q = pt // 4, pt % 4
                    o = ps[32 * q:32 * q + 32, bk, ct * 32:ct * 32 + 32]

                    def fn(e, o=o, pt=pt, q=q):
                        e.matmul(o, lhsT=Lm[:, 0, pt, :], rhs=Rm[:, 0, pt, :], start=True, stop=False,
                                 tile_position=(0, 32 * q))
                        return e.matmul(o, lhsT=Lm[:, 1, pt, :], rhs=Rm[:, 1, pt, :], start=False, stop=True,
                                        tile_position=(0, 32 * q))
                    prog.add("pe", fn, ["Lm", "Rm"], [("ps", bk)])
                for q in range(4):
                    x = ps[32 * q:32 * q + 32, bk, 0:256].rearrange("p (c k) -> p c k", c=8)
                    if lag == 0:
                        o = ITf[32 * q:32 * q + 32, :, 32 * q:32 * q + 32]
                        prog.add("dve", lambda e, o=o, x=x: e.tensor_copy(out=o, in_=x), [("ps", bk)], ["ITf"])
                    else:
                        o = IT[32 * q:32 * q + 32, :, lag, 32 * q:32 * q + 32]
                        prog.add("dve", lambda e, o=o, x=x: e.tensor_copy(out=o, in_=x), [("ps", bk)], ["IT"])
                if lag == 0:
                    for ct in range(8):
                        o = ITf[:, ct, :]
                        d = pcol("dsk", ct, ct + 1)
                        prog.add("dve", lambda e, o=o, d=d: e.scalar_tensor_tensor(out=o, in0=ident, scalar=d, in1=o,
                                                                                  op0=ALU.mult, op1=ALU.add),
                                 ["par", "ITf"], ["ITf"])
                    prog.add("dve", lambda e: e.tensor_copy(out=IT[:, :, 0, :], in_=ITf[:]), ["ITf"], ["IT"])
            prog.add("sp", lambda e: None, ["IT", "W4", "W2", "LC", "g1s", "g2s", "modv", "onesb", "onesf", "zerob"], [])
            with nc.Block() as block:
                prog.emit(block, sems)
            nc.all_engine_barrier()

        with ExitStack() as e2:
            def sb2(name, shape, dt=F32):
                return e2.enter_context(nc.sbuf_tensor(name, list(shape), dt))

            def sem2(name):
                return e2.enter_context(nc.semaphore(name))

            prog = Prog()
            sems = {k: sem2("s2_" + k) for k in ("pe", "act", "dve", "pool")}
            wsem = [sem2(f"s2_w{i}") for i in range(NSLOT)]
            wsemH = [sem2(f"s2_wh{i}") for i in range(NSLOT)]
            hsem = sem2("s2_h")
            osem = sem2("s2_o")
            h = sb2("h", [128, 16, NT])
            u = sb2("u", [128, 16, NT], BF16)
            v = sb2("v", [128, 8, 32 + NT], BF16)
            vssm = sb2("vssm", [128, 8, NT], BF16)
            R16 = sb2("R16", [128, 16, NT], BF16)
            sAB = sb2("sAB", [128, 8, NT], BF16)
            rstd = sb2("rstd", [128, NT])
            mu = sb2("mu", [128, NT])
            sq = [sb2(f"sq{i}", [128, NT], BF16) for i in range(2)]
            tf = [sb2(f"tf{i}", [128, NT]) for i in range(2)]
            X = sb2("X", [128, 32, MC + 1, 2])
            G = sb2("G", [128, 32, NCH + 1, 2])
            ZR = sb2("ZR", [128, 32, MC], BF16)
            ZI = sb2("ZI", [128, 32, MC], BF16)
            P1 = sb2("P1", [128, 32, NCH, 2])
            P2 = sb2("P2", [128, 32, NCH, 2])
            uh = sb2("uh", [128, 16, 32], BF16)
            sgh = sb2("sgh", [128, 8, 32], BF16)
            ring["i"] = 0
            NDG = 4
            DG = sb2("DG", [128, NDG, 128], BF16)
            cnt = {"sq": 0, "tf": 0, "dg": 0}

            shift1, gate1 = modv[:, 0:16], modv[:, 32:48]
            shift2, gate2 = modv[:, 48:64], modv[:, 80:96]

            mode = {"bf": False}

            def wload(W, kt0, nkt, c0, ncols):
                s = ring["i"] % NSLOT
                ring["i"] += 1
                wv = wslot[s][:, 0:nkt * ncols].rearrange("p (k c) -> p k c", k=nkt)
                nm = WNAME.get(id(W))
                if mode["bf"] and nm is not None:
                    src = WBF[nm].rearrange("(kt p) c -> p kt c", p=128)[:, kt0:kt0 + nkt, c0:c0 + ncols]
                    prog.add("sp", lambda e: e.dma_start(out=wv, in_=src), cvt_keys[nm], [("w", s)], dsem=wsemH[s])
                else:
                    src = W.rearrange("(kt p) c -> p kt c", p=128)[:, kt0:kt0 + nkt, c0:c0 + ncols]
                    prog.add("pool", lambda e: e.dma_start(out=wv, in_=src), [], [("w", s)], dsem=wsem[s])
                return wv, ("w", s)

            cvt_keys = {nm: [("cvt", nm, i) for i in range(n)] for nm, n in CVPARTS.items()}
            for i_ in range(CVPARTS["w_in"]):
                prog.add("pool", lambda e: None, [], [("cvt", "w_in", i_)], dsem=cvsem["w_in"][i_])
            cvt_plan = [(nm, i_) for nm in ("w_co", "w_gb", "w_ga", "w_out", "w_ff1") for i_ in range(CVPARTS[nm])]

            def cvt_next():
                if cvt_plan:
                    nm, i_ = cvt_plan.pop(0)
                    cvt_piece(prog, nm, i_)

            def mmg(out, pairs, reads, writes):
                def fn(e):
                    ins = None
                    n = len(pairs)
                    for i, (l, r) in enumerate(pairs):
                        ins = e.matmul(out, lhsT=l, rhs=r, start=(i == 0), stop=(i == n - 1))
                    return ins
                prog.add("pe", fn, reads, writes)

            def mod_rest(chs):
                for ch in chs:
                    wv, wk = wload(w_ada, 0, 16, ch * 512, 512)
                    b = bank()
                    for ti in range(4):
                        def fn(e, wv=wv, ti=ti, b=b):
                            ins = None
                            for kt in range(16):
                                ins = e.matmul(ps[:, b, ti:ti + 1], lhsT=wv[:, kt, ti * 128:(ti + 1) * 128],
                                               rhs=cab[:, kt:kt + 1], start=(kt == 0), stop=(kt == 15))
                            return ins
                        prog.add("pe", fn, [wk], [("ps", b)])
                    j0 = ch * 4
                    prog.add("dve", lambda e, b=b, j0=j0: e.tensor_tensor(out=modv[:, j0:j0 + 4], in0=ps[:, b, 0:4],
                                                                          in1=pcol("bada", j0, j0 + 4), op=ALU.add),
                             [("ps", b)], ["modv2"])
                if 23 in chs:
                    prog.add("dve", lambda e: e.scalar_tensor_tensor(out=g2s[:], in0=modv[:, 64:80], scalar=1.0, in1=pcol("n2g"),
                                                                     op0=ALU.add, op1=ALU.mult), ["modv2"], ["modv2"])

            def rmsnorm_stats():
                b = bank()
                for kt in range(16):
                    i = cnt["sq"] % 2
                    cnt["sq"] += 1
                    sqa = sq[i]
                    prog.add("act", lambda e, sqa=sqa, kt=kt: e.activation(out=sqa[:], in_=h[:, kt, :], func=AF.Square),
                             [("h", kt)], [("sq", i)])
                    prog.add("pe", lambda e, sqa=sqa, kt=kt: e.matmul(ps[:, b, :], lhsT=onesb[:], rhs=sqa[:],
                                                                      start=(kt == 0), stop=(kt == 15)),
                             [("sq", i)], [("ps", b)])
                prog.add("act", lambda e: e.activation(out=rstd[:], in_=ps[:, b, :], func=AF.Sqrt, bias=EPS, scale=1.0 / D),
                         [("ps", b)], ["rstd"])
                prog.add("dve", lambda e: e.reciprocal(out=rstd[:], in_=rstd[:]), ["rstd"], ["rstd"])

            def modulate(gs, sh, extra):
                for kt in range(16):
                    i = cnt["tf"] % 2
                    cnt["tf"] += 1
                    t = tf[i]
                    prog.add("dve", lambda e, t=t, kt=kt: e.tensor_tensor(out=t[:], in0=h[:, kt, :], in1=rstd[:], op=ALU.mult),
                             [("h", kt), "rstd"], [("tf", i)])
                    prog.add("act", lambda e, t=t, kt=kt: e.activation(out=u[:, kt, :], in_=t[:], func=AF.Identity,
                                                                       bias=sh[:, kt:kt + 1], scale=gs[:, kt:kt + 1]),
                             [("tf", i)] + extra, [("u", kt)])

            ukeys = [("u", kt) for kt in range(16)]

            SCAN_ENG = "dve"

            def cmuladd(dst, ci, src, n, kd, ks):
                shp = [128, 32, n, 2]
                ca = CA[:, ci, :, :].unsqueeze(2).to_broadcast(shp)
                cb = CB[:, ci, :, :].unsqueeze(2).to_broadcast(shp)
                p1, p2 = P1[:, :, 0:n, :], P2[:, :, 0:n, :]
                srcsw = src[:, :, :, ::-1]
                prog.add(SCAN_ENG, lambda e: e.tensor_tensor(out=p1, in0=src, in1=ca, op=ALU.mult), [ks], ["P1"])
                prog.add(SCAN_ENG, lambda e: e.tensor_tensor(out=p2, in0=srcsw, in1=cb, op=ALU.mult), [ks], ["P2"])
                prog.add(SCAN_ENG, lambda e: e.tensor_tensor(out=p1, in0=p1, in1=p2, op=ALU.add), ["P1", "P2"], ["P1"])
                prog.add(SCAN_ENG, lambda e: e.tensor_tensor(out=dst, in0=dst, in1=p1, op=ALU.add), ["P1", kd], [kd])

            def Xc(a, n=NCH, step=4):
                return X[:, :, 1 + a:2 + a + step * (n - 1):step, :]

            def ssm_q(sb_i):
                t0 = sb_i * SUB
                for half in range(2):
                    banks = [bank() for _ in range(4)]
                    for cl in range(4):
                        ct = half * 4 + cl
                        for q in range(4):
                            for plane in range(2):
                                c0 = (cl * 2 + plane) * MC
                                o = ps[:, banks[q], c0:c0 + MC]

                                def fn(e, o=o, ct=ct, q=q, plane=plane):
                                    ins = None
                                    for b in range(4):
                                        ins = e.matmul(o, lhsT=W2[32 * q:32 * q + 32, ct, b, plane, :],
                                                       rhs=vssm[32 * q:32 * q + 32, ct, t0 + b * MC:t0 + (b + 1) * MC],
                                                       start=(b == 0), stop=(b == 3), tile_position=(32 * q, 0))
                                    return ins
                                prog.add("pe", fn, [("vssm", ct, sb_i)], [("ps", banks[q])])
                    for q in range(4):
                        src = ps[:, banks[q], 0:8 * MC].rearrange("p (c l m) -> p c l m", c=4, l=2)
                        for plane in range(2):
                            o = X[:, half * 16 + q:half * 16 + 16:4, 1:MC + 1, plane]
                            x = src[:, :, plane, :]
                            prog.add("act", lambda e, o=o, x=x: e.activation(out=o, in_=x, func=AF.Identity),
                                     [("ps", banks[q])], ["X"])

            def ssm_scan(prefix, last_prefix_sub):
                for a in range(1, 4):
                    cmuladd(Xc(a), 0, Xc(a - 1), NCH, "X", "X")
                if prefix:
                    cmuladd(X[:, :, 8:33:8, :], 3, X[:, :, 4:29:8, :], 4, "X", "X")
                    cmuladd(X[:, :, 16:33:16, :], 4, X[:, :, 8:25:16, :], 2, "X", "X")
                    cmuladd(X[:, :, 32:33, :], 5, X[:, :, 16:17, :], 1, "X", "X")
                    cmuladd(X[:, :, 32:33, :], 6, G[:, :, 0:1, :], 1, "X", "G")
                    if last_prefix_sub:
                        fl = pcol("flag")
                        prog.add(SCAN_ENG, lambda e: e.tensor_scalar(out=G[:, :, 0:1, :], in0=X[:, :, 32:33, :], scalar1=fl,
                                                                  scalar2=None, op0=ALU.mult), ["X"], ["G"])
                        prog.add(SCAN_ENG, lambda e: e.tensor_copy(out=X[:, :, 0:1, :], in_=G[:, :, 0:1, :]), ["G"], ["X"])
                    else:
                        prog.add(SCAN_ENG, lambda e: e.tensor_copy(out=G[:, :, 0:1, :], in_=X[:, :, 32:33, :]), ["X"], ["G"])
                    return
                prog.add(SCAN_ENG, lambda e: e.tensor_copy(out=G[:, :, 1:NCH + 1, :], in_=X[:, :, 4:4 * NCH + 1:4, :]),
                         ["X"], ["G"])
                for k in range(4):
                    d = 1 << k
                    cmuladd(G[:, :, d:NCH + 1, :], 3 + k, G[:, :, 0:NCH + 1 - d, :], NCH + 1 - d, "G", "G")
                for a in range(3):
                    cmuladd(Xc(a), a, G[:, :, 0:NCH, :], NCH, "X", "G")
                prog.add(SCAN_ENG, lambda e: e.tensor_copy(out=Xc(3), in_=G[:, :, 1:NCH + 1, :]), ["G"], ["X"])
                for Z, pl in ((ZR, 0), (ZI, 1)):
                    prog.add("act", lambda e, Z=Z, pl=pl: e.activation(out=Z[:], in_=X[:, :, 0:MC, pl], func=AF.Identity),
                             ["X"], ["Z"])
                prog.add(SCAN_ENG, lambda e: e.tensor_copy(out=G[:, :, 0:1, :], in_=G[:, :, NCH:NCH + 1, :]), ["G"], ["G"])
                prog.add(SCAN_ENG, lambda e: e.tensor_copy(out=X[:, :, 0:1, :], in_=G[:, :, 0:1, :]), ["G", "Z"], ["X"])

            def ssm_y(sb_i):
                t0 = sb_i * SUB
                for ct in range(8):
                    b_ = bank()
                    Y = ps[:, b_, 0:SUB]

                    def fn(e, ct=ct, b_=b_):
                        e.matmul(ps[:, b_, 0:SUB], lhsT=zerob[:], rhs=vssm[:, ct, t0:t0 + SUB], start=True, stop=False)
                        for b in range(4):
                            for b2 in range(b + 1):
                                e.matmul(ps[:, b_, b * MC:(b + 1) * MC], lhsT=IT[:, ct, b - b2, :],
                                         rhs=vssm[:, ct, t0 + b2 * MC:t0 + (b2 + 1) * MC], start=False, stop=False)
                        ins = None
                        for q in range(4):
                            pt = ct * 4 + q
                            for b in range(4):
                                for plane, Z in ((0, ZR), (1, ZI)):
                                    last = (b == 3 and plane == 1)
                                    ins = e.matmul(ps[32 * q:32 * q + 32, b_, b * MC:(b + 1) * MC],
                                                   lhsT=W4[:, pt, b, plane, :], rhs=Z[:, pt, :], start=False, stop=last,
                                                   tile_position=(0, 32 * q))
                        return ins
                    prog.add("pe", fn, [("vssm", ct, sb_i), "Z"], [("ps", b_)])
                    o = vssm[:, ct, t0:t0 + SUB].rearrange("p (m b) -> p b m", b=4)
                    Yv = Y.rearrange("p (b m) -> p b m", b=4)
                    prog.add("act", lambda e, o=o, Yv=Yv: e.activation(out=o, in_=Yv, func=AF.Gelu_apprx_tanh),
                             [("ps", b_)], [("vssm", ct, sb_i)])

            def win_tile(wv, wk, ti, evac):
                b = bank()
                wkl = list(wk) if isinstance(wk, list) else [wk]
                mmg(ps[:, b, :], [(wv[:, kt, ti * 128:(ti + 1) * 128], u[:, kt, :]) for kt in range(16)],
                    wkl + ukeys, [("ps", b)])
                evac(b)

            def pre_vssm():
                return [wload(w_in, 0, 16, ch * 512, 512) for ch in (4, 5)]

            def do_vssm(pre):
                ct = 0
                for wv, wk, ntl in pre:
                    for ti in range(ntl):
                        def evac(b, ct=ct):
                            o = vssm[:, ct, :].rearrange("p (s b m) -> p s b m", s=NSUB, b=4)
                            x = ps[:, b, :].rearrange("p (s m b) -> p s b m", s=NSUB, b=4)
                            prog.add("act", lambda e, o=o, x=x: e.activation(out=o, in_=x, func=AF.Identity),
                                     [("ps", b)], [("vssm", ct, s_) for s_ in range(NSUB)])
                        win_tile(wv, wk, ti, evac)
                        ct += 1

            RK = [("R", j) for j in range(16)]
            SK = [("sAB", j) for j in range(8)]
            VK = [("v", c) for c in range(8)] + [("vh", c) for c in range(8)]

            def load_resident():
                w_in_r = w_in.rearrange("(kt p) c -> p kt c", p=128)
                sabw = sAB[:].rearrange("p c t -> p (c t)").rearrange("p (k c) -> p k c", k=16)
                vw = v[:].rearrange("p c t -> p (c t)")[:, 0:4096].rearrange("p (k c) -> p k c", k=16)
                res = []
                for i, (dst, c0, nc_, keys, ntl) in enumerate(((R16[:], 2048, 512, RK, 4), (sabw, 2560, 256, SK, 2),
                                                              (vw, 2816, 256, VK, 2))):
                    sem = sem2(f"s2_res{i}")
                    src = w_in_r[:, :, c0:c0 + nc_]
                    prog.add("pool", lambda e, dst=dst, src=src: e.dma_start(out=dst, in_=src), [], keys, dsem=sem)
                    res.append((dst, keys, ntl))
                return res

            def do_glu(first):
                fl = pcol("flag")
                for ch in (2, 3, 0, 1):
                    wv, wk = wload(w_in, 0, 16, ch * 512, 512)
                    for ti in range(4):
                        ct = (ch % 2) * 4 + ti
                        if ch >= 2:
                            def evac(b, ct=ct):
                                prog.add("act", lambda e: e.activation(out=sAB[:, ct, :], in_=ps[:, b, :], func=AF.Sigmoid),
                                         [("ps", b)], [("sAB", ct)])
                        else:
                            def evac(b, ct=ct):
                                prog.add("dve", lambda e: e.tensor_tensor(out=v[:, ct, 32:32 + NT], in0=ps[:, b, :],
                                                                          in1=sAB[:, ct, :], op=ALU.mult),
                                         [("ps", b), ("sAB", ct)], [("v", ct)])
                        win_tile(wv, wk, ti, evac)
                        if first:
                            b2 = bank()
                            mmg(ps[:, b2, 0:32], [(wv[:, kt, ti * 128:(ti + 1) * 128], uh[:, kt, :]) for kt in range(16)],
                                [wk, "uh"], [("ps", b2)])
                            if ch >= 2:
                                prog.add("act", lambda e, ct=ct, b2=b2: e.activation(out=sgh[:, ct, :], in_=ps[:, b2, 0:32], func=AF.Sigmoid),
                                         [("ps", b2)], [("sgh", ct)])
                            else:
                                prog.add("dve", lambda e, ct=ct, b2=b2: e.scalar_tensor_tensor(out=v[:, ct, 0:32], in0=ps[:, b2, 0:32], scalar=fl,
                                                                                             in1=sgh[:, ct, :], op0=ALU.mult, op1=ALU.mult),
                                         [("ps", b2), ("sgh", ct)], [("vh", ct)])

            def halo_copy(ct, use_flag):
                if use_flag:
                    fl = pcol("flag")
                    prog.add("dve", lambda e: e.tensor_scalar(out=v[:, ct, 0:32], in0=v[:, ct, NT:NT + 32], scalar1=fl,
                                                              scalar2=None, op0=ALU.mult), [("v", ct)], [("vh", ct)])
                else:
                    prog.add("dve", lambda e: e.tensor_copy(out=v[:, ct, 0:32], in_=v[:, ct, NT:NT + 32]),
                             [("v", ct)], [("vh", ct)])

            def conv_ct(ct):
                wdw = pcol("wdw")
                ident = pcol("ident")
                b = bank()
                for k in range(31):
                    i = cnt["dg"] % NDG
                    cnt["dg"] += 1
                    wk_ = wdw[:, ct * 31 + k:ct * 31 + k + 1]
                    prog.add("act", lambda e, i=i, wk_=wk_: e.activation(out=DG[:, i, :], in_=ident, func=AF.Identity, scale=wk_),
                             [], [("dg", i)])
                    src = v[:, ct, 2 + k:2 + k + NT]
                    prog.add("pe", lambda e, i=i, src=src, k=k: e.matmul(ps[:, b, :], lhsT=DG[:, i, :], rhs=src,
                                                                        start=(k == 0), stop=(k == 30)),
                             [("dg", i), ("v", ct), ("vh", ct)], [("ps", b)])
                bd = pcol("bdw", ct, ct + 1)
                prog.add("act", lambda e: e.activation(out=sAB[:, ct, :], in_=ps[:, b, :], func=AF.Identity, bias=bd),
                         [("ps", b)], [("sAB", ct)])
                halo_copy(ct, False)

            def do_ln():
                b1, b2 = bank(), bank()
                for ct in range(8):
                    rk = [("sAB", ct)]
                    x = sAB[:, ct, :]
                    prog.add("pe", lambda e, x=x, ct=ct: e.matmul(ps[:, b1, :], lhsT=onesb[:], rhs=x, start=(ct == 0), stop=(ct == 7)),
                             rk, [("ps", b1)])
                    i = cnt["sq"] % 2
                    cnt["sq"] += 1
                    t = sq[i]
                    prog.add("act", lambda e, x=x, t=t: e.activation(out=t[:], in_=x, func=AF.Square), rk, [("sq", i)])
                    prog.add("pe", lambda e, t=t, ct=ct: e.matmul(ps[:, b2, :], lhsT=onesb[:], rhs=t[:], start=(ct == 0), stop=(ct == 7)),
                             [("sq", i)], [("ps", b2)])
                prog.add("dve", lambda e: e.tensor_scalar(out=mu[:], in0=ps[:, b1, :], scalar1=1.0 / 1024, scalar2=None, op0=ALU.mult),
                         [("ps", b1)], ["mu"])
                prog.add("dve", lambda e: e.tensor_tensor(out=rstd[:], in0=mu[:], in1=mu[:], op=ALU.mult), ["mu"], ["rstd"])
                prog.add("dve", lambda e: e.scalar_tensor_tensor(out=rstd[:], in0=ps[:, b2, :], scalar=1.0 / 1024, in1=rstd[:],
                                                                 op0=ALU.mult, op1=ALU.subtract), [("ps", b2), "rstd"], ["rstd"])
                prog.add("act", lambda e: e.activation(out=rstd[:], in_=rstd[:], func=AF.Sqrt, bias=EPS, scale=1.0), ["rstd"], ["rstd"])
                prog.add("dve", lambda e: e.reciprocal(out=rstd[:], in_=rstd[:]), ["rstd"], ["rstd"])
                for ct in range(8):
                    rk = [("sAB", ct)]
                    i = cnt["tf"] % 2
                    cnt["tf"] += 1
                    t = tf[i]
                    x = sAB[:, ct, :]
                    prog.add("dve", lambda e, x=x, t=t: e.tensor_tensor(out=t[:], in0=x, in1=mu[:], op=ALU.subtract),
                             rk + ["mu"], [("tf", i)])
                    prog.add("dve", lambda e, t=t: e.tensor_tensor(out=t[:], in0=t[:], in1=rstd[:], op=ALU.mult),
                             [("tf", i), "rstd"], [("tf", i)])
                    lg, lb = pcol("lng", ct, ct + 1), pcol("lnb", ct, ct + 1)
                    prog.add("act", lambda e, t=t, ct=ct, lg=lg, lb=lb: e.activation(out=v[:, ct, 32:32 + NT], in_=t[:], func=AF.Silu,
                                                                                    bias=lb, scale=lg),
                             [("tf", i), ("vh", ct)], [("v", ct)])

            def merge_p1(chs, pre):
                for ch in chs:
                    wv, wk = pre
                    for ti in range(4):
                        j = (ch - 6) * 4 + ti

                        def evac(b, j=j):
                            prog.add("act", lambda e: e.activation(out=R16[:, j, :], in_=ps[:, b, :], func=AF.Sigmoid),
                                     [("ps", b)], [("R", j)])
                        win_tile(wv, wk, ti, evac)

            def merge_p23():
                for ch in range(2):
                    wv, wk = wload(w_co, 0, 8, ch * 1024, 1024)
                    for ti in range(8):
                        j = ch * 8 + ti
                        b = bank()
                        mmg(ps[:, b, :], [(wv[:, kt, ti * 128:(ti + 1) * 128], v[:, kt, 32:32 + NT]) for kt in range(8)],
                            [wk] + [("v", kt) for kt in range(8)], [("ps", b)])
                        prog.add("dve", lambda e, j=j, b=b: e.tensor_tensor(out=R16[:, j, :], in0=ps[:, b, :], in1=R16[:, j, :],
                                                                            op=ALU.mult), [("ps", b), ("R", j)], [("R", j)])
                skeys = [("vssm", kt, s_) for kt in range(8) for s_ in range(NSUB)]
                for jb in range(4):
                    wv, wk = wload(w_in, 0, 16, (10 + jb) * 512, 512)
                    for ti in range(4):
                        def evac(b, ti=ti):
                            prog.add("act", lambda e: e.activation(out=sAB[:, ti, :], in_=ps[:, b, :], func=AF.Sigmoid),
                                     [("ps", b)], [("sAB", ti)])
                        win_tile(wv, wk, ti, evac)
                    for which, W in ((0, w_gb), (1, w_ga)):
                        wv, wk = wload(W, 0, 8, jb * 512, 512)
                        for ti in range(4):
                            j = jb * 4 + ti
                            b = bank()
                            mmg(ps[:, b, :], [(wv[:, kt, ti * 128:(ti + 1) * 128], vssm[:, kt, :]) for kt in range(8)],
                                [wk] + skeys, [("ps", b)])
                            kB = ("sAB", 4 + ti)
                            if which == 0:
                                prog.add("act", lambda e, ti=ti, b=b: e.activation(out=sAB[:, 4 + ti, :], in_=ps[:, b, :], func=AF.Sigmoid),
                                         [("ps", b)], [kB])
                            else:
                                prog.add("dve", lambda e, ti=ti, b=b: e.tensor_tensor(out=sAB[:, 4 + ti, :], in0=ps[:, b, :], in1=sAB[:, 4 + ti, :],
                                                                                      op=ALU.mult), [("ps", b), kB], [kB])
                                prog.add("dve", lambda e, ti=ti: e.tensor_tensor(out=sAB[:, 4 + ti, :], in0=sAB[:, 4 + ti, :], in1=sAB[:, ti, :],
                                                                                 op=ALU.mult), [kB, ("sAB", ti)], [kB])
                                prog.add("dve", lambda e, ti=ti, j=j: e.tensor_tensor(out=R16[:, j, :], in0=R16[:, j, :], in1=sAB[:, 4 + ti, :],
                                                                                      op=ALU.add), [kB, ("R", j)], [("R", j)])

            def do_wout():
                for ch in range(4):
                    wv, wk = wload(w_out, 0, 16, ch * 512, 512)
                    for ti in range(4):
                        j = ch * 4 + ti
                        b = bank()
                        mmg(ps[:, b, :], [(wv[:, kt, ti * 128:(ti + 1) * 128], R16[:, kt, :]) for kt in range(16)],
                            [wk] + [("R", kt) for kt in range(16)], [("ps", b)])
                        prog.add("dve", lambda e, j=j, b=b: e.scalar_tensor_tensor(out=h[:, j, :], in0=ps[:, b, :], scalar=gate1[:, j:j + 1],
                                                                                   in1=h[:, j, :], op0=ALU.mult, op1=ALU.add),
                                 [("ps", b), ("h", j), "modv2"], [("h", j)])

            def do_ffn():
                for hb in range(4):
                    for c4 in range(4):
                        wv, wk = wload(w_ff1, 0, 16, (hb * 4 + c4) * 512, 512)
                        for ti in range(4):
                            i_ = c4 * 4 + ti

                            def evac(b, i_=i_):
                                k = cnt["sq"] % 2
                                cnt["sq"] += 1
                                s_ = sq[k]
                                prog.add("act", lambda e: e.activation(out=s_[:], in_=ps[:, b, :], func=AF.Relu), [("ps", b)], [("sq", k)])
                                prog.add("dve", lambda e: e.tensor_tensor(out=R16[:, i_, :], in0=ps[:, b, :], in1=s_[:], op=ALU.mult),
                                         [("ps", b), ("sq", k)], [("R", i_)])
                            win_tile(wv, wk, ti, evac)
                    for cb in range(4):
                        wv, wk = wload(w_ff2, hb * 16, 16, cb * 512, 512)
                        for ti in range(4):
                            j = cb * 4 + ti
                            b = bank()
                            mmg(ps[:, b, :], [(wv[:, kt, ti * 128:(ti + 1) * 128], R16[:, kt, :]) for kt in range(16)],
                                [wk] + [("R", kt) for kt in range(16)], [("ps", b)])
                            prog.add("dve", lambda e, j=j, b=b: e.scalar_tensor_tensor(out=h[:, j, :], in0=ps[:, b, :], scalar=gate2[:, j:j + 1],
                                                                                       in1=h[:, j, :], op0=ALU.mult, op1=ALU.add),
                                     [("ps", b), ("h", j)], [("h", j)])

            hkeys = [("h", kt) for kt in range(16)]
            prog.add("dve", lambda e: e.memset(G[:], 0.0), [], ["G"])
            prog.add("dve", lambda e: e.memset(X[:], 0.0), [], ["X"])
            prog.add("dve", lambda e: e.memset(v[:], 0.0), [], [("v", c) for c in range(8)] + [("vh", c) for c in range(8)])

            steps = [("p", i) for i in range(NTILE)] + [("m", i) for i in range(NTILE)]
            resw = load_resident()
            for kind, ti_ in steps:
                srcT = xpT if kind == "p" else xT
                src = srcT.rearrange("(kt p) t -> p kt t", p=128)[:, :, ti_ * NT:(ti_ + 1) * NT]
                prog.add("sp", lambda e, src=src: e.dma_start(out=h[:], in_=src), [], hkeys, dsem=hsem)
                rmsnorm_stats()
                modulate(g1s, shift1, [])
                if kind == "p":
                    do_vssm(resw)
                    if ti_ == 0:
                        mod_rest(range(8, 24))
                    for s_i in range(NSUB):
                        ssm_q(s_i)
                        cvt_next()
                        ssm_scan(True, ti_ == NTILE - 1 and s_i == NSUB - 1)
                    if ti_ == NTILE - 1:
                        prog.add("act", lambda e: e.activation(out=uh[:], in_=u[:, :, NT - 32:NT], func=AF.Identity), ukeys, ["uh"])
                    continue
                while cvt_plan:
                    cvt_next()
                mode["bf"] = True
                do_glu(ti_ == 0)
                do_vssm([(wv_, wk_, 4) for wv_, wk_ in pre_vssm()])
                for s_i in range(NSUB):
                    p1w = wload(w_in, 0, 16, (6 + s_i) * 512, 512)
                    ssm_q(s_i)
                    ssm_scan(False, False)
                    conv_ct(2 * s_i)
                    conv_ct(2 * s_i + 1)
                    merge_p1([6 + s_i], p1w)
                    ssm_y(s_i)
                do_ln()
                merge_p23()
                do_wout()
                rmsnorm_stats()
                modulate(g2s, shift2, ["modv2"])
                do_ffn()
                rmsnorm_stats()
                fgc = pcol("fg")
                for kt in range(16):
                    prog.add("dve", lambda e, kt=kt: e.scalar_tensor_tensor(out=h[:, kt, :], in0=h[:, kt, :], scalar=fgc[:, kt:kt + 1],
                                                                            in1=rstd[:], op0=ALU.mult, op1=ALU.mult),
                             [("h", kt), "rstd"], [("h", kt)])
                dst = outT.rearrange("(kt p) t -> p kt t", p=128)[:, :, ti_ * NT:(ti_ + 1) * NT]
                prog.add("sp", lambda e, dst=dst: e.dma_start(out=dst, in_=h[:]), hkeys, [("out", ti_)], dsem=osem)
            prog.add("sp", lambda e: None, [("out", i) for i in range(NTILE)], [])
            with nc.Block() as block:
                prog.emit(block, sems)
    return nc


_NC = None


def _prep(x, c, w_ada, b_ada, norm1_g, w_in, w_dw, b_dw, ln_g, ln_b, w_conv_out, a_re, a_im, log_dt, b_re, b_im,
          c_re, c_im, d_skip, w_glu_a, w_glu_b, w_out, norm2_g, w_ff1, w_ff2, final_g):
    f = np.float32
    shared = {
        "w_ada": np.ascontiguousarray(w_ada[0], f), "w_in": np.ascontiguousarray(w_in[0], f),
        "w_conv_out": np.ascontiguousarray(w_conv_out[0], f), "w_glu_a": np.ascontiguousarray(w_glu_a[0], f),
        "w_glu_b": np.ascontiguousarray(w_glu_b[0], f), "w_out": np.ascontiguousarray(w_out[0], f),
        "w_ff1": np.ascontiguousarray(w_ff1[0], f), "w_ff2": np.ascontiguousarray(w_ff2[0], f),
    }
    are, aim, ldt = a_re[0], a_im[0], log_dt[0]
    bre, bim, cre, cim = b_re[0], b_im[0], c_re[0], c_im[0]
    def Bl_a(a):
        t = a.reshape(8, 8, 64).transpose(1, 0, 2)
        return np.broadcast_to(t[:, None], (8, 16, 8, 64)).reshape(128, 512)
    ldB = np.broadcast_to(ldt.reshape(8, 8).T[:, None, :, None], (8, 16, 8, 64)).reshape(128, 512)
    def Bl_b(b):
        return b.reshape(8, 8, 64, 16).transpose(1, 3, 0, 2).reshape(128, 512)
    ssmB = np.stack([Bl_a(are), Bl_a(aim), ldB, Bl_b(bre), Bl_b(bim)], axis=1).astype(f)
    def Cl_a(a):
        t = a.reshape(32, 2, 64).transpose(1, 2, 0)
        return np.broadcast_to(t[..., None], (2, 64, 32, 16)).reshape(128, 512)
    ldC = np.broadcast_to(ldt.reshape(32, 2).T[:, None, :, None], (2, 64, 32, 16)).reshape(128, 512)
    def Cl_c(cc):
        return cc.reshape(32, 2, 16, 64).transpose(1, 3, 0, 2).reshape(128, 512)
    def Cl_b(b):
        return b.reshape(32, 2, 64, 16).transpose(1, 2, 0, 3).reshape(128, 512)
    ssmC = np.stack([Cl_a(are), Cl_a(aim), ldC, Cl_c(cre), Cl_c(cim), Cl_b(bre), Cl_b(bim)], axis=1).astype(f)
    shared["ssmB"] = np.ascontiguousarray(ssmB)
    shared["ssmC"] = np.ascontiguousarray(ssmC)

    def fm(vec, n):
        return np.asarray(vec, f).reshape(n, 128).T
    par_base = np.zeros((128, NPAR), f)
    def put(name, arr):
        o, w = PC[name]
        par_base[:, o:o + w] = arr
    put("bada", fm(b_ada[0], 96))
    put("n1g", fm(norm1_g[0], 16)); put("n2g", fm(norm2_g[0], 16)); put("fg", fm(final_g, 16))
    put("wdw", np.asarray(w_dw[0], f).reshape(31, 8, 128).transpose(2, 1, 0).reshape(128, 248))
    put("bdw", fm(b_dw[0], 8)); put("lng", fm(ln_g[0], 8)); put("lnb", fm(ln_b[0], 8)); put("dsk", fm(d_skip[0], 8))
    pidx = np.arange(128)
    mB = np.stack([((pidx // 16) % 2 == j) for j in range(2)], axis=1).astype(f)
    mC = np.stack([((pidx // 64) == j) for j in range(2)], axis=1).astype(f)
    put("mB", mB); put("mC", mC); put("ident", np.eye(128, dtype=f))
    in_maps = []
    zeros = np.zeros((D, NTOK), f)
    for core in range(NCORE):
        b, s = core // 2, core % 2
        m = dict(shared)
        m["xT"] = np.ascontiguousarray(x[b, s * NTOK:(s + 1) * NTOK, :].T)
        m["xpT"] = np.ascontiguousarray(x[b, 0:NTOK, :].T) if s == 1 else zeros
        p = par_base.copy()
        o, w = PC["cvec"]; p[:, o:o + w] = fm(c[b], 16)
        o, w = PC["flag"]; p[:, o:o + w] = float(s)
        m["par"] = p
        in_maps.append(m)
    return in_maps


def kernel(**inputs):
    global _NC
    inputs = {k: np.asarray(v) for k, v in inputs.items()}
    in_maps = _prep(**inputs)
    if _NC is None:
        _NC = build()
    res = run_bass_kernel_spmd(_NC, in_maps, core_ids=list(range(NCORE)))
    out = np.empty((4, 4096, D), np.float32)
    for core in range(NCORE):
        b, s = core // 2, core % 2
        out[b, s * NTOK:(s + 1) * NTOK, :] = res.results[core]["outT"].T
    return out
```

```python
import math
import numpy as np
from contextlib import ExitStack
import concourse.bass as bass
import concourse.mybir as mybir
from concourse.bass_utils import run_bass_kernel_spmd

F32, BF16 = mybir.dt.float32, mybir.dt.bfloat16
AF = mybir.ActivationFunctionType
ALU = mybir.AluOpType

NCORE = 8
D = 2048
NTOK = 2048
NT = 512
NTILE = NTOK // NT
SUB = 128
NSUB = NT // SUB
MC = SUB // 4
NCH = MC // 4
NSLOT = 2
EPS = 1e-6
PI = math.pi

PC = {}
_off = 0
for _n, _w in [("cvec", 16), ("bada", 96), ("n1g", 16), ("n2g", 16), ("fg", 16), ("wdw", 248),
               ("bdw", 8), ("lng", 8), ("lnb", 8), ("dsk", 8), ("flag", 1), ("mB", 2), ("mC", 2),
               ("ident", 128)]:
    PC[_n] = (_off, _w)
    _off += _w
NPAR = _off


class Op:
    __slots__ = ("eng", "fn", "deps", "needs", "val", "dsem", "dval")


class Prog:
    ENG = ("pe", "act", "dve", "pool", "sp")

    def __init__(self):
        self.ops = {e: [] for e in self.ENG}
        self.lastw = {}
        self.readers = {}
        self.dmacount = {}

    def add(self, eng, fn, reads=(), writes=(), dsem=None):
        op = Op()
        op.eng, op.fn, op.needs, op.val, op.dsem, op.dval = eng, fn, False, None, dsem, None
        if dsem is not None:
            self.dmacount[id(dsem)] = self.dmacount.get(id(dsem), 0) + 16
            op.dval = self.dmacount[id(dsem)]
        deps, seen = [], set()
        cand = []
        for k in reads:
            w = self.lastw.get(k)
            if w is not None:
                cand.append(w)
        for k in writes:
            w = self.lastw.get(k)
            if w is not None:
                cand.append(w)
            cand.extend(self.readers.get(k, ()))
        for d in cand:
            if id(d) in seen:
                continue
            seen.add(id(d))
            if d.dsem is None and d.eng == eng and eng == "pe":
                continue
            deps.append(d)
        op.deps = deps
        for k in reads:
            self.readers.setdefault(k, []).append(op)
        for k in writes:
            self.lastw[k] = op
            self.readers[k] = []
        self.ops[eng].append(op)
        return op

    def emit(self, block, sems):
        for e in self.ENG:
            for op in self.ops[e]:
                for d in op.deps:
                    if d.dsem is None:
                        d.needs = True
        for e in self.ENG:
            n = 0
            for op in self.ops[e]:
                if op.dsem is None and op.needs:
                    n += 1
                    op.val = n

        def run(engname, eobj):
            known = {}
            for op in self.ops[engname]:
                need = {}
                for d in op.deps:
                    if d.dsem is not None:
                        sem, val = d.dsem, d.dval
                    else:
                        sem, val = sems[d.eng], d.val
                    if need.get(id(sem), (None, 0))[1] < val:
                        need[id(sem)] = (sem, val)
                for sid, (sem, val) in need.items():
                    if known.get(sid, 0) >= val:
                        continue
                    eobj.wait_ge(sem, val)
                    known[sid] = val
                ins = op.fn(eobj)
                if ins is None:
                    continue
                if op.dsem is not None:
                    ins.then_inc(op.dsem, 16)
                elif op.needs:
                    ins.then_inc(sems[engname], 1)

        block.tensor(lambda e: run("pe", e))
        block.scalar(lambda e: run("act", e))
        block.vector(lambda e: run("dve", e))
        block.gpsimd(lambda e: run("pool", e))
        block.sync(lambda e: run("sp", e))


def build():
    nc = bass.Bass("TRN2", target_bir_lowering=False)

    def din(name, shape):
        return nc.dram_tensor(name, list(shape), F32, kind="ExternalInput").ap()

    xT = din("xT", [D, NTOK])
    xpT = din("xpT", [D, NTOK])
    par_d = din("par", [128, NPAR])
    ssmB_d = din("ssmB", [128, 5, 512])
    ssmC_d = din("ssmC", [128, 7, 512])
    w_ada = din("w_ada", [D, 6 * D])
    w_in = din("w_in", [D, 7168])
    w_co = din("w_conv_out", [1024, D])
    w_ga = din("w_glu_a", [1024, D])
    w_gb = din("w_glu_b", [1024, D])
    w_out = din("w_out", [D, D])
    w_ff1 = din("w_ff1", [D, 4 * D])
    w_ff2 = din("w_ff2", [4 * D, D])
    outT = nc.dram_tensor("outT", [D, NTOK], F32, kind="ExternalOutput").ap()
    WSRC = {"w_in": w_in, "w_co": w_co, "w_ga": w_ga, "w_gb": w_gb, "w_out": w_out, "w_ff1": w_ff1}
    WBF = {k: nc.dram_tensor("bf_" + k, list(a.shape), BF16, kind="Internal").ap() for k, a in WSRC.items()}
    WNAME = {id(a): k for k, a in WSRC.items()}

    es = ExitStack()
    with es:
        def sb(name, shape, dt=F32):
            return es.enter_context(nc.sbuf_tensor(name, list(shape), dt))

        par = sb("par_sb", [128, NPAR])
        modv = sb("modv", [128, 96])
        g1s = sb("g1s", [128, 16])
        g2s = sb("g2s", [128, 16])
        W2 = sb("W2", [128, 8, 4, 2, 128], BF16)
        W4 = sb("W4", [128, 32, 4, 2, 32], BF16)
        IT = sb("IT", [128, 8, 4, 128], BF16)
        CA = sb("CA", [128, 7, 32, 2])
        CB = sb("CB", [128, 7, 32, 2])
        cab = sb("cab", [128, 16], BF16)
        onesb = sb("onesb", [128, 128], BF16)
        onesf = sb("onesf", [128, 128])
        zerob = sb("zerob", [128, 128], BF16)
        wslot = [sb(f"wslot{i}", [128, 8192], BF16) for i in range(NSLOT)]
        ps = es.enter_context(nc.psum_tensor("ps", [128, 8, 512], F32))

        def pcol(name, a=0, b=None):
            o, w = PC[name]
            b = w if b is None else b
            return par[:, o + a:o + b]

        ring = {"i": 0}
        bankc = {"i": 0}
        CVPARTS = {"w_in": 7, "w_co": 1, "w_ga": 1, "w_gb": 1, "w_out": 2, "w_ff1": 8}
        cvsem = {nm: [es.enter_context(nc.semaphore(f"cv_{nm}_{i}")) for i in range(n)] for nm, n in CVPARTS.items()}

        def cvt_piece(prog_, nm, i, gate=()):
            srcv = WSRC[nm].rearrange("k (a c) -> (k a) c", c=1024)
            dstv = WBF[nm].rearrange("k (a c) -> (k a) c", c=1024)
            step = srcv.shape[0] // CVPARTS[nm]
            a, b = i * step, (i + 1) * step
            prog_.add("pool", lambda e: e.dma_start(out=dstv[a:b, :], in_=srcv[a:b, :]), list(gate), [("cvt", nm, i)], dsem=cvsem[nm][i])

        def bank():
            b = bankc["i"] % 8
            bankc["i"] += 1
            return b

        with ExitStack() as e1:
            def sb1(name, shape, dt=F32):
                return e1.enter_context(nc.sbuf_tensor(name, list(shape), dt))

            def sem1(name):
                return e1.enter_context(nc.semaphore(name))

            prog = Prog()
            sems = {k: sem1("s1_" + k) for k in ("pe", "act", "dve", "pool")}
            wsem = [sem1(f"s1_w{i}") for i in range(NSLOT)]
            ldsem = [sem1(f"s1_ld{i}") for i in range(3)]
            sB = sb1("sB", [128, 5, 512])
            sC = sb1("sC", [128, 7, 512])
            tmps = {}

            def T(name):
                if name not in tmps:
                    tmps[name] = sb1("t_" + name, [128, 512])
                return tmps[name]

            def ap_of(x):
                return T(x)[:] if isinstance(x, str) else x[0]

            def key_of(x):
                return ("t", x) if isinstance(x, str) else x[1]

            def tt(out, a, b, op, eng="dve"):
                o, x, y = ap_of(out), ap_of(a), ap_of(b)
                prog.add(eng, lambda e: e.tensor_tensor(out=o, in0=x, in1=y, op=op),
                         [key_of(a), key_of(b)], [key_of(out)])

            def ts(out, a, s1, s2, op0, op1=None, eng="dve", extra=()):
                o, x = ap_of(out), ap_of(a)
                if op1 is None:
                    prog.add(eng, lambda e: e.tensor_scalar(out=o, in0=x, scalar1=s1, scalar2=None, op0=op0),
                             [key_of(a)] + list(extra), [key_of(out)])
                else:
                    prog.add(eng, lambda e: e.tensor_scalar(out=o, in0=x, scalar1=s1, scalar2=s2, op0=op0, op1=op1),
                             [key_of(a)] + list(extra), [key_of(out)])

            def act(out, a, func, scale=1.0, bias=0.0, extra=()):
                o, x = ap_of(out), ap_of(a)
                prog.add("act", lambda e: e.activation(out=o, in_=x, func=func, bias=bias, scale=scale),
                         [key_of(a)] + list(extra), [key_of(out)])

            prog.add("sp", lambda e: e.dma_start(out=par[:], in_=par_d), [], ["par"], dsem=ldsem[0])
            prog.add("sp", lambda e: e.dma_start(out=sB[:], in_=ssmB_d), [], ["sB"], dsem=ldsem[1])
            prog.add("sp", lambda e: e.dma_start(out=sC[:], in_=ssmC_d), [], ["sC"], dsem=ldsem[2])
            prog.add("dve", lambda e: e.memset(onesb[:], 1.0), [], ["onesb"])
            prog.add("dve", lambda e: e.memset(onesf[:], 1.0), [], ["onesf"])
            prog.add("dve", lambda e: e.memset(zerob[:], 0.0), [], ["zerob"])

            prog.add("act", lambda e: e.activation(out=cab[:], in_=pcol("cvec"), func=AF.Silu),
                     ["par"], ["cab"])
            w_ada_r = w_ada.rearrange("(kt p) c -> p kt c", p=128)
            mb = bank()
            for ch in range(8):
                s = ring["i"] % NSLOT
                ring["i"] += 1
                wv = wslot[s][:, 0:8192].rearrange("p (k c) -> p k c", k=16)
                src = w_ada_r[:, :, ch * 512:(ch + 1) * 512]
                prog.add("pool", lambda e, wv=wv, src=src: e.dma_start(out=wv, in_=src),
                         [], [("w", s)], dsem=wsem[s])
                for ti in range(4):
                    j = ch * 4 + ti

                    def fn(e, wv=wv, ti=ti, j=j):
                        ins = None
                        for kt in range(16):
                            ins = e.matmul(ps[:, mb, j:j + 1], lhsT=wv[:, kt, ti * 128:(ti + 1) * 128],
                                           rhs=cab[:, kt:kt + 1], start=(kt == 0), stop=(kt == 15))
                        return ins
                    prog.add("pe", fn, [("w", s), "cab"], [("ps", mb)])
            for i_ in range(CVPARTS["w_in"]):
                cvt_piece(prog, "w_in", i_)
            prog.add("dve", lambda e: e.tensor_tensor(out=modv[:, 0:32], in0=ps[:, mb, 0:32], in1=pcol("bada", 0, 32), op=ALU.add),
                     [("ps", mb), "par"], ["modv"])
            prog.add("dve", lambda e: e.scalar_tensor_tensor(out=g1s[:], in0=modv[:, 16:32], scalar=1.0, in1=pcol("n1g"),
                                                             op0=ALU.add, op1=ALU.mult), ["modv", "par"], ["g1s"])

            def cmul(outr, outi, ar, ai, br, bi):
                tt("c1", ar, br, ALU.mult)
                tt("c2", ai, bi, ALU.mult)
                tt(outr, "c1", "c2", ALU.subtract)
                tt("c1", ar, bi, ALU.mult)
                tt("c2", ai, br, ALU.mult)
                tt(outi, "c1", "c2", ALU.add)

            def lam_tables(are, aim, ldt):
                act("s0", ldt, AF.Exp)
                tt("s1", are, "s0", ALU.mult)
                act("s1", "s1", AF.Exp)
                tt("s2", aim, "s0", ALU.mult)
                for outn, shift in (("li", 0.0), ("lr", PI / 2)):
                    ts("s3", "s2", shift, None, ALU.add)
                    ts("s4", "s2", shift, None, ALU.add)
                    for k in range(8):
                        thr = (2 * k + 1) * PI
                        ts("s5", "s4", thr, -2 * PI, ALU.is_gt, ALU.mult)
                        tt("s3", "s3", "s5", ALU.add)
                    act("s3", "s3", AF.Sin)
                    tt(outn, "s1", "s3", ALU.mult)
                tt("s0", are, are, ALU.mult)
                tt("s1", aim, aim, ALU.mult)
                tt("s0", "s0", "s1", ALU.add)
                o, x = T("s1")[:], T("s0")[:]
                prog.add("dve", lambda e: e.reciprocal(out=o, in_=x), [("t", "s0")], [("t", "s1")])
                ts("s0", "lr", -1.0, None, ALU.add)
                tt("s2", "s0", are, ALU.mult)
                tt("s3", "li", aim, ALU.mult)
                tt("s2", "s2", "s3", ALU.add)
                tt("fr", "s2", "s1", ALU.mult)
                tt("s2", "li", are, ALU.mult)
                tt("s3", "s0", aim, ALU.mult)
                tt("s2", "s2", "s3", ALU.subtract)
                tt("fi", "s2", "s1", ALU.mult)

            def sBk(i):
                return (sB[:, i, :], "sB")

            def sCk(i):
                return (sC[:, i, :], "sC")

            lam_tables(sBk(0), sBk(1), sBk(2))
            cmul("br", "bi", "fr", "fi", sBk(3), sBk(4))
            cmul("l2r", "l2i", "lr", "li", "lr", "li")
            cmul("l3r", "l3i", "l2r", "l2i", "lr", "li")
            for b in range(4):
                if b == 3:
                    vr, vi = "br", "bi"
                else:
                    pw = {0: ("l3r", "l3i"), 1: ("l2r", "l2i"), 2: ("lr", "li")}[b]
                    cmul("wr", "wi", pw[0], pw[1], "br", "bi")
                    vr, vi = "wr", "wi"
                for plane, vn in ((0, vr), (1, vi)):
                    for j in range(2):
                        o = W2[:, :, b, plane, j * 64:(j + 1) * 64]
                        x = T(vn)[:].rearrange("p (c q) -> p c q", c=8)
                        m = pcol("mB", j, j + 1)
                        prog.add("dve", lambda e, o=o, x=x, m=m: e.tensor_scalar(out=o, in0=x, scalar1=m, scalar2=None,
                                                                                  op0=ALU.mult),
                                 [("t", vn), "par"], ["W2"])

            lam_tables(sCk(0), sCk(1), sCk(2))
            cmul("l2r", "l2i", "lr", "li", "lr", "li")
            cmul("l3r", "l3i", "l2r", "l2i", "lr", "li")
            cmul("l4r", "l4i", "l2r", "l2i", "l2r", "l2i")
            cmul("l8r", "l8i", "l4r", "l4i", "l4r", "l4i")
            cmul("l12r", "l12i", "l8r", "l8i", "l4r", "l4i")
            cmul("l16r", "l16i", "l8r", "l8i", "l8r", "l8i")
            cmul("l32r", "l32i", "l16r", "l16i", "l16r", "l16i")
            cmul("l64r", "l64i", "l32r", "l32i", "l32r", "l32i")
            cmul("l128r", "l128i", "l64r", "l64i", "l64r", "l64i")
            for i, nm in enumerate(("l4", "l8", "l12", "l16", "l32", "l64", "l128")):
                xr = T(nm + "r")[:].rearrange("p (a h) -> p a h", h=16)[:, :, 0]
                xi = T(nm + "i")[:].rearrange("p (a h) -> p a h", h=16)[:, :, 0]
                for o, x, sg, kn in ((CA[:, i, :, 0], xr, 1.0, nm + "r"), (CA[:, i, :, 1], xr, 1.0, nm + "r"),
                                     (CB[:, i, :, 0], xi, -1.0, nm + "i"), (CB[:, i, :, 1], xi, 1.0, nm + "i")):
                    prog.add("dve", lambda e, o=o, x=x, sg=sg: e.tensor_scalar(out=o, in0=x, scalar1=sg, scalar2=None,
                                                                              op0=ALU.mult), [("t", kn)], ["LC"])
            pws = [("lr", "li"), ("l2r", "l2i"), ("l3r", "l3i"), ("l4r", "l4i")]
            for b in range(4):
                cmul("wr", "wi", pws[b][0], pws[b][1], sCk(3), sCk(4))
                for plane, vn, sgn in ((0, "wr", 1.0), (1, "wi", -1.0)):
                    for j in range(2):
                        o = W4[:, :, b, plane, j * 16:(j + 1) * 16]
                        x = T(vn)[:].rearrange("p (a h) -> p a h", h=16)
                        m = pcol("mC", j, j + 1)
                        prog.add("dve", lambda e, o=o, x=x, m=m, sgn=sgn: e.tensor_scalar(
                            out=o, in0=x, scalar1=m, scalar2=sgn, op0=ALU.mult, op1=ALU.mult),
                            [("t", vn), "par"], ["W4"])
            cmul("br", "bi", "fr", "fi", sCk(5), sCk(6))
            Lm = sb1("Lm", [128, 2, 32, 32])
            Rm = sb1("Rm", [128, 2, 32, 32])
            ITf = sb1("ITf", [128, 8, 128])
            for plane, (src, sgn) in enumerate(((sCk(3), 1.0), (sCk(4), -1.0))):
                for j in range(2):
                    o = Rm[:, plane, :, j * 16:(j + 1) * 16]
                    x = src[0].rearrange("p (a h) -> p a h", h=16)
                    m = pcol("mC", j, j + 1)
                    prog.add("dve", lambda e, o=o, x=x, m=m, sgn=sgn: e.tensor_scalar(
                        out=o, in0=x, scalar1=m, scalar2=sgn, op0=ALU.mult, op1=ALU.mult), ["sC", "par"], ["Rm"])
            prog.add("dve", lambda e: e.memset(ITf[:], 0.0), [], ["ITf"])
            prog.add("dve", lambda e: e.memset(IT[:], 0.0), [], ["IT"])
            ident = pcol("ident")
            for lag in range(4):
                if lag == 0:
                    vr, vi = "br", "bi"
                else:
                    cmul("wr", "wi", pws[lag - 1][0], pws[lag - 1][1], "br", "bi")
                    vr, vi = "wr", "wi"
                for plane, vn in ((0, vr), (1, vi)):
                    for j in range(2):
                        o = Lm[:, plane, :, j * 16:(j + 1) * 16]
                        x = T(vn)[:].rearrange("p (a h) -> p a h", h=16)
                        m = pcol("mC", j, j + 1)
                        prog.add("dve", lambda e, o=o, x=x, m=m: e.tensor_scalar(out=o, in0=x, scalar1=m, scalar2=None,
                                                                                  op0=ALU.mult),
                                 [("t", vn), "par"], ["Lm"])
                bk = bank()
                for pt in range(32):
                    ct, q = pt // 4, pt % 4
                    o = ps[32 * q:32 * q + 32, bk, ct * 32:ct * 32 + 32]

                    def fn(e, o=o, pt=pt, q=q):
                        e.matmul(o, lhsT=Lm[:, 0, pt, :], rhs=Rm[:, 0, pt, :], start=True, stop=False,
                                 tile_position=(0, 32 * q))
                        return e.matmul(o, lhsT=Lm[:, 1, pt, :], rhs=Rm[:, 1, pt, :], start=False, stop=True,
                                        tile_position=(0, 32 * q))
                    prog.add("pe", fn, ["Lm", "Rm"], [("ps", bk)])
                for q in range(4):
                    x = ps[32 * q:32 * q + 32, bk, 0:256].rearrange("p (c k) -> p c k", c=8)
                    if lag == 0:
                        o = ITf[32 * q:32 * q + 32, :, 32 * q:32 * q + 32]
                        prog.add("dve", lambda e, o=o, x=x: e.tensor_copy(out=o, in_=x), [("ps", bk)], ["ITf"])
                    else:
                        o = IT[32 * q:32 * q + 32, :, lag, 32 * q:32 * q + 32]
                        prog.add("dve", lambda e, o=o, x=x: e.tensor_copy(out=o, in_=x), [("ps", bk)], ["IT"])
                if lag == 0:
                    for ct in range(8):
                        o = ITf[:, ct, :]
                        d = pcol("dsk", ct, ct + 1)
                        prog.add("dve", lambda e, o=o, d=d: e.scalar_tensor_tensor(out=o, in0=ident, scalar=d, in1=o,
                                                                                  op0=ALU.mult, op1=ALU.add),
                                 ["par", "ITf"], ["ITf"])
                    prog.add("dve", lambda e: e.tensor_copy(out=IT[:, :, 0, :], in_=ITf[:]), ["ITf"], ["IT"])
            prog.add("sp", lambda e: None, ["IT", "W4", "W2", "LC", "g1s", "g2s", "modv", "onesb", "onesf", "zerob"], [])
            with nc.Block() as block:
                prog.emit(block, sems)
            nc.all_engine_barrier()

        with ExitStack() as e2:
            def sb2(name, shape, dt=F32):
                return e2.enter_context(nc.sbuf_tensor(name, list(shape), dt))

            def sem2(name):
                return e2.enter_context(nc.semaphore(name))

            prog = Prog()
            sems = {k: sem2("s2_" + k) for k in ("pe", "act", "dve", "pool")}
            wsem = [sem2(f"s2_w{i}") for i in range(NSLOT)]
            wsemH = [sem2(f"s2_wh{i}") for i in range(NSLOT)]
            hsem = sem2("s2_h")
            osem = sem2("s2_o")
            h = sb2("h", [128, 16, NT])
            u = sb2("u", [128, 16, NT], BF16)
            v = sb2("v", [128, 8, 32 + NT], BF16)
            vssm = sb2("vssm", [128, 8, NT], BF16)
            R16 = sb2("R16", [128, 16, NT], BF16)
            sAB = sb2("sAB", [128, 8, NT], BF16)
            rstd = sb2("rstd", [128, NT])
            mu = sb2("mu", [128, NT])
            sq = [sb2(f"sq{i}", [128, NT], BF16) for i in range(2)]
            tf = [sb2(f"tf{i}", [128, NT]) for i in range(2)]
            X = sb2("X", [128, 32, MC + 1, 2])
            G = sb2("G", [128, 32, NCH + 1, 2])
            ZR = sb2("ZR", [128, 32, MC], BF16)
            ZI = sb2("ZI", [128, 32, MC], BF16)
            P1 = sb2("P1", [128, 32, NCH, 2])
            P2 = sb2("P2", [128, 32, NCH, 2])
            uh = sb2("uh", [128, 16, 32], BF16)
            sgh = sb2("sgh", [128, 8, 32], BF16)
            ring["i"] = 0
            NDG = 4
            DG = sb2("DG", [128, NDG, 128], BF16)
            cnt = {"sq": 0, "tf": 0, "dg": 0}

            shift1, gate1 = modv[:, 0:16], modv[:, 32:48]
            shift2, gate2 = modv[:, 48:64], modv[:, 80:96]

            mode = {"bf": False}

            def wload(W, kt0, nkt, c0, ncols):
                s = ring["i"] % NSLOT
                ring["i"] += 1
                wv = wslot[s][:, 0:nkt * ncols].rearrange("p (k c) -> p k c", k=nkt)
                nm = WNAME.get(id(W))
                if mode["bf"] and nm is not None:
                    src = WBF[nm].rearrange("(kt p) c -> p kt c", p=128)[:, kt0:kt0 + nkt, c0:c0 + ncols]
                    prog.add("sp", lambda e: e.dma_start(out=wv, in_=src), cvt_keys[nm], [("w", s)], dsem=wsemH[s])
                else:
                    src = W.rearrange("(kt p) c -> p kt c", p=128)[:, kt0:kt0 + nkt, c0:c0 + ncols]
                    prog.add("pool", lambda e: e.dma_start(out=wv, in_=src), [], [("w", s)], dsem=wsem[s])
                return wv, ("w", s)

            cvt_keys = {nm: [("cvt", nm, i) for i in range(n)] for nm, n in CVPARTS.items()}
            for i_ in range(CVPARTS["w_in"]):
                prog.add("pool", lambda e: None, [], [("cvt", "w_in", i_)], dsem=cvsem["w_in"][i_])
            cvt_plan = [(nm, i_) for nm in ("w_co", "w_gb", "w_ga", "w_out", "w_ff1") for i_ in range(CVPARTS[nm])]

            def cvt_next(gate=()):
                if cvt_plan:
                    nm, i_ = cvt_plan.pop(0)
                    cvt_piece(prog, nm, i_, gate)

            def mmg(out, pairs, reads, writes):
                def fn(e):
                    ins = None
                    n = len(pairs)
                    for i, (l, r) in enumerate(pairs):
                        ins = e.matmul(out, lhsT=l, rhs=r, start=(i == 0), stop=(i == n - 1))
                    return ins
                prog.add("pe", fn, reads, writes)

            def mod_load(chs):
                return [(wload(w_ada, 0, 16, ch * 512, 512), ch) for ch in chs]

            def mod_compute(loaded):
                for (wv, wk), ch in loaded:
                    b = bank()
                    for ti in range(4):
                        def fn(e, wv=wv, ti=ti, b=b):
                            ins = None
                            for kt in range(16):
                                ins = e.matmul(ps[:, b, ti:ti + 1], lhsT=wv[:, kt, ti * 128:(ti + 1) * 128],
                                               rhs=cab[:, kt:kt + 1], start=(kt == 0), stop=(kt == 15))
                            return ins
                        prog.add("pe", fn, [wk], [("ps", b)])
                    j0 = ch * 4
                    prog.add("dve", lambda e, b=b, j0=j0: e.tensor_tensor(out=modv[:, j0:j0 + 4], in0=ps[:, b, 0:4],
                                                                          in1=pcol("bada", j0, j0 + 4), op=ALU.add),
                             [("ps", b)], ["modv2"])
                    if ch == 23:
                        prog.add("dve", lambda e: e.scalar_tensor_tensor(out=g2s[:], in0=modv[:, 64:80], scalar=1.0, in1=pcol("n2g"),
                                                                         op0=ALU.add, op1=ALU.mult), ["modv2"], ["modv2"])

            def rmsnorm_stats():
                b = bank()
                for kt in range(16):
                    i = cnt["sq"] % 2
                    cnt["sq"] += 1
                    sqa = sq[i]
                    prog.add("act", lambda e, sqa=sqa, kt=kt: e.activation(out=sqa[:], in_=h[:, kt, :], func=AF.Square),
                             [("h", kt)], [("sq", i)])
                    prog.add("pe", lambda e, sqa=sqa, kt=kt: e.matmul(ps[:, b, :], lhsT=onesb[:], rhs=sqa[:],
                                                                      start=(kt == 0), stop=(kt == 15)),
                             [("sq", i)], [("ps", b)])
                prog.add("act", lambda e: e.activation(out=rstd[:], in_=ps[:, b, :], func=AF.Sqrt, bias=EPS, scale=1.0 / D),
                         [("ps", b)], ["rstd"])
                prog.add("dve", lambda e: e.reciprocal(out=rstd[:], in_=rstd[:]), ["rstd"], ["rstd"])

            def modulate(gs, sh, extra):
                for kt in range(16):
                    i = cnt["tf"] % 2
                    cnt["tf"] += 1
                    t = tf[i]
                    prog.add("dve", lambda e, t=t, kt=kt: e.tensor_tensor(out=t[:], in0=h[:, kt, :], in1=rstd[:], op=ALU.mult),
                             [("h", kt), "rstd"], [("tf", i)])
                    prog.add("act", lambda e, t=t, kt=kt: e.activation(out=u[:, kt, :], in_=t[:], func=AF.Identity,
                                                                       bias=sh[:, kt:kt + 1], scale=gs[:, kt:kt + 1]),
                             [("tf", i)] + extra, [("u", kt)])

            ukeys = [("u", kt) for kt in range(16)]

            SCAN_ENG = "dve"

            def cmuladd(dst, ci, src, n, kd, ks):
                shp = [128, 32, n, 2]
                ca = CA[:, ci, :, :].unsqueeze(2).to_broadcast(shp)
                cb = CB[:, ci, :, :].unsqueeze(2).to_broadcast(shp)
                p1, p2 = P1[:, :, 0:n, :], P2[:, :, 0:n, :]
                srcsw = src[:, :, :, ::-1]
                prog.add(SCAN_ENG, lambda e: e.tensor_tensor(out=p1, in0=src, in1=ca, op=ALU.mult), [ks], ["P1"])
                prog.add(SCAN_ENG, lambda e: e.tensor_tensor(out=p2, in0=srcsw, in1=cb, op=ALU.mult), [ks], ["P2"])
                prog.add(SCAN_ENG, lambda e: e.tensor_tensor(out=p1, in0=p1, in1=p2, op=ALU.add), ["P1", "P2"], ["P1"])
                prog.add(SCAN_ENG, lambda e: e.tensor_tensor(out=dst, in0=dst, in1=p1, op=ALU.add), ["P1", kd], [kd])

            def Xc(a, n=NCH, step=4):
                return X[:, :, 1 + a:2 + a + step * (n - 1):step, :]

            def ssm_q(sb_i):
                t0 = sb_i * SUB
                for half in range(2):
                    banks = [bank() for _ in range(4)]
                    for cl in range(4):
                        ct = half * 4 + cl
                        for q in range(4):
                            for plane in range(2):
                                c0 = (cl * 2 + plane) * MC
                                o = ps[:, banks[q], c0:c0 + MC]

                                def fn(e, o=o, ct=ct, q=q, plane=plane):
                                    ins = None
                                    for b in range(4):
                                        ins = e.matmul(o, lhsT=W2[32 * q:32 * q + 32, ct, b, plane, :],
                                                       rhs=vssm[32 * q:32 * q + 32, ct, t0 + b * MC:t0 + (b + 1) * MC],
                                                       start=(b == 0), stop=(b == 3), tile_position=(32 * q, 0))
                                    return ins
                                prog.add("pe", fn, [("vssm", ct, sb_i)], [("ps", banks[q])])
                    for q in range(4):
                        src = ps[:, banks[q], 0:8 * MC].rearrange("p (c l m) -> p c l m", c=4, l=2)
                        for plane in range(2):
                            o = X[:, half * 16 + q:half * 16 + 16:4, 1:MC + 1, plane]
                            x = src[:, :, plane, :]
                            prog.add("act", lambda e, o=o, x=x: e.activation(out=o, in_=x, func=AF.Identity),
                                     [("ps", banks[q])], ["X"])

            def ssm_scan(prefix, last_prefix_sub):
                for a in range(1, 4):
                    cmuladd(Xc(a), 0, Xc(a - 1), NCH, "X", "X")
                if prefix:
                    cmuladd(X[:, :, 8:33:8, :], 3, X[:, :, 4:29:8, :], 4, "X", "X")
                    cmuladd(X[:, :, 16:33:16, :], 4, X[:, :, 8:25:16, :], 2, "X", "X")
                    cmuladd(X[:, :, 32:33, :], 5, X[:, :, 16:17, :], 1, "X", "X")
                    cmuladd(X[:, :, 32:33, :], 6, G[:, :, 0:1, :], 1, "X", "G")
                    if last_prefix_sub:
                        fl = pcol("flag")
                        prog.add(SCAN_ENG, lambda e: e.tensor_scalar(out=G[:, :, 0:1, :], in0=X[:, :, 32:33, :], scalar1=fl,
                                                                  scalar2=None, op0=ALU.mult), ["X"], ["G"])
                        prog.add(SCAN_ENG, lambda e: e.tensor_copy(out=X[:, :, 0:1, :], in_=G[:, :, 0:1, :]), ["G"], ["X"])
                    else:
                        prog.add(SCAN_ENG, lambda e: e.tensor_copy(out=G[:, :, 0:1, :], in_=X[:, :, 32:33, :]), ["X"], ["G"])
                    return
                prog.add(SCAN_ENG, lambda e: e.tensor_copy(out=G[:, :, 1:NCH + 1, :], in_=X[:, :, 4:4 * NCH + 1:4, :]),
                         ["X"], ["G"])
                for k in range(4):
                    d = 1 << k
                    cmuladd(G[:, :, d:NCH + 1, :], 3 + k, G[:, :, 0:NCH + 1 - d, :], NCH + 1 - d, "G", "G")
                for a in range(3):
                    cmuladd(Xc(a), a, G[:, :, 0:NCH, :], NCH, "X", "G")
                prog.add(SCAN_ENG, lambda e: e.tensor_copy(out=Xc(3), in_=G[:, :, 1:NCH + 1, :]), ["G"], ["X"])
                for Z, pl in ((ZR, 0), (ZI, 1)):
                    prog.add("act", lambda e, Z=Z, pl=pl: e.activation(out=Z[:], in_=X[:, :, 0:MC, pl], func=AF.Identity),
                             ["X"], ["Z"])
                prog.add(SCAN_ENG, lambda e: e.tensor_copy(out=G[:, :, 0:1, :], in_=G[:, :, NCH:NCH + 1, :]), ["G"], ["G"])
                prog.add(SCAN_ENG, lambda e: e.tensor_copy(out=X[:, :, 0:1, :], in_=G[:, :, 0:1, :]), ["G", "Z"], ["X"])

            def ssm_y(sb_i):
                t0 = sb_i * SUB
                for ct in range(8):
                    b_ = bank()
                    Y = ps[:, b_, 0:SUB]

                    def fn(e, ct=ct, b_=b_):
                        e.matmul(ps[:, b_, 0:SUB], lhsT=zerob[:], rhs=vssm[:, ct, t0:t0 + SUB], start=True, stop=False)
                        for b in range(4):
                            for b2 in range(b + 1):
                                e.matmul(ps[:, b_, b * MC:(b + 1) * MC], lhsT=IT[:, ct, b - b2, :],
                                         rhs=vssm[:, ct, t0 + b2 * MC:t0 + (b2 + 1) * MC], start=False, stop=False)
                        ins = None
                        for q in range(4):
                            pt = ct * 4 + q
                            for b in range(4):
                                for plane, Z in ((0, ZR), (1, ZI)):
                                    last = (b == 3 and plane == 1)
                                    ins = e.matmul(ps[32 * q:32 * q + 32, b_, b * MC:(b + 1) * MC],
                                                   lhsT=W4[:, pt, b, plane, :], rhs=Z[:, pt, :], start=False, stop=last,
                                                   tile_position=(0, 32 * q))
                        return ins
                    prog.add("pe", fn, [("vssm", ct, sb_i), "Z"], [("ps", b_)])
                    o = vssm[:, ct, t0:t0 + SUB].rearrange("p (m b) -> p b m", b=4)
                    Yv = Y.rearrange("p (b m) -> p b m", b=4)
                    prog.add("act", lambda e, o=o, Yv=Yv: e.activation(out=o, in_=Yv, func=AF.Gelu_apprx_tanh),
                             [("ps", b_)], [("vssm", ct, sb_i)])

            def win_tile(wv, wk, ti, evac):
                b = bank()
                wkl = list(wk) if isinstance(wk, list) else [wk]
                mmg(ps[:, b, :], [(wv[:, kt, ti * 128:(ti + 1) * 128], u[:, kt, :]) for kt in range(16)],
                    wkl + ukeys, [("ps", b)])
                evac(b)

            def pre_vssm():
                return [wload(w_in, 0, 16, ch * 512, 512) for ch in (4, 5)]

            def do_vssm(pre):
                ct = 0
                for wv, wk, ntl in pre:
                    for ti in range(ntl):
                        def evac(b, ct=ct):
                            o = vssm[:, ct, :].rearrange("p (s b m) -> p s b m", s=NSUB, b=4)
                            x = ps[:, b, :].rearrange("p (s m b) -> p s b m", s=NSUB, b=4)
                            prog.add("act", lambda e, o=o, x=x: e.activation(out=o, in_=x, func=AF.Identity),
                                     [("ps", b)], [("vssm", ct, s_) for s_ in range(NSUB)])
                        win_tile(wv, wk, ti, evac)
                        ct += 1

            RK = [("R", j) for j in range(16)]
            SK = [("sAB", j) for j in range(8)]
            VK = [("v", c) for c in range(8)] + [("vh", c) for c in range(8)]

            def load_resident():
                w_in_r = w_in.rearrange("(kt p) c -> p kt c", p=128)
                sabw = sAB[:].rearrange("p c t -> p (c t)").rearrange("p (k c) -> p k c", k=16)
                vw = v[:].rearrange("p c t -> p (c t)")[:, 0:4096].rearrange("p (k c) -> p k c", k=16)
                res = []
                for i, (dst, c0, nc_, keys, ntl) in enumerate(((R16[:], 2048, 512, RK, 4), (sabw, 2560, 256, SK, 2),
                                                              (vw, 2816, 256, VK, 2))):
                    sem = sem2(f"s2_res{i}")
                    src = w_in_r[:, :, c0:c0 + nc_]
                    prog.add("pool", lambda e, dst=dst, src=src: e.dma_start(out=dst, in_=src), [], keys, dsem=sem)
                    res.append((dst, keys, ntl))
                return res

            def do_glu(first):
                fl = pcol("flag")
                for ch in (2, 3, 0, 1):
                    wv, wk = wload(w_in, 0, 16, ch * 512, 512)
                    for ti in range(4):
                        ct = (ch % 2) * 4 + ti
                        if ch >= 2:
                            def evac(b, ct=ct):
                                prog.add("act", lambda e: e.activation(out=sAB[:, ct, :], in_=ps[:, b, :], func=AF.Sigmoid),
                                         [("ps", b)], [("sAB", ct)])
                        else:
                            def evac(b, ct=ct):
                                prog.add("dve", lambda e: e.tensor_tensor(out=v[:, ct, 32:32 + NT], in0=ps[:, b, :],
                                                                          in1=sAB[:, ct, :], op=ALU.mult),
                                         [("ps", b), ("sAB", ct)], [("v", ct)])
                        win_tile(wv, wk, ti, evac)
                        if first:
                            b2 = bank()
                            mmg(ps[:, b2, 0:32], [(wv[:, kt, ti * 128:(ti + 1) * 128], uh[:, kt, :]) for kt in range(16)],
                                [wk, "uh"], [("ps", b2)])
                            if ch >= 2:
                                prog.add("act", lambda e, ct=ct, b2=b2: e.activation(out=sgh[:, ct, :], in_=ps[:, b2, 0:32], func=AF.Sigmoid),
                                         [("ps", b2)], [("sgh", ct)])
                            else:
                                prog.add("dve", lambda e, ct=ct, b2=b2: e.scalar_tensor_tensor(out=v[:, ct, 0:32], in0=ps[:, b2, 0:32], scalar=fl,
                                                                                             in1=sgh[:, ct, :], op0=ALU.mult, op1=ALU.mult),
                                         [("ps", b2), ("sgh", ct)], [("vh", ct)])

            def halo_copy(ct, use_flag):
                if use_flag:
                    fl = pcol("flag")
                    prog.add("dve", lambda e: e.tensor_scalar(out=v[:, ct, 0:32], in0=v[:, ct, NT:NT + 32], scalar1=fl,
                                                              scalar2=None, op0=ALU.mult), [("v", ct)], [("vh", ct)])
                else:
                    prog.add("dve", lambda e: e.tensor_copy(out=v[:, ct, 0:32], in_=v[:, ct, NT:NT + 32]),
                             [("v", ct)], [("vh", ct)])

            def conv_ct(ct):
                wdw = pcol("wdw")
                ident = pcol("ident")
                b = bank()
                for k in range(31):
                    i = cnt["dg"] % NDG
                    cnt["dg"] += 1
                    wk_ = wdw[:, ct * 31 + k:ct * 31 + k + 1]
                    prog.add("act", lambda e, i=i, wk_=wk_: e.activation(out=DG[:, i, :], in_=ident, func=AF.Identity, scale=wk_),
                             [], [("dg", i)])
                    src = v[:, ct, 2 + k:2 + k + NT]
                    prog.add("pe", lambda e, i=i, src=src, k=k: e.matmul(ps[:, b, :], lhsT=DG[:, i, :], rhs=src,
                                                                        start=(k == 0), stop=(k == 30)),
                             [("dg", i), ("v", ct), ("vh", ct)], [("ps", b)])
                bd = pcol("bdw", ct, ct + 1)
                prog.add("act", lambda e: e.activation(out=sAB[:, ct, :], in_=ps[:, b, :], func=AF.Identity, bias=bd),
                         [("ps", b)], [("sAB", ct)])
                halo_copy(ct, False)

            def do_ln():
                b1, b2 = bank(), bank()
                for ct in range(8):
                    rk = [("sAB", ct)]
                    x = sAB[:, ct, :]
                    prog.add("pe", lambda e, x=x, ct=ct: e.matmul(ps[:, b1, :], lhsT=onesb[:], rhs=x, start=(ct == 0), stop=(ct == 7)),
                             rk, [("ps", b1)])
                    i = cnt["sq"] % 2
                    cnt["sq"] += 1
                    t = sq[i]
                    prog.add("act", lambda e, x=x, t=t: e.activation(out=t[:], in_=x, func=AF.Square), rk, [("sq", i)])
                    prog.add("pe", lambda e, t=t, ct=ct: e.matmul(ps[:, b2, :], lhsT=onesb[:], rhs=t[:], start=(ct == 0), stop=(ct == 7)),
                             [("sq", i)], [("ps", b2)])
                prog.add("dve", lambda e: e.tensor_scalar(out=mu[:], in0=ps[:, b1, :], scalar1=1.0 / 1024, scalar2=None, op0=ALU.mult),
                         [("ps", b1)], ["mu"])
                prog.add("dve", lambda e: e.tensor_tensor(out=rstd[:], in0=mu[:], in1=mu[:], op=ALU.mult), ["mu"], ["rstd"])
                prog.add("dve", lambda e: e.scalar_tensor_tensor(out=rstd[:], in0=ps[:, b2, :], scalar=1.0 / 1024, in1=rstd[:],
                                                                 op0=ALU.mult, op1=ALU.subtract), [("ps", b2), "rstd"], ["rstd"])
                prog.add("act", lambda e: e.activation(out=rstd[:], in_=rstd[:], func=AF.Sqrt, bias=EPS, scale=1.0), ["rstd"], ["rstd"])
                prog.add("dve", lambda e: e.reciprocal(out=rstd[:], in_=rstd[:]), ["rstd"], ["rstd"])
                for ct in range(8):
                    rk = [("sAB", ct)]
                    i = cnt["tf"] % 2
                    cnt["tf"] += 1
                    t = tf[i]
                    x = sAB[:, ct, :]
                    prog.add("dve", lambda e, x=x, t=t: e.tensor_tensor(out=t[:], in0=x, in1=mu[:], op=ALU.subtract),
                             rk + ["mu"], [("tf", i)])
                    prog.add("dve", lambda e, t=t: e.tensor_tensor(out=t[:], in0=t[:], in1=rstd[:], op=ALU.mult),
                             [("tf", i), "rstd"], [("tf", i)])
                    lg, lb = pcol("lng", ct, ct + 1), pcol("lnb", ct, ct + 1)
                    prog.add("act", lambda e, t=t, ct=ct, lg=lg, lb=lb: e.activation(out=v[:, ct, 32:32 + NT], in_=t[:], func=AF.Silu,
                                                                                    bias=lb, scale=lg),
                             [("tf", i), ("vh", ct)], [("v", ct)])

            def merge_p1(chs, pre):
                for ch in chs:
                    wv, wk = pre
                    for ti in range(4):
                        j = (ch - 6) * 4 + ti

                        def evac(b, j=j):
                            prog.add("act", lambda e: e.activation(out=R16[:, j, :], in_=ps[:, b, :], func=AF.Sigmoid),
                                     [("ps", b)], [("R", j)])
                        win_tile(wv, wk, ti, evac)

            def merge_p23():
                for ch in range(2):
                    wv, wk = wload(w_co, 0, 8, ch * 1024, 1024)
                    for ti in range(8):
                        j = ch * 8 + ti
                        b = bank()
                        mmg(ps[:, b, :], [(wv[:, kt, ti * 128:(ti + 1) * 128], v[:, kt, 32:32 + NT]) for kt in range(8)],
                            [wk] + [("v", kt) for kt in range(8)], [("ps", b)])
                        prog.add("dve", lambda e, j=j, b=b: e.tensor_tensor(out=R16[:, j, :], in0=ps[:, b, :], in1=R16[:, j, :],
                                                                            op=ALU.mult), [("ps", b), ("R", j)], [("R", j)])
                skeys = [("vssm", kt, s_) for kt in range(8) for s_ in range(NSUB)]
                for jb in range(4):
                    wv, wk = wload(w_in, 0, 16, (10 + jb) * 512, 512)
                    for ti in range(4):
                        def evac(b, ti=ti):
                            prog.add("act", lambda e: e.activation(out=sAB[:, ti, :], in_=ps[:, b, :], func=AF.Sigmoid),
                                     [("ps", b)], [("sAB", ti)])
                        win_tile(wv, wk, ti, evac)
                    for which, W in ((0, w_gb), (1, w_ga)):
                        wv, wk = wload(W, 0, 8, jb * 512, 512)
                        for ti in range(4):
                            j = jb * 4 + ti
                            b = bank()
                            mmg(ps[:, b, :], [(wv[:, kt, ti * 128:(ti + 1) * 128], vssm[:, kt, :]) for kt in range(8)],
                                [wk] + skeys, [("ps", b)])
                            kB = ("sAB", 4 + ti)
                            if which == 0:
                                prog.add("act", lambda e, ti=ti, b=b: e.activation(out=sAB[:, 4 + ti, :], in_=ps[:, b, :], func=AF.Sigmoid),
                                         [("ps", b)], [kB])
                            else:
                                prog.add("dve", lambda e, ti=ti, b=b: e.tensor_tensor(out=sAB[:, 4 + ti, :], in0=ps[:, b, :], in1=sAB[:, 4 + ti, :],
                                                                                      op=ALU.mult), [("ps", b), kB], [kB])
                                prog.add("dve", lambda e, ti=ti: e.tensor_tensor(out=sAB[:, 4 + ti, :], in0=sAB[:, 4 + ti, :], in1=sAB[:, ti, :],
                                                                                 op=ALU.mult), [kB, ("sAB", ti)], [kB])
                                prog.add("dve", lambda e, ti=ti, j=j: e.tensor_tensor(out=R16[:, j, :], in0=R16[:, j, :], in1=sAB[:, 4 + ti, :],
                                                                                      op=ALU.add), [kB, ("R", j)], [("R", j)])

            def do_wout():
                for ch in range(4):
                    wv, wk = wload(w_out, 0, 16, ch * 512, 512)
                    for ti in range(4):
                        j = ch * 4 + ti
                        b = bank()
                        mmg(ps[:, b, :], [(wv[:, kt, ti * 128:(ti + 1) * 128], R16[:, kt, :]) for kt in range(16)],
                            [wk] + [("R", kt) for kt in range(16)], [("ps", b)])
                        prog.add("dve", lambda e, j=j, b=b: e.scalar_tensor_tensor(out=h[:, j, :], in0=ps[:, b, :], scalar=gate1[:, j:j + 1],
                                                                                   in1=h[:, j, :], op0=ALU.mult, op1=ALU.add),
                                 [("ps", b), ("h", j), "modv2"], [("h", j)])

            def do_ffn():
                for hb in range(4):
                    for c4 in range(4):
                        wv, wk = wload(w_ff1, 0, 16, (hb * 4 + c4) * 512, 512)
                        for ti in range(4):
                            i_ = c4 * 4 + ti

                            def evac(b, i_=i_):
                                k = cnt["sq"] % 2
                                cnt["sq"] += 1
                                s_ = sq[k]
                                prog.add("act", lambda e: e.activation(out=s_[:], in_=ps[:, b, :], func=AF.Relu), [("ps", b)], [("sq", k)])
                                prog.add("dve", lambda e: e.tensor_tensor(out=R16[:, i_, :], in0=ps[:, b, :], in1=s_[:], op=ALU.mult),
                                         [("ps", b), ("sq", k)], [("R", i_)])
                            win_tile(wv, wk, ti, evac)
                    for cb in range(4):
                        wv, wk = wload(w_ff2, hb * 16, 16, cb * 512, 512)
                        for ti in range(4):
                            j = cb * 4 + ti
                            b = bank()
                            mmg(ps[:, b, :], [(wv[:, kt, ti * 128:(ti + 1) * 128], R16[:, kt, :]) for kt in range(16)],
                                [wk] + [("R", kt) for kt in range(16)], [("ps", b)])
                            prog.add("dve", lambda e, j=j, b=b: e.scalar_tensor_tensor(out=h[:, j, :], in0=ps[:, b, :], scalar=gate2[:, j:j + 1],
                                                                                       in1=h[:, j, :], op0=ALU.mult, op1=ALU.add),
                                     [("ps", b), ("h", j)], [("h", j)])

            hkeys = [("h", kt) for kt in range(16)]
            prog.add("dve", lambda e: e.memset(G[:], 0.0), [], ["G"])
            prog.add("dve", lambda e: e.memset(X[:], 0.0), [], ["X"])
            prog.add("dve", lambda e: e.memset(v[:], 0.0), [], [("v", c) for c in range(8)] + [("vh", c) for c in range(8)])

            steps = [("p", i) for i in range(NTILE)] + [("m", i) for i in range(NTILE)]
            resw = load_resident()
            for kind, ti_ in steps:
                srcT = xpT if kind == "p" else xT
                src = srcT.rearrange("(kt p) t -> p kt t", p=128)[:, :, ti_ * NT:(ti_ + 1) * NT]
                prog.add("sp", lambda e, src=src: e.dma_start(out=h[:], in_=src), [], hkeys, dsem=hsem)
                rmsnorm_stats()
                modulate(g1s, shift1, [])
                if kind == "p":
                    ml = mod_load([8 + 2 * ti_, 9 + 2 * ti_])
                    do_vssm(resw)
                    for s_i in range(NSUB):
                        ssm_q(s_i)
                        cvt_next(["X"])
                        ssm_scan(True, ti_ == NTILE - 1 and s_i == NSUB - 1)
                    mod_compute(ml)
                    if ti_ == NTILE - 1:
                        prog.add("act", lambda e: e.activation(out=uh[:], in_=u[:, :, NT - 32:NT], func=AF.Identity), ukeys, ["uh"])
                    continue
                while cvt_plan:
                    cvt_next()
                mode["bf"] = True
                do_glu(ti_ == 0)
                do_vssm([(wv_, wk_, 4) for wv_, wk_ in pre_vssm()])
                for s_i in range(NSUB):
                    p1w = wload(w_in, 0, 16, (6 + s_i) * 512, 512)
                    ssm_q(s_i)
                    ssm_scan(False, False)
                    conv_ct(2 * s_i)
                    conv_ct(2 * s_i + 1)
                    merge_p1([6 + s_i], p1w)
                    if ti_ == 0:
                        mod_compute(mod_load([16 + 2 * s_i, 17 + 2 * s_i]))
                    ssm_y(s_i)
                do_ln()
                merge_p23()
                do_wout()
                rmsnorm_stats()
                modulate(g2s, shift2, ["modv2"])
                do_ffn()
                rmsnorm_stats()
                fgc = pcol("fg")
                for kt in range(16):
                    prog.add("dve", lambda e, kt=kt: e.scalar_tensor_tensor(out=h[:, kt, :], in0=h[:, kt, :], scalar=fgc[:, kt:kt + 1],
                                                                            in1=rstd[:], op0=ALU.mult, op1=ALU.mult),
                             [("h", kt), "rstd"], [("h", kt)])
                dst = outT.rearrange("(kt p) t -> p kt t", p=128)[:, :, ti_ * NT:(ti_ + 1) * NT]
                prog.add("sp", lambda e, dst=dst: e.dma_start(out=dst, in_=h[:]), hkeys, [("out", ti_)], dsem=osem)
            prog.add("sp", lambda e: None, [("out", i) for i in range(NTILE)], [])
            with nc.Block() as block:
                prog.emit(block, sems)
    return nc


_NC = None


def _prep(x, c, w_ada, b_ada, norm1_g, w_in, w_dw, b_dw, ln_g, ln_b, w_conv_out, a_re, a_im, log_dt, b_re, b_im,
          c_re, c_im, d_skip, w_glu_a, w_glu_b, w_out, norm2_g, w_ff1, w_ff2, final_g):
    f = np.float32
    shared = {
        "w_ada": np.ascontiguousarray(w_ada[0], f), "w_in": np.ascontiguousarray(w_in[0], f),
        "w_conv_out": np.ascontiguousarray(w_conv_out[0], f), "w_glu_a": np.ascontiguousarray(w_glu_a[0], f),
        "w_glu_b": np.ascontiguousarray(w_glu_b[0], f), "w_out": np.ascontiguousarray(w_out[0], f),
        "w_ff1": np.ascontiguousarray(w_ff1[0], f), "w_ff2": np.ascontiguousarray(w_ff2[0], f),
    }
    are, aim, ldt = a_re[0], a_im[0], log_dt[0]
    bre, bim, cre, cim = b_re[0], b_im[0], c_re[0], c_im[0]
    def Bl_a(a):
        t = a.reshape(8, 8, 64).transpose(1, 0, 2)
        return np.broadcast_to(t[:, None], (8, 16, 8, 64)).reshape(128, 512)
    ldB = np.broadcast_to(ldt.reshape(8, 8).T[:, None, :, None], (8, 16, 8, 64)).reshape(128, 512)
    def Bl_b(b):
        return b.reshape(8, 8, 64, 16).transpose(1, 3, 0, 2).reshape(128, 512)
    ssmB = np.stack([Bl_a(are), Bl_a(aim), ldB, Bl_b(bre), Bl_b(bim)], axis=1).astype(f)
    def Cl_a(a):
        t = a.reshape(32, 2, 64).transpose(1, 2, 0)
        return np.broadcast_to(t[..., None], (2, 64, 32, 16)).reshape(128, 512)
    ldC = np.broadcast_to(ldt.reshape(32, 2).T[:, None, :, None], (2, 64, 32, 16)).reshape(128, 512)
    def Cl_c(cc):
        return cc.reshape(32, 2, 16, 64).transpose(1, 3, 0, 2).reshape(128, 512)
    def Cl_b(b):
        return b.reshape(32, 2, 64, 16).transpose(1, 2, 0, 3).reshape(128, 512)
    ssmC = np.stack([Cl_a(are), Cl_a(aim), ldC, Cl_c(cre), Cl_c(cim), Cl_b(bre), Cl_b(bim)], axis=1).astype(f)
    shared["ssmB"] = np.ascontiguousarray(ssmB)
    shared["ssmC"] = np.ascontiguousarray(ssmC)

    def fm(vec, n):
        return np.asarray(vec, f).reshape(n, 128).T
    par_base = np.zeros((128, NPAR), f)
    def put(name, arr):
        o, w = PC[name]
        par_base[:, o:o + w] = arr
    put("bada", fm(b_ada[0], 96))
    put("n1g", fm(norm1_g[0], 16)); put("n2g", fm(norm2_g[0], 16)); put("fg", fm(final_g, 16))
    put("wdw", np.asarray(w_dw[0], f).reshape(31, 8, 128).transpose(2, 1, 0).reshape(128, 248))
    put("bdw", fm(b_dw[0], 8)); put("lng", fm(ln_g[0], 8)); put("lnb", fm(ln_b[0], 8)); put("dsk", fm(d_skip[0], 8))
    pidx = np.arange(128)
    mB = np.stack([((pidx // 16) % 2 == j) for j in range(2)], axis=1).astype(f)
    mC = np.stack([((pidx // 64) == j) for j in range(2)], axis=1).astype(f)
    put("mB", mB); put("mC", mC); put("ident", np.eye(128, dtype=f))
    in_maps = []
    zeros = np.zeros((D, NTOK), f)
    for core in range(NCORE):
        b, s = core // 2, core % 2
        m = dict(shared)
        m["xT"] = np.ascontiguousarray(x[b, s * NTOK:(s + 1) * NTOK, :].T)
        m["xpT"] = np.ascontiguousarray(x[b, 0:NTOK, :].T) if s == 1 else zeros
        p = par_base.copy()
        o, w = PC["cvec"]; p[:, o:o + w] = fm(c[b], 16)
        o, w = PC["flag"]; p[:, o:o + w] = float(s)
        m["par"] = p
        in_maps.append(m)
    return in_maps


def kernel(**inputs):
    global _NC
    inputs = {k: np.asarray(v) for k, v in inputs.items()}
    in_maps = _prep(**inputs)
    if _NC is None:
        _NC = build()
    res = run_bass_kernel_spmd(_NC, in_maps, core_ids=list(range(NCORE)))
    out = np.empty((4, 4096, D), np.float32)
    for core in range(NCORE):
        b, s = core // 2, core % 2
        out[b, s * NTOK:(s + 1) * NTOK, :] = res.results[core]["outT"].T
    return out
```

```python
import math
import numpy as np
from contextlib import ExitStack
import concourse.bass as bass
import concourse.mybir as mybir
from concourse.bass_utils import run_bass_kernel_spmd

F32, BF16 = mybir.dt.float32, mybir.dt.bfloat16
AF = mybir.ActivationFunctionType
ALU = mybir.AluOpType

NCORE = 8
D = 2048
NTOK = 2048
NT = 512
NTILE = NTOK // NT
SUB = 128
NSUB = NT // SUB
MC = SUB // 4
NCH = MC // 4
NSLOT = 2
EPS = 1e-6
PI = math.pi

PC = {}
_off = 0
for _n, _w in [("cvec", 16), ("bada", 96), ("n1g", 16), ("n2g", 16), ("fg", 16), ("wdw", 248),
               ("bdw", 8), ("lng", 8), ("lnb", 8), ("dsk", 8), ("flag", 1), ("mB", 2), ("mC", 2),
               ("ident", 128)]:
    PC[_n] = (_off, _w)
    _off += _w
NPAR = _off


class Op:
    __slots__ = ("eng", "fn", "deps", "needs", "val", "dsem", "dval")


class Prog:
    ENG = ("pe", "act", "dve", "pool", "sp")

    def __init__(self):
        self.ops = {e: [] for e in self.ENG}
        self.lastw = {}
        self.readers = {}
        self.dmacount = {}

    def add(self, eng, fn, reads=(), writes=(), dsem=None):
        op = Op()
        op.eng, op.fn, op.needs, op.val, op.dsem, op.dval = eng, fn, False, None, dsem, None
        if dsem is not None:
            self.dmacount[id(dsem)] = self.dmacount.get(id(dsem), 0) + 16
            op.dval = self.dmacount[id(dsem)]
        deps, seen = [], set()
        cand = []
        for k in reads:
            w = self.lastw.get(k)
            if w is not None:
                cand.append(w)
        for k in writes:
            w = self.lastw.get(k)
            if w is not None:
                cand.append(w)
            cand.extend(self.readers.get(k, ()))
        for d in cand:
            if id(d) in seen:
                continue
            seen.add(id(d))
            if d.dsem is None and d.eng == eng and eng == "pe":
                continue
            deps.append(d)
        op.deps = deps
        for k in reads:
            self.readers.setdefault(k, []).append(op)
        for k in writes:
            self.lastw[k] = op
            self.readers[k] = []
        self.ops[eng].append(op)
        return op

    def emit(self, block, sems):
        for e in self.ENG:
            for op in self.ops[e]:
                for d in op.deps:
                    if d.dsem is None:
                        d.needs = True
        for e in self.ENG:
            n = 0
            for op in self.ops[e]:
                if op.dsem is None and op.needs:
                    n += 1
                    op.val = n

        def run(engname, eobj):
            known = {}
            for op in self.ops[engname]:
                need = {}
                for d in op.deps:
                    if d.dsem is not None:
                        sem, val = d.dsem, d.dval
                    else:
                        sem, val = sems[d.eng], d.val
                    if need.get(id(sem), (None, 0))[1] < val:
                        need[id(sem)] = (sem, val)
                for sid, (sem, val) in need.items():
                    if known.get(sid, 0) >= val:
                        continue
                    eobj.wait_ge(sem, val)
                    known[sid] = val
                ins = op.fn(eobj)
                if ins is None:
                    continue
                if op.dsem is not None:
                    ins.then_inc(op.dsem, 16)
                elif op.needs:
                    ins.then_inc(sems[engname], 1)

        block.tensor(lambda e: run("pe", e))
        block.scalar(lambda e: run("act", e))
        block.vector(lambda e: run("dve", e))
        block.gpsimd(lambda e: run("pool", e))
        block.sync(lambda e: run("sp", e))


def build():
    nc = bass.Bass("TRN2", target_bir_lowering=False)

    def din(name, shape):
        return nc.dram_tensor(name, list(shape), F32, kind="ExternalInput").ap()

    xT = din("xT", [D, NTOK])
    xpT = din("xpT", [D, NTOK])
    par_d = din("par", [128, NPAR])
    ssmB_d = din("ssmB", [128, 5, 512])
    ssmC_d = din("ssmC", [128, 7, 512])
    w_ada = din("w_ada", [D, 6 * D])
    w_in = din("w_in", [D, 7168])
    w_co = din("w_conv_out", [1024, D])
    w_ga = din("w_glu_a", [1024, D])
    w_gb = din("w_glu_b", [1024, D])
    w_out = din("w_out", [D, D])
    w_ff1 = din("w_ff1", [D, 4 * D])
    w_ff2 = din("w_ff2", [4 * D, D])
    outT = nc.dram_tensor("outT", [D, NTOK], F32, kind="ExternalOutput").ap()
    WSRC = {"w_in": w_in, "w_co": w_co, "w_ga": w_ga, "w_gb": w_gb, "w_out": w_out, "w_ff1": w_ff1}
    WBF = {k: nc.dram_tensor("bf_" + k, list(a.shape), BF16, kind="Internal").ap() for k, a in WSRC.items()}
    WNAME = {id(a): k for k, a in WSRC.items()}

    es = ExitStack()
    with es:
        def sb(name, shape, dt=F32):
            return es.enter_context(nc.sbuf_tensor(name, list(shape), dt))

        par = sb("par_sb", [128, NPAR])
        modv = sb("modv", [128, 96])
        g1s = sb("g1s", [128, 16])
        g2s = sb("g2s", [128, 16])
        W2 = sb("W2", [128, 8, 4, 2, 128], BF16)
        W4 = sb("W4", [128, 32, 4, 2, 32], BF16)
        IT = sb("IT", [128, 8, 4, 128], BF16)
        CA = sb("CA", [128, 7, 32, 2])
        CB = sb("CB", [128, 7, 32, 2])
        cab = sb("cab", [128, 16], BF16)
        onesb = sb("onesb", [128, 128], BF16)
        onesf = sb("onesf", [128, 128])
        zerob = sb("zerob", [128, 128], BF16)
        wslot = [sb(f"wslot{i}", [128, 8192], BF16) for i in range(NSLOT)]
        ps = es.enter_context(nc.psum_tensor("ps", [128, 8, 512], F32))

        def pcol(name, a=0, b=None):
            o, w = PC[name]
            b = w if b is None else b
            return par[:, o + a:o + b]

        ring = {"i": 0}
        bankc = {"i": 0}
        CVPARTS = {"w_in": 7, "w_co": 1, "w_ga": 1, "w_gb": 1, "w_out": 2, "w_ff1": 8}
        cvsem = {nm: [es.enter_context(nc.semaphore(f"cv_{nm}_{i}")) for i in range(n)] for nm, n in CVPARTS.items()}

        def cvt_piece(prog_, nm, i, gate=()):
            srcv = WSRC[nm].rearrange("k (a c) -> (k a) c", c=1024)
            dstv = WBF[nm].rearrange("k (a c) -> (k a) c", c=1024)
            step = srcv.shape[0] // CVPARTS[nm]
            a, b = i * step, (i + 1) * step
            prog_.add("pool", lambda e: e.dma_start(out=dstv[a:b, :], in_=srcv[a:b, :]), list(gate), [("cvt", nm, i)], dsem=cvsem[nm][i])

        def bank():
            b = bankc["i"] % 8
            bankc["i"] += 1
            return b

        with ExitStack() as e1:
            def sb1(name, shape, dt=F32):
                return e1.enter_context(nc.sbuf_tensor(name, list(shape), dt))

            def sem1(name):
                return e1.enter_context(nc.semaphore(name))

            prog = Prog()
            sems = {k: sem1("s1_" + k) for k in ("pe", "act", "dve", "pool")}
            wsem = [sem1(f"s1_w{i}") for i in range(NSLOT)]
            ldsem = [sem1(f"s1_ld{i}") for i in range(3)]
            sB = sb1("sB", [128, 5, 512])
            sC = sb1("sC", [128, 7, 512])
            tmps = {}

            def T(name):
                if name not in tmps:
                    tmps[name] = sb1("t_" + name, [128, 512])
                return tmps[name]

            def ap_of(x):
                return T(x)[:] if isinstance(x, str) else x[0]

            def key_of(x):
                return ("t", x) if isinstance(x, str) else x[1]

            def tt(out, a, b, op, eng="dve"):
                o, x, y = ap_of(out), ap_of(a), ap_of(b)
                prog.add(eng, lambda e: e.tensor_tensor(out=o, in0=x, in1=y, op=op),
                         [key_of(a), key_of(b)], [key_of(out)])

            def ts(out, a, s1, s2, op0, op1=None, eng="dve", extra=()):
                o, x = ap_of(out), ap_of(a)
                if op1 is None:
                    prog.add(eng, lambda e: e.tensor_scalar(out=o, in0=x, scalar1=s1, scalar2=None, op0=op0),
                             [key_of(a)] + list(extra), [key_of(out)])
                else:
                    prog.add(eng, lambda e: e.tensor_scalar(out=o, in0=x, scalar1=s1, scalar2=s2, op0=op0, op1=op1),
                             [key_of(a)] + list(extra), [key_of(out)])

            def act(out, a, func, scale=1.0, bias=0.0, extra=()):
                o, x = ap_of(out), ap_of(a)
                prog.add("act", lambda e: e.activation(out=o, in_=x, func=func, bias=bias, scale=scale),
                         [key_of(a)] + list(extra), [key_of(out)])

            prog.add("sp", lambda e: e.dma_start(out=par[:], in_=par_d), [], ["par"], dsem=ldsem[0])
            prog.add("sp", lambda e: e.dma_start(out=sB[:], in_=ssmB_d), [], ["sB"], dsem=ldsem[1])
            prog.add("sp", lambda e: e.dma_start(out=sC[:], in_=ssmC_d), [], ["sC"], dsem=ldsem[2])
            prog.add("dve", lambda e: e.memset(onesb[:], 1.0), [], ["onesb"])
            prog.add("dve", lambda e: e.memset(onesf[:], 1.0), [], ["onesf"])
            prog.add("dve", lambda e: e.memset(zerob[:], 0.0), [], ["zerob"])

            prog.add("act", lambda e: e.activation(out=cab[:], in_=pcol("cvec"), func=AF.Silu),
                     ["par"], ["cab"])
            w_ada_r = w_ada.rearrange("(kt p) c -> p kt c", p=128)
            mb = bank()
            for ch in range(8):
                s = ring["i"] % NSLOT
                ring["i"] += 1
                wv = wslot[s][:, 0:8192].rearrange("p (k c) -> p k c", k=16)
                src = w_ada_r[:, :, ch * 512:(ch + 1) * 512]
                prog.add("pool", lambda e, wv=wv, src=src: e.dma_start(out=wv, in_=src),
                         [], [("w", s)], dsem=wsem[s])
                for ti in range(4):
                    j = ch * 4 + ti

                    def fn(e, wv=wv, ti=ti, j=j):
                        ins = None
                        for kt in range(16):
                            ins = e.matmul(ps[:, mb, j:j + 1], lhsT=wv[:, kt, ti * 128:(ti + 1) * 128],
                                           rhs=cab[:, kt:kt + 1], start=(kt == 0), stop=(kt == 15))
                        return ins
                    prog.add("pe", fn, [("w", s), "cab"], [("ps", mb)])
            for i_ in range(CVPARTS["w_in"]):
                cvt_piece(prog, "w_in", i_)
            prog.add("dve", lambda e: e.tensor_tensor(out=modv[:, 0:32], in0=ps[:, mb, 0:32], in1=pcol("bada", 0, 32), op=ALU.add),
                     [("ps", mb), "par"], ["modv"])
            prog.add("dve", lambda e: e.scalar_tensor_tensor(out=g1s[:], in0=modv[:, 16:32], scalar=1.0, in1=pcol("n1g"),
                                                             op0=ALU.add, op1=ALU.mult), ["modv", "par"], ["g1s"])

            def cmul(outr, outi, ar, ai, br, bi):
                tt("c1", ar, br, ALU.mult)
                tt("c2", ai, bi, ALU.mult)
                tt(outr, "c1", "c2", ALU.subtract)
                tt("c1", ar, bi, ALU.mult)
                tt("c2", ai, br, ALU.mult)
                tt(outi, "c1", "c2", ALU.add)

            def lam_tables(are, aim, ldt):
                act("s0", ldt, AF.Exp)
                tt("s1", are, "s0", ALU.mult)
                act("s1", "s1", AF.Exp)
                tt("s2", aim, "s0", ALU.mult)
                for outn, shift in (("li", 0.0), ("lr", PI / 2)):
                    ts("s3", "s2", shift, None, ALU.add)
                    ts("s4", "s2", shift, None, ALU.add)
                    for k in range(8):
                        thr = (2 * k + 1) * PI
                        ts("s5", "s4", thr, -2 * PI, ALU.is_gt, ALU.mult)
                        tt("s3", "s3", "s5", ALU.add)
                    act("s3", "s3", AF.Sin)
                    tt(outn, "s1", "s3", ALU.mult)
                tt("s0", are, are, ALU.mult)
                tt("s1", aim, aim, ALU.mult)
                tt("s0", "s0", "s1", ALU.add)
                o, x = T("s1")[:], T("s0")[:]
                prog.add("dve", lambda e: e.reciprocal(out=o, in_=x), [("t", "s0")], [("t", "s1")])
                ts("s0", "lr", -1.0, None, ALU.add)
                tt("s2", "s0", are, ALU.mult)
                tt("s3", "li", aim, ALU.mult)
                tt("s2", "s2", "s3", ALU.add)
                tt("fr", "s2", "s1", ALU.mult)
                tt("s2", "li", are, ALU.mult)
                tt("s3", "s0", aim, ALU.mult)
                tt("s2", "s2", "s3", ALU.subtract)
                tt("fi", "s2", "s1", ALU.mult)

            def sBk(i):
                return (sB[:, i, :], "sB")

            def sCk(i):
                return (sC[:, i, :], "sC")

            lam_tables(sBk(0), sBk(1), sBk(2))
            cmul("br", "bi", "fr", "fi", sBk(3), sBk(4))
            cmul("l2r", "l2i", "lr", "li", "lr", "li")
            cmul("l3r", "l3i", "l2r", "l2i", "lr", "li")
            for b in range(4):
                if b == 3:
                    vr, vi = "br", "bi"
                else:
                    pw = {0: ("l3r", "l3i"), 1: ("l2r", "l2i"), 2: ("lr", "li")}[b]
                    cmul("wr", "wi", pw[0], pw[1], "br", "bi")
                    vr, vi = "wr", "wi"
                for plane, vn in ((0, vr), (1, vi)):
                    for j in range(2):
                        o = W2[:, :, b, plane, j * 64:(j + 1) * 64]
                        x = T(vn)[:].rearrange("p (c q) -> p c q", c=8)
                        m = pcol("mB", j, j + 1)
                        prog.add("dve", lambda e, o=o, x=x, m=m: e.tensor_scalar(out=o, in0=x, scalar1=m, scalar2=None,
                                                                                  op0=ALU.mult),
                                 [("t", vn), "par"], ["W2"])

            lam_tables(sCk(0), sCk(1), sCk(2))
            cmul("l2r", "l2i", "lr", "li", "lr", "li")
            cmul("l3r", "l3i", "l2r", "l2i", "lr", "li")
            cmul("l4r", "l4i", "l2r", "l2i", "l2r", "l2i")
            cmul("l8r", "l8i", "l4r", "l4i", "l4r", "l4i")
            cmul("l12r", "l12i", "l8r", "l8i", "l4r", "l4i")
            cmul("l16r", "l16i", "l8r", "l8i", "l8r", "l8i")
            cmul("l32r", "l32i", "l16r", "l16i", "l16r", "l16i")
            cmul("l64r", "l64i", "l32r", "l32i", "l32r", "l32i")
            cmul("l128r", "l128i", "l64r", "l64i", "l64r", "l64i")
            for i, nm in enumerate(("l4", "l8", "l12", "l16", "l32", "l64", "l128")):
                xr = T(nm + "r")[:].rearrange("p (a h) -> p a h", h=16)[:, :, 0]
                xi = T(nm + "i")[:].rearrange("p (a h) -> p a h", h=16)[:, :, 0]
                for o, x, sg, kn in ((CA[:, i, :, 0], xr, 1.0, nm + "r"), (CA[:, i, :, 1], xr, 1.0, nm + "r"),
                                     (CB[:, i, :, 0], xi, -1.0, nm + "i"), (CB[:, i, :, 1], xi, 1.0, nm + "i")):
                    prog.add("dve", lambda e, o=o, x=x, sg=sg: e.tensor_scalar(out=o, in0=x, scalar1=sg, scalar2=None,
                                                                              op0=ALU.mult), [("t", kn)], ["LC"])
            pws = [("lr", "li"), ("l2r", "l2i"), ("l3r", "l3i"), ("l4r", "l4i")]
            for b in range(4):
                cmul("wr", "wi", pws[b][0], pws[b][1], sCk(3), sCk(4))
                for plane, vn, sgn in ((0, "wr", 1.0), (1, "wi", -1.0)):
                    for j in range(2):
                        o = W4[:, :, b, plane, j * 16:(j + 1) * 16]
                        x = T(vn)[:].rearrange("p (a h) -> p a h", h=16)
                        m = pcol("mC", j, j + 1)
                        prog.add("dve", lambda e, o=o, x=x, m=m, sgn=sgn: e.tensor_scalar(
                            out=o, in0=x, scalar1=m, scalar2=sgn, op0=ALU.mult, op1=ALU.mult),
                            [("t", vn), "par"], ["W4"])
            cmul("br", "bi", "fr", "fi", sCk(5), sCk(6))
            Lm = sb1("Lm", [128, 2, 32, 32])
            Rm = sb1("Rm", [128, 2, 32, 32])
            ITf = sb1("ITf", [128, 8, 128])
            for plane, (src, sgn) in enumerate(((sCk(3), 1.0), (sCk(4), -1.0))):
                for j in range(2):
                    o = Rm[:, plane, :, j * 16:(j + 1) * 16]
                    x = src[0].rearrange("p (a h) -> p a h", h=16)
                    m = pcol("mC", j, j + 1)
                    prog.add("dve", lambda e, o=o, x=x, m=m, sgn=sgn: e.tensor_scalar(
                        out=o, in0=x, scalar1=m, scalar2=sgn, op0=ALU.mult, op1=ALU.mult), ["sC", "par"], ["Rm"])
            prog.add("dve", lambda e: e.memset(ITf[:], 0.0), [], ["ITf"])
            prog.add("dve", lambda e: e.memset(IT[:], 0.0), [], ["IT"])
            ident = pcol("ident")
            for lag in range(4):
                if lag == 0:
                    vr, vi = "br", "bi"
                else:
                    cmul("wr", "wi", pws[lag - 1][0], pws[lag - 1][1], "br", "bi")
                    vr, vi = "wr", "wi"
                for plane, vn in ((0, vr), (1, vi)):
                    for j in range(2):
                        o = Lm[:, plane, :, j * 16:(j + 1) * 16]
                        x = T(vn)[:].rearrange("p (a h) -> p a h", h=16)
                        m = pcol("mC", j, j + 1)
                        prog.add("dve", lambda e, o=o, x=x, m=m: e.tensor_scalar(out=o, in0=x, scalar1=m, scalar2=None,
                                                                                  op0=ALU.mult),
                                 [("t", vn), "par"], ["Lm"])
                bk = bank()
                for pt in range(32):
                    ct, q = pt // 4, pt % 4
                    o = ps[32 * q:32 * q + 32, bk, ct * 32:ct * 32 + 32]

                    def fn(e, o=o, pt=pt, q=q):
                        e.matmul(o, lhsT=Lm[:, 0, pt, :], rhs=Rm[:, 0, pt, :], start=True, stop=False,
                                 tile_position=(0, 32 * q))
                        return e.matmul(o, lhsT=Lm[:, 1, pt, :], rhs=Rm[:, 1, pt, :], start=False, stop=True,
                                        tile_position=(0, 32 * q))
                    prog.add("pe", fn, ["Lm", "Rm"], [("ps", bk)])
                for q in range(4):
                    x = ps[32 * q:32 * q + 32, bk, 0:256].rearrange("p (c k) -> p c k", c=8)
                    if lag == 0:
                        o = ITf[32 * q:32 * q + 32, :, 32 * q:32 * q + 32]
                        prog.add("dve", lambda e, o=o, x=x: e.tensor_copy(out=o, in_=x), [("ps", bk)], ["ITf"])
                    else:
                        o = IT[32 * q:32 * q + 32, :, lag, 32 * q:32 * q + 32]
                        prog.add("dve", lambda e, o=o, x=x: e.tensor_copy(out=o, in_=x), [("ps", bk)], ["IT"])
                if lag == 0:
                    for ct in range(8):
                        o = ITf[:, ct, :]
                        d = pcol("dsk", ct, ct + 1)
                        prog.add("dve", lambda e, o=o, d=d: e.scalar_tensor_tensor(out=o, in0=ident, scalar=d, in1=o,
                                                                                  op0=ALU.mult, op1=ALU.add),
                                 ["par", "ITf"], ["ITf"])
                    prog.add("dve", lambda e: e.tensor_copy(out=IT[:, :, 0, :], in_=ITf[:]), ["ITf"], ["IT"])
            prog.add("sp", lambda e: None, ["IT", "W4", "W2", "LC", "g1s", "g2s", "modv", "onesb", "onesf", "zerob"], [])
            with nc.Block() as block:
                prog.emit(block, sems)
            nc.all_engine_barrier()

        with ExitStack() as e2:
            def sb2(name, shape, dt=F32):
                return e2.enter_context(nc.sbuf_tensor(name, list(shape), dt))

            def sem2(name):
                return e2.enter_context(nc.semaphore(name))

            prog = Prog()
            sems = {k: sem2("s2_" + k) for k in ("pe", "act", "dve", "pool")}
            wsem = [sem2(f"s2_w{i}") for i in range(NSLOT)]
            wsemH = [sem2(f"s2_wh{i}") for i in range(NSLOT)]
            hsem = sem2("s2_h")
            osem = sem2("s2_o")
            h = sb2("h", [128, 16, NT])
            u = sb2("u", [128, 16, NT], BF16)
            v = sb2("v", [128, 8, 32 + NT], BF16)
            vssm = sb2("vssm", [128, 8, NT], BF16)
            R16 = sb2("R16", [128, 16, NT], BF16)
            sAB = sb2("sAB", [128, 8, NT], BF16)
            rstd = sb2("rstd", [128, NT])
            mu = sb2("mu", [128, NT])
            sq = [sb2(f"sq{i}", [128, NT], BF16) for i in range(2)]
            tf = [sb2(f"tf{i}", [128, NT]) for i in range(2)]
            X = sb2("X", [128, 32, MC + 1, 2])
            G = sb2("G", [128, 32, NCH + 1, 2])
            ZR = sb2("ZR", [128, 32, MC], BF16)
            ZI = sb2("ZI", [128, 32, MC], BF16)
            P1 = sb2("P1", [128, 32, NCH, 2])
            P2 = sb2("P2", [128, 32, NCH, 2])
            uh = sb2("uh", [128, 16, 32], BF16)
            sgh = sb2("sgh", [128, 8, 32], BF16)
            ring["i"] = 0
            NDG = 4
            DG = sb2("DG", [128, NDG, 128], BF16)
            cnt = {"sq": 0, "tf": 0, "dg": 0}

            shift1, gate1 = modv[:, 0:16], modv[:, 32:48]
            shift2, gate2 = modv[:, 48:64], modv[:, 80:96]

            mode = {"bf": False}

            def wload(W, kt0, nkt, c0, ncols):
                s = ring["i"] % NSLOT
                ring["i"] += 1
                wv = wslot[s][:, 0:nkt * ncols].rearrange("p (k c) -> p k c", k=nkt)
                nm = WNAME.get(id(W))
                if mode["bf"] and nm is not None:
                    src = WBF[nm].rearrange("(kt p) c -> p kt c", p=128)[:, kt0:kt0 + nkt, c0:c0 + ncols]
                    prog.add("sp", lambda e: e.dma_start(out=wv, in_=src), cvt_keys[nm], [("w", s)], dsem=wsemH[s])
                else:
                    src = W.rearrange("(kt p) c -> p kt c", p=128)[:, kt0:kt0 + nkt, c0:c0 + ncols]
                    prog.add("pool", lambda e: e.dma_start(out=wv, in_=src), [], [("w", s)], dsem=wsem[s])
                return wv, ("w", s)

            cvt_keys = {nm: [("cvt", nm, i) for i in range(n)] for nm, n in CVPARTS.items()}
            for i_ in range(CVPARTS["w_in"]):
                prog.add("pool", lambda e: None, [], [("cvt", "w_in", i_)], dsem=cvsem["w_in"][i_])
            cvt_plan = [(nm, i_) for nm in ("w_co", "w_gb", "w_ga", "w_out", "w_ff1") for i_ in range(CVPARTS[nm])]

            def cvt_next(gate=()):
                if cvt_plan:
                    nm, i_ = cvt_plan.pop(0)
                    cvt_piece(prog, nm, i_, gate)

            def mmg(out, pairs, reads, writes):
                def fn(e):
                    ins = None
                    n = len(pairs)
                    for i, (l, r) in enumerate(pairs):
                        ins = e.matmul(out, lhsT=l, rhs=r, start=(i == 0), stop=(i == n - 1))
                    return ins
                prog.add("pe", fn, reads, writes)

            def mod_load(chs):
                return [(wload(w_ada, 0, 16, ch * 512, 512), ch) for ch in chs]

            def mod_compute(loaded):
                for (wv, wk), ch in loaded:
                    b = bank()
                    for ti in range(4):
                        def fn(e, wv=wv, ti=ti, b=b):
                            ins = None
                            for kt in range(16):
                                ins = e.matmul(ps[:, b, ti:ti + 1], lhsT=wv[:, kt, ti * 128:(ti + 1) * 128],
                                               rhs=cab[:, kt:kt + 1], start=(kt == 0), stop=(kt == 15))
                            return ins
                        prog.add("pe", fn, [wk], [("ps", b)])
                    j0 = ch * 4
                    prog.add("dve", lambda e, b=b, j0=j0: e.tensor_tensor(out=modv[:, j0:j0 + 4], in0=ps[:, b, 0:4],
                                                                          in1=pcol("bada", j0, j0 + 4), op=ALU.add),
                             [("ps", b)], ["modv2"])
                    if ch == 23:
                        prog.add("dve", lambda e: e.scalar_tensor_tensor(out=g2s[:], in0=modv[:, 64:80], scalar=1.0, in1=pcol("n2g"),
                                                                         op0=ALU.add, op1=ALU.mult), ["modv2"], ["modv2"])

            def rmsnorm_stats():
                b = bank()
                for kt in range(16):
                    i = cnt["sq"] % 2
                    cnt["sq"] += 1
                    sqa = sq[i]
                    prog.add("act", lambda e, sqa=sqa, kt=kt: e.activation(out=sqa[:], in_=h[:, kt, :], func=AF.Square),
                             [("h", kt)], [("sq", i)])
                    prog.add("pe", lambda e, sqa=sqa, kt=kt: e.matmul(ps[:, b, :], lhsT=onesb[:], rhs=sqa[:],
                                                                      start=(kt == 0), stop=(kt == 15)),
                             [("sq", i)], [("ps", b)])
                prog.add("act", lambda e: e.activation(out=rstd[:], in_=ps[:, b, :], func=AF.Sqrt, bias=EPS, scale=1.0 / D),
                         [("ps", b)], ["rstd"])
                prog.add("dve", lambda e: e.reciprocal(out=rstd[:], in_=rstd[:]), ["rstd"], ["rstd"])

            def modulate(gs, sh, extra):
                for kt in range(16):
                    i = cnt["tf"] % 2
                    cnt["tf"] += 1
                    t = tf[i]
                    prog.add("dve", lambda e, t=t, kt=kt: e.tensor_tensor(out=t[:], in0=h[:, kt, :], in1=rstd[:], op=ALU.mult),
                             [("h", kt), "rstd"], [("tf", i)])
                    prog.add("act", lambda e, t=t, kt=kt: e.activation(out=u[:, kt, :], in_=t[:], func=AF.Identity,
                                                                       bias=sh[:, kt:kt + 1], scale=gs[:, kt:kt + 1]),
                             [("tf", i)] + extra, [("u", kt)])

            ukeys = [("u", kt) for kt in range(16)]

            SCAN_ENG = "dve"

            def cmuladd(dst, ci, src, n, kd, ks):
                shp = [128, 32, n, 2]
                ca = CA[:, ci, :, :].unsqueeze(2).to_broadcast(shp)
                cb = CB[:, ci, :, :].unsqueeze(2).to_broadcast(shp)
                p1, p2 = P1[:, :, 0:n, :], P2[:, :, 0:n, :]
                srcsw = src[:, :, :, ::-1]
                prog.add(SCAN_ENG, lambda e: e.tensor_tensor(out=p1, in0=src, in1=ca, op=ALU.mult), [ks], ["P1"])
                prog.add(SCAN_ENG, lambda e: e.tensor_tensor(out=p2, in0=srcsw, in1=cb, op=ALU.mult), [ks], ["P2"])
                prog.add(SCAN_ENG, lambda e: e.tensor_tensor(out=p1, in0=p1, in1=p2, op=ALU.add), ["P1", "P2"], ["P1"])
                prog.add(SCAN_ENG, lambda e: e.tensor_tensor(out=dst, in0=dst, in1=p1, op=ALU.add), ["P1", kd], [kd])

            def Xc(a, n=NCH, step=4):
                return X[:, :, 1 + a:2 + a + step * (n - 1):step, :]

            def ssm_q(sb_i):
                t0 = sb_i * SUB
                for half in range(2):
                    banks = [bank() for _ in range(4)]
                    for cl in range(4):
                        ct = half * 4 + cl
                        for q in range(4):
                            for plane in range(2):
                                c0 = (cl * 2 + plane) * MC
                                o = ps[:, banks[q], c0:c0 + MC]

                                def fn(e, o=o, ct=ct, q=q, plane=plane):
                                    ins = None
                                    for b in range(4):
                                        ins = e.matmul(o, lhsT=W2[32 * q:32 * q + 32, ct, b, plane, :],
                                                       rhs=vssm[32 * q:32 * q + 32, ct, t0 + b * MC:t0 + (b + 1) * MC],
                                                       start=(b == 0), stop=(b == 3), tile_position=(32 * q, 0))
                                    return ins
                                prog.add("pe", fn, [("vssm", ct, sb_i)], [("ps", banks[q])])
                    for q in range(4):
                        src = ps[:, banks[q], 0:8 * MC].rearrange("p (c l m) -> p c l m", c=4, l=2)
                        for plane in range(2):
                            o = X[:, half * 16 + q:half * 16 + 16:4, 1:MC + 1, plane]
                            x = src[:, :, plane, :]
                            prog.add("act", lambda e, o=o, x=x: e.activation(out=o, in_=x, func=AF.Identity),
                                     [("ps", banks[q])], ["X"])

            def ssm_scan(prefix, last_prefix_sub):
                for a in range(1, 4):
                    cmuladd(Xc(a), 0, Xc(a - 1), NCH, "X", "X")
                if prefix:
                    cmuladd(X[:, :, 8:33:8, :], 3, X[:, :, 4:29:8, :], 4, "X", "X")
                    cmuladd(X[:, :, 16:33:16, :], 4, X[:, :, 8:25:16, :], 2, "X", "X")
                    cmuladd(X[:, :, 32:33, :], 5, X[:, :, 16:17, :], 1, "X", "X")
                    cmuladd(X[:, :, 32:33, :], 6, G[:, :, 0:1, :], 1, "X", "G")
                    if last_prefix_sub:
                        fl = pcol("flag")
                        prog.add(SCAN_ENG, lambda e: e.tensor_scalar(out=G[:, :, 0:1, :], in0=X[:, :, 32:33, :], scalar1=fl,
                                                                  scalar2=None, op0=ALU.mult), ["X"], ["G"])
                        prog.add(SCAN_ENG, lambda e: e.tensor_copy(out=X[:, :, 0:1, :], in_=G[:, :, 0:1, :]), ["G"], ["X"])
                    else:
                        prog.add(SCAN_ENG, lambda e: e.tensor_copy(out=G[:, :, 0:1, :], in_=X[:, :, 32:33, :]), ["X"], ["G"])
                    return
                prog.add(SCAN_ENG, lambda e: e.tensor_copy(out=G[:, :, 1:NCH + 1, :], in_=X[:, :, 4:4 * NCH + 1:4, :]),
                         ["X"], ["G"])
                for k in range(4):
                    d = 1 << k
                    cmuladd(G[:, :, d:NCH + 1, :], 3 + k, G[:, :, 0:NCH + 1 - d, :], NCH + 1 - d, "G", "G")
                for a in range(3):
                    cmuladd(Xc(a), a, G[:, :, 0:NCH, :], NCH, "X", "G")
                prog.add(SCAN_ENG, lambda e: e.tensor_copy(out=Xc(3), in_=G[:, :, 1:NCH + 1, :]), ["G"], ["X"])
                for Z, pl in ((ZR, 0), (ZI, 1)):
                    prog.add(SCAN_ENG, lambda e, Z=Z, pl=pl: e.tensor_copy(out=Z[:], in_=X[:, :, 0:MC, pl]), ["X"], ["Z"])
                prog.add(SCAN_ENG, lambda e: e.tensor_copy(out=G[:, :, 0:1, :], in_=G[:, :, NCH:NCH + 1, :]), ["G"], ["G"])
                prog.add(SCAN_ENG, lambda e: e.tensor_copy(out=X[:, :, 0:1, :], in_=G[:, :, 0:1, :]), ["G", "Z"], ["X"])

            def ssm_y(sb_i):
                t0 = sb_i * SUB
                for ct in range(8):
                    b_ = bank()
                    Y = ps[:, b_, 0:SUB]

                    def fn(e, ct=ct, b_=b_):
                        e.matmul(ps[:, b_, 0:SUB], lhsT=zerob[:], rhs=vssm[:, ct, t0:t0 + SUB], start=True, stop=False)
                        for b in range(4):
                            for b2 in range(b + 1):
                                e.matmul(ps[:, b_, b * MC:(b + 1) * MC], lhsT=IT[:, ct, b - b2, :],
                                         rhs=vssm[:, ct, t0 + b2 * MC:t0 + (b2 + 1) * MC], start=False, stop=False)
                        ins = None
                        for q in range(4):
                            pt = ct * 4 + q
                            for b in range(4):
                                for plane, Z in ((0, ZR), (1, ZI)):
                                    last = (b == 3 and plane == 1)
                                    ins = e.matmul(ps[32 * q:32 * q + 32, b_, b * MC:(b + 1) * MC],
                                                   lhsT=W4[:, pt, b, plane, :], rhs=Z[:, pt, :], start=False, stop=last,
                                                   tile_position=(0, 32 * q))
                        return ins
                    prog.add("pe", fn, [("vssm", ct, sb_i), "Z"], [("ps", b_)])
                    o = vssm[:, ct, t0:t0 + SUB].rearrange("p (m b) -> p b m", b=4)
                    Yv = Y.rearrange("p (b m) -> p b m", b=4)
                    prog.add("act", lambda e, o=o, Yv=Yv: e.activation(out=o, in_=Yv, func=AF.Gelu_apprx_tanh),
                             [("ps", b_)], [("vssm", ct, sb_i)])

            def win_tile(wv, wk, ti, evac):
                b = bank()
                wkl = list(wk) if isinstance(wk, list) else [wk]
                mmg(ps[:, b, :], [(wv[:, kt, ti * 128:(ti + 1) * 128], u[:, kt, :]) for kt in range(16)],
                    wkl + ukeys, [("ps", b)])
                evac(b)

            def pre_vssm():
                return [wload(w_in, 0, 16, ch * 512, 512) for ch in (4, 5)]

            def do_vssm(pre):
                ct = 0
                for wv, wk, ntl in pre:
                    for ti in range(ntl):
                        def evac(b, ct=ct):
                            o = vssm[:, ct, :].rearrange("p (s b m) -> p s b m", s=NSUB, b=4)
                            x = ps[:, b, :].rearrange("p (s m b) -> p s b m", s=NSUB, b=4)
                            prog.add("act", lambda e, o=o, x=x: e.activation(out=o, in_=x, func=AF.Identity),
                                     [("ps", b)], [("vssm", ct, s_) for s_ in range(NSUB)])
                        win_tile(wv, wk, ti, evac)
                        ct += 1

            RK = [("R", j) for j in range(16)]
            SK = [("sAB", j) for j in range(8)]
            VK = [("v", c) for c in range(8)] + [("vh", c) for c in range(8)]

            def load_resident():
                w_in_r = w_in.rearrange("(kt p) c -> p kt c", p=128)
                sabw = sAB[:].rearrange("p c t -> p (c t)").rearrange("p (k c) -> p k c", k=16)
                vw = v[:].rearrange("p c t -> p (c t)")[:, 0:4096].rearrange("p (k c) -> p k c", k=16)
                res = []
                for i, (dst, c0, nc_, keys, ntl) in enumerate(((R16[:], 2048, 512, RK, 4), (sabw, 2560, 256, SK, 2),
                                                              (vw, 2816, 256, VK, 2))):
                    sem = sem2(f"s2_res{i}")
                    src = w_in_r[:, :, c0:c0 + nc_]
                    prog.add("pool", lambda e, dst=dst, src=src: e.dma_start(out=dst, in_=src), [], keys, dsem=sem)
                    res.append((dst, keys, ntl))
                return res

            def do_glu(first):
                fl = pcol("flag")
                for ch in (2, 3, 0, 1):
                    wv, wk = wload(w_in, 0, 16, ch * 512, 512)
                    for ti in range(4):
                        ct = (ch % 2) * 4 + ti
                        if ch >= 2:
                            def evac(b, ct=ct):
                                prog.add("act", lambda e: e.activation(out=sAB[:, ct, :], in_=ps[:, b, :], func=AF.Sigmoid),
                                         [("ps", b)], [("sAB", ct)])
                        else:
                            def evac(b, ct=ct):
                                prog.add("dve", lambda e: e.tensor_tensor(out=v[:, ct, 32:32 + NT], in0=ps[:, b, :],
                                                                          in1=sAB[:, ct, :], op=ALU.mult),
                                         [("ps", b), ("sAB", ct)], [("v", ct)])
                        win_tile(wv, wk, ti, evac)
                        if first:
                            b2 = bank()
                            mmg(ps[:, b2, 0:32], [(wv[:, kt, ti * 128:(ti + 1) * 128], uh[:, kt, :]) for kt in range(16)],
                                [wk, "uh"], [("ps", b2)])
                            if ch >= 2:
                                prog.add("act", lambda e, ct=ct, b2=b2: e.activation(out=sgh[:, ct, :], in_=ps[:, b2, 0:32], func=AF.Sigmoid),
                                         [("ps", b2)], [("sgh", ct)])
                            else:
                                prog.add("dve", lambda e, ct=ct, b2=b2: e.scalar_tensor_tensor(out=v[:, ct, 0:32], in0=ps[:, b2, 0:32], scalar=fl,
                                                                                             in1=sgh[:, ct, :], op0=ALU.mult, op1=ALU.mult),
                                         [("ps", b2), ("sgh", ct)], [("vh", ct)])

            def halo_copy(ct, use_flag):
                if use_flag:
                    fl = pcol("flag")
                    prog.add("dve", lambda e: e.tensor_scalar(out=v[:, ct, 0:32], in0=v[:, ct, NT:NT + 32], scalar1=fl,
                                                              scalar2=None, op0=ALU.mult), [("v", ct)], [("vh", ct)])
                else:
                    prog.add("pool", lambda e: e.tensor_copy(out=v[:, ct, 0:32], in_=v[:, ct, NT:NT + 32]),
                             [("v", ct)], [("vh", ct)])

            def conv_ct(ct):
                wdw = pcol("wdw")
                ident = pcol("ident")
                b = bank()
                for k in range(31):
                    i = cnt["dg"] % NDG
                    cnt["dg"] += 1
                    wk_ = wdw[:, ct * 31 + k:ct * 31 + k + 1]
                    prog.add("act", lambda e, i=i, wk_=wk_: e.activation(out=DG[:, i, :], in_=ident, func=AF.Identity, scale=wk_),
                             [], [("dg", i)])
                    src = v[:, ct, 2 + k:2 + k + NT]
                    prog.add("pe", lambda e, i=i, src=src, k=k: e.matmul(ps[:, b, :], lhsT=DG[:, i, :], rhs=src,
                                                                        start=(k == 0), stop=(k == 30)),
                             [("dg", i), ("v", ct), ("vh", ct)], [("ps", b)])
                bd = pcol("bdw", ct, ct + 1)
                prog.add("act", lambda e: e.activation(out=sAB[:, ct, :], in_=ps[:, b, :], func=AF.Identity, bias=bd),
                         [("ps", b)], [("sAB", ct)])
                halo_copy(ct, False)

            def do_ln():
                b1, b2 = bank(), bank()
                for ct in range(8):
                    rk = [("sAB", ct)]
                    x = sAB[:, ct, :]
                    prog.add("pe", lambda e, x=x, ct=ct: e.matmul(ps[:, b1, :], lhsT=onesb[:], rhs=x, start=(ct == 0), stop=(ct == 7)),
                             rk, [("ps", b1)])
                    i = cnt["sq"] % 2
                    cnt["sq"] += 1
                    t = sq[i]
                    prog.add("act", lambda e, x=x, t=t: e.activation(out=t[:], in_=x, func=AF.Square), rk, [("sq", i)])
                    prog.add("pe", lambda e, t=t, ct=ct: e.matmul(ps[:, b2, :], lhsT=onesb[:], rhs=t[:], start=(ct == 0), stop=(ct == 7)),
                             [("sq", i)], [("ps", b2)])
                prog.add("dve", lambda e: e.tensor_scalar(out=mu[:], in0=ps[:, b1, :], scalar1=1.0 / 1024, scalar2=None, op0=ALU.mult),
                         [("ps", b1)], ["mu"])
                prog.add("dve", lambda e: e.tensor_tensor(out=rstd[:], in0=mu[:], in1=mu[:], op=ALU.mult), ["mu"], ["rstd"])
                prog.add("dve", lambda e: e.scalar_tensor_tensor(out=rstd[:], in0=ps[:, b2, :], scalar=1.0 / 1024, in1=rstd[:],
                                                                 op0=ALU.mult, op1=ALU.subtract), [("ps", b2), "rstd"], ["rstd"])
                prog.add("act", lambda e: e.activation(out=rstd[:], in_=rstd[:], func=AF.Sqrt, bias=EPS, scale=1.0), ["rstd"], ["rstd"])
                prog.add("dve", lambda e: e.reciprocal(out=rstd[:], in_=rstd[:]), ["rstd"], ["rstd"])
                for ct in range(8):
                    rk = [("sAB", ct)]
                    i = cnt["tf"] % 2
                    cnt["tf"] += 1
                    t = tf[i]
                    x = sAB[:, ct, :]
                    prog.add("dve", lambda e, x=x, t=t: e.tensor_tensor(out=t[:], in0=x, in1=mu[:], op=ALU.subtract),
                             rk + ["mu"], [("tf", i)])
                    prog.add("dve", lambda e, t=t: e.tensor_tensor(out=t[:], in0=t[:], in1=rstd[:], op=ALU.mult),
                             [("tf", i), "rstd"], [("tf", i)])
                    lg, lb = pcol("lng", ct, ct + 1), pcol("lnb", ct, ct + 1)
                    prog.add("act", lambda e, t=t, ct=ct, lg=lg, lb=lb: e.activation(out=v[:, ct, 32:32 + NT], in_=t[:], func=AF.Silu,
                                                                                    bias=lb, scale=lg),
                             [("tf", i), ("vh", ct)], [("v", ct)])

            def merge_p1(chs, pre):
                for ch in chs:
                    wv, wk = pre
                    for ti in range(4):
                        j = (ch - 6) * 4 + ti

                        def evac(b, j=j):
                            prog.add("act", lambda e: e.activation(out=R16[:, j, :], in_=ps[:, b, :], func=AF.Sigmoid),
                                     [("ps", b)], [("R", j)])
                        win_tile(wv, wk, ti, evac)

            def merge_p23():
                for ch in range(2):
                    wv, wk = wload(w_co, 0, 8, ch * 1024, 1024)
                    for ti in range(8):
                        j = ch * 8 + ti
                        b = bank()
                        mmg(ps[:, b, :], [(wv[:, kt, ti * 128:(ti + 1) * 128], v[:, kt, 32:32 + NT]) for kt in range(8)],
                            [wk] + [("v", kt) for kt in range(8)], [("ps", b)])
                        prog.add("dve", lambda e, j=j, b=b: e.tensor_tensor(out=R16[:, j, :], in0=ps[:, b, :], in1=R16[:, j, :],
                                                                            op=ALU.mult), [("ps", b), ("R", j)], [("R", j)])
                skeys = [("vssm", kt, s_) for kt in range(8) for s_ in range(NSUB)]
                for jb in range(4):
                    wv, wk = wload(w_in, 0, 16, (10 + jb) * 512, 512)
                    for ti in range(4):
                        def evac(b, ti=ti):
                            prog.add("act", lambda e: e.activation(out=sAB[:, ti, :], in_=ps[:, b, :], func=AF.Sigmoid),
                                     [("ps", b)], [("sAB", ti)])
                        win_tile(wv, wk, ti, evac)
                    for which, W in ((0, w_gb), (1, w_ga)):
                        wv, wk = wload(W, 0, 8, jb * 512, 512)
                        for ti in range(4):
                            j = jb * 4 + ti
                            b = bank()
                            mmg(ps[:, b, :], [(wv[:, kt, ti * 128:(ti + 1) * 128], vssm[:, kt, :]) for kt in range(8)],
                                [wk] + skeys, [("ps", b)])
                            kB = ("sAB", 4 + ti)
                            if which == 0:
                                prog.add("act", lambda e, ti=ti, b=b: e.activation(out=sAB[:, 4 + ti, :], in_=ps[:, b, :], func=AF.Sigmoid),
                                         [("ps", b)], [kB])
                            else:
                                prog.add("dve", lambda e, ti=ti, b=b: e.tensor_tensor(out=sAB[:, 4 + ti, :], in0=ps[:, b, :], in1=sAB[:, 4 + ti, :],
                                                                                      op=ALU.mult), [("ps", b), kB], [kB])
                                prog.add("dve", lambda e, ti=ti: e.tensor_tensor(out=sAB[:, 4 + ti, :], in0=sAB[:, 4 + ti, :], in1=sAB[:, ti, :],
                                                                                 op=ALU.mult), [kB, ("sAB", ti)], [kB])
                                prog.add("dve", lambda e, ti=ti, j=j: e.tensor_tensor(out=R16[:, j, :], in0=R16[:, j, :], in1=sAB[:, 4 + ti, :],
                                                                                      op=ALU.add), [kB, ("R", j)], [("R", j)])

            def do_wout():
                for ch in range(4):
                    wv, wk = wload(w_out, 0, 16, ch * 512, 512)
                    for ti in range(4):
                        j = ch * 4 + ti
                        b = bank()
                        mmg(ps[:, b, :], [(wv[:, kt, ti * 128:(ti + 1) * 128], R16[:, kt, :]) for kt in range(16)],
                            [wk] + [("R", kt) for kt in range(16)], [("ps", b)])
                        prog.add("dve", lambda e, j=j, b=b: e.scalar_tensor_tensor(out=h[:, j, :], in0=ps[:, b, :], scalar=gate1[:, j:j + 1],
                                                                                   in1=h[:, j, :], op0=ALU.mult, op1=ALU.add),
                                 [("ps", b), ("h", j), "modv2"], [("h", j)])

            def do_ffn():
                for hb in range(4):
                    for c4 in range(4):
                        wv, wk = wload(w_ff1, 0, 16, (hb * 4 + c4) * 512, 512)
                        for ti in range(4):
                            i_ = c4 * 4 + ti

                            def evac(b, i_=i_):
                                k = cnt["sq"] % 2
                                cnt["sq"] += 1
                                s_ = sq[k]
                                prog.add("act", lambda e: e.activation(out=s_[:], in_=ps[:, b, :], func=AF.Relu), [("ps", b)], [("sq", k)])
                                prog.add("dve", lambda e: e.tensor_tensor(out=R16[:, i_, :], in0=ps[:, b, :], in1=s_[:], op=ALU.mult),
                                         [("ps", b), ("sq", k)], [("R", i_)])
                            win_tile(wv, wk, ti, evac)
                    for cb in range(4):
                        wv, wk = wload(w_ff2, hb * 16, 16, cb * 512, 512)
                        for ti in range(4):
                            j = cb * 4 + ti
                            b = bank()
                            mmg(ps[:, b, :], [(wv[:, kt, ti * 128:(ti + 1) * 128], R16[:, kt, :]) for kt in range(16)],
                                [wk] + [("R", kt) for kt in range(16)], [("ps", b)])
                            prog.add("dve", lambda e, j=j, b=b: e.scalar_tensor_tensor(out=h[:, j, :], in0=ps[:, b, :], scalar=gate2[:, j:j + 1],
                                                                                       in1=h[:, j, :], op0=ALU.mult, op1=ALU.add),
                                     [("ps", b), ("h", j)], [("h", j)])

            hkeys = [("h", kt) for kt in range(16)]
            prog.add("dve", lambda e: e.memset(G[:], 0.0), [], ["G"])
            prog.add("dve", lambda e: e.memset(X[:], 0.0), [], ["X"])
            prog.add("dve", lambda e: e.memset(v[:], 0.0), [], [("v", c) for c in range(8)] + [("vh", c) for c in range(8)])

            steps = [("p", i) for i in range(NTILE)] + [("m", i) for i in range(NTILE)]
            resw = load_resident()
            for kind, ti_ in steps:
                srcT = xpT if kind == "p" else xT
                src = srcT.rearrange("(kt p) t -> p kt t", p=128)[:, :, ti_ * NT:(ti_ + 1) * NT]
                prog.add("sp", lambda e, src=src: e.dma_start(out=h[:], in_=src), [], hkeys, dsem=hsem)
                rmsnorm_stats()
                modulate(g1s, shift1, [])
                if kind == "p":
                    ml = mod_load([8 + 2 * ti_, 9 + 2 * ti_])
                    do_vssm(resw)
                    for s_i in range(NSUB):
                        ssm_q(s_i)
                        cvt_next(["X"])
                        ssm_scan(True, ti_ == NTILE - 1 and s_i == NSUB - 1)
                    mod_compute(ml)
                    if ti_ == NTILE - 1:
                        prog.add("act", lambda e: e.activation(out=uh[:], in_=u[:, :, NT - 32:NT], func=AF.Identity), ukeys, ["uh"])
                    continue
                while cvt_plan:
                    cvt_next()
                mode["bf"] = True
                do_glu(ti_ == 0)
                do_vssm([(wv_, wk_, 4) for wv_, wk_ in pre_vssm()])
                for s_i in range(NSUB):
                    p1w = wload(w_in, 0, 16, (6 + s_i) * 512, 512)
                    ssm_q(s_i)
                    ssm_scan(False, False)
                    conv_ct(2 * s_i)
                    conv_ct(2 * s_i + 1)
                    merge_p1([6 + s_i], p1w)
                    if ti_ == 0:
                        mod_compute(mod_load([16 + 2 * s_i, 17 + 2 * s_i]))
                    ssm_y(s_i)
                do_ln()
                merge_p23()
                do_wout()
                rmsnorm_stats()
                modulate(g2s, shift2, ["modv2"])
                do_ffn()
                rmsnorm_stats()
                fgc = pcol("fg")
                for kt in range(16):
                    prog.add("dve", lambda e, kt=kt: e.scalar_tensor_tensor(out=h[:, kt, :], in0=h[:, kt, :], scalar=fgc[:, kt:kt + 1],
                                                                            in1=rstd[:], op0=ALU.mult, op1=ALU.mult),
                             [("h", kt), "rstd"], [("h", kt)])
                dst = outT.rearrange("(kt p) t -> p kt t", p=128)[:, :, ti_ * NT:(ti_ + 1) * NT]
                prog.add("sp", lambda e, dst=dst: e.dma_start(out=dst, in_=h[:]), hkeys, [("out", ti_)], dsem=osem)
            prog.add("sp", lambda e: None, [("out", i) for i in range(NTILE)], [])
            with nc.Block() as block:
                prog.emit(block, sems)
    return nc


_NC = None


def _prep(x, c, w_ada, b_ada, norm1_g, w_in, w_dw, b_dw, ln_g, ln_b, w_conv_out, a_re, a_im, log_dt, b_re, b_im,
          c_re, c_im, d_skip, w_glu_a, w_glu_b, w_out, norm2_g, w_ff1, w_ff2, final_g):
    f = np.float32
    shared = {
        "w_ada": np.ascontiguousarray(w_ada[0], f), "w_in": np.ascontiguousarray(w_in[0], f),
        "w_conv_out": np.ascontiguousarray(w_conv_out[0], f), "w_glu_a": np.ascontiguousarray(w_glu_a[0], f),
        "w_glu_b": np.ascontiguousarray(w_glu_b[0], f), "w_out": np.ascontiguousarray(w_out[0], f),
        "w_ff1": np.ascontiguousarray(w_ff1[0], f), "w_ff2": np.ascontiguousarray(w_ff2[0], f),
    }
    are, aim, ldt = a_re[0], a_im[0], log_dt[0]
    bre, bim, cre, cim = b_re[0], b_im[0], c_re[0], c_im[0]
    def Bl_a(a):
        t = a.reshape(8, 8, 64).transpose(1, 0, 2)
        return np.broadcast_to(t[:, None], (8, 16, 8, 64)).reshape(128, 512)
    ldB = np.broadcast_to(ldt.reshape(8, 8).T[:, None, :, None], (8, 16, 8, 64)).reshape(128, 512)
    def Bl_b(b):
        return b.reshape(8, 8, 64, 16).transpose(1, 3, 0, 2).reshape(128, 512)
    ssmB = np.stack([Bl_a(are), Bl_a(aim), ldB, Bl_b(bre), Bl_b(bim)], axis=1).astype(f)
    def Cl_a(a):
        t = a.reshape(32, 2, 64).transpose(1, 2, 0)
        return np.broadcast_to(t[..., None], (2, 64, 32, 16)).reshape(128, 512)
    ldC = np.broadcast_to(ldt.reshape(32, 2).T[:, None, :, None], (2, 64, 32, 16)).reshape(128, 512)
    def Cl_c(cc):
        return cc.reshape(32, 2, 16, 64).transpose(1, 3, 0, 2).reshape(128, 512)
    def Cl_b(b):
        return b.reshape(32, 2, 64, 16).transpose(1, 2, 0, 3).reshape(128, 512)
    ssmC = np.stack([Cl_a(are), Cl_a(aim), ldC, Cl_c(cre), Cl_c(cim), Cl_b(bre), Cl_b(bim)], axis=1).astype(f)
    shared["ssmB"] = np.ascontiguousarray(ssmB)
    shared["ssmC"] = np.ascontiguousarray(ssmC)

    def fm(vec, n):
        return np.asarray(vec, f).reshape(n, 128).T
    par_base = np.zeros((128, NPAR), f)
    def put(name, arr):
        o, w = PC[name]
        par_base[:, o:o + w] = arr
    put("bada", fm(b_ada[0], 96))
    put("n1g", fm(norm1_g[0], 16)); put("n2g", fm(norm2_g[0], 16)); put("fg", fm(final_g, 16))
    put("wdw", np.asarray(w_dw[0], f).reshape(31, 8, 128).transpose(2, 1, 0).reshape(128, 248))
    put("bdw", fm(b_dw[0], 8)); put("lng", fm(ln_g[0], 8)); put("lnb", fm(ln_b[0], 8)); put("dsk", fm(d_skip[0], 8))
    pidx = np.arange(128)
    mB = np.stack([((pidx // 16) % 2 == j) for j in range(2)], axis=1).astype(f)
    mC = np.stack([((pidx // 64) == j) for j in range(2)], axis=1).astype(f)
    put("mB", mB); put("mC", mC); put("ident", np.eye(128, dtype=f))
    in_maps = []
    zeros = np.zeros((D, NTOK), f)
    for core in range(NCORE):
        b, s = core // 2, core % 2
        m = dict(shared)
        m["xT"] = np.ascontiguousarray(x[b, s * NTOK:(s + 1) * NTOK, :].T)
        m["xpT"] = np.ascontiguousarray(x[b, 0:NTOK, :].T) if s == 1 else zeros
        p = par_base.copy()
        o, w = PC["cvec"]; p[:, o:o + w] = fm(c[b], 16)
        o, w = PC["flag"]; p[:, o:o + w] = float(s)
        m["par"] = p
        in_maps.append(m)
    return in_maps


def kernel(**inputs):
    global _NC
    inputs = {k: np.asarray(v) for k, v in inputs.items()}
    in_maps = _prep(**inputs)
    if _NC is None:
        _NC = build()
    res = run_bass_kernel_spmd(_NC, in_maps, core_ids=list(range(NCORE)))
    out = np.empty((4, 4096, D), np.float32)
    for core in range(NCORE):
        b, s = core // 2, core % 2
        out[b, s * NTOK:(s + 1) * NTOK, :] = res.results[core]["outT"].T
    return out
```

```python
import math
import numpy as np
from contextlib import ExitStack
import concourse.bass as bass
import concourse.mybir as mybir
from concourse.bass_utils import run_bass_kernel_spmd

F32, BF16 = mybir.dt.float32, mybir.dt.bfloat16
AF = mybir.ActivationFunctionType
ALU = mybir.AluOpType

NCORE = 8
D = 2048
NTOK = 2048
NT = 512
NTILE = NTOK // NT
SUB = 128
NSUB = NT // SUB
MC = SUB // 4
NCH = MC // 4
NSLOT = 2
EPS = 1e-6
PI = math.pi

PC = {}
_off = 0
for _n, _w in [("cvec", 16), ("bada", 96), ("n1g", 16), ("n2g", 16), ("fg", 16), ("wdw", 248),
               ("bdw", 8), ("lng", 8), ("lnb", 8), ("dsk", 8), ("flag", 1), ("mB", 2), ("mC", 2),
               ("ident", 128)]:
    PC[_n] = (_off, _w)
    _off += _w
NPAR = _off


class Op:
    __slots__ = ("eng", "fn", "deps", "needs", "val", "dsem", "dval")


class Prog:
    ENG = ("pe", "act", "dve", "pool", "sp")

    def __init__(self):
        self.ops = {e: [] for e in self.ENG}
        self.lastw = {}
        self.readers = {}
        self.dmacount = {}

    def add(self, eng, fn, reads=(), writes=(), dsem=None):
        op = Op()
        op.eng, op.fn, op.needs, op.val, op.dsem, op.dval = eng, fn, False, None, dsem, None
        if dsem is not None:
            self.dmacount[id(dsem)] = self.dmacount.get(id(dsem), 0) + 16
            op.dval = self.dmacount[id(dsem)]
        deps, seen = [], set()
        cand = []
        for k in reads:
            w = self.lastw.get(k)
            if w is not None:
                cand.append(w)
        for k in writes:
            w = self.lastw.get(k)
            if w is not None:
                cand.append(w)
            cand.extend(self.readers.get(k, ()))
        for d in cand:
            if id(d) in seen:
                continue
            seen.add(id(d))
            if d.dsem is None and d.eng == eng and eng == "pe":
                continue
            deps.append(d)
        op.deps = deps
        for k in reads:
            self.readers.setdefault(k, []).append(op)
        for k in writes:
            self.lastw[k] = op
            self.readers[k] = []
        self.ops[eng].append(op)
        return op

    def emit(self, block, sems):
        for e in self.ENG:
            for op in self.ops[e]:
                for d in op.deps:
                    if d.dsem is None:
                        d.needs = True
        for e in self.ENG:
            n = 0
            for op in self.ops[e]:
                if op.dsem is None and op.needs:
                    n += 1
                    op.val = n

        def run(engname, eobj):
            known = {}
            for op in self.ops[engname]:
                need = {}
                for d in op.deps:
                    if d.dsem is not None:
                        sem, val = d.dsem, d.dval
                    else:
                        sem, val = sems[d.eng], d.val
                    if need.get(id(sem), (None, 0))[1] < val:
                        need[id(sem)] = (sem, val)
                for sid, (sem, val) in need.items():
                    if known.get(sid, 0) >= val:
                        continue
                    eobj.wait_ge(sem, val)
                    known[sid] = val
                ins = op.fn(eobj)
                if ins is None:
                    continue
                if op.dsem is not None:
                    ins.then_inc(op.dsem, 16)
                elif op.needs:
                    ins.then_inc(sems[engname], 1)

        block.tensor(lambda e: run("pe", e))
        block.scalar(lambda e: run("act", e))
        block.vector(lambda e: run("dve", e))
        block.gpsimd(lambda e: run("pool", e))
        block.sync(lambda e: run("sp", e))


def build():
    nc = bass.Bass("TRN2", target_bir_lowering=False)

    def din(name, shape):
        return nc.dram_tensor(name, list(shape), F32, kind="ExternalInput").ap()

    xT = din("xT", [D, NTOK])
    xpT = din("xpT", [D, NTOK])
    par_d = din("par", [128, NPAR])
    ssmB_d = din("ssmB", [128, 5, 512])
    ssmC_d = din("ssmC", [128, 7, 512])
    w_ada = din("w_ada", [D, 6 * D])
    w_in = din("w_in", [D, 7168])
    w_co = din("w_conv_out", [1024, D])
    w_ga = din("w_glu_a", [1024, D])
    w_gb = din("w_glu_b", [1024, D])
    w_out = din("w_out", [D, D])
    w_ff1 = din("w_ff1", [D, 4 * D])
    w_ff2 = din("w_ff2", [4 * D, D])
    outT = nc.dram_tensor("outT", [D, NTOK], F32, kind="ExternalOutput").ap()
    WSRC = {"w_in": w_in, "w_co": w_co, "w_ga": w_ga, "w_gb": w_gb, "w_out": w_out, "w_ff1": w_ff1}
    WBF = {k: nc.dram_tensor("bf_" + k, list(a.shape), BF16, kind="Internal").ap() for k, a in WSRC.items()}
    WNAME = {id(a): k for k, a in WSRC.items()}

    es = ExitStack()
    with es:
        def sb(name, shape, dt=F32):
            return es.enter_context(nc.sbuf_tensor(name, list(shape), dt))

        par = sb("par_sb", [128, NPAR])
        modv = sb("modv", [128, 96])
        g1s = sb("g1s", [128, 16])
        g2s = sb("g2s", [128, 16])
        W2 = sb("W2", [128, 8, 4, 2, 128], BF16)
        W4 = sb("W4", [128, 32, 4, 2, 32], BF16)
        IT = sb("IT", [128, 8, 4, 128], BF16)
        CA = sb("CA", [128, 7, 32, 2])
        CB = sb("CB", [128, 7, 32, 2])
        cab = sb("cab", [128, 16], BF16)
        onesb = sb("onesb", [128, 128], BF16)
        onesf = sb("onesf", [128, 128])
        zerob = sb("zerob", [128, 128], BF16)
        wslot = [sb(f"wslot{i}", [128, 8192], BF16) for i in range(NSLOT)]
        ps = es.enter_context(nc.psum_tensor("ps", [128, 8, 512], F32))

        def pcol(name, a=0, b=None):
            o, w = PC[name]
            b = w if b is None else b
            return par[:, o + a:o + b]

        ring = {"i": 0}
        bankc = {"i": 0}
        CVPARTS = {"w_in": 7, "w_co": 1, "w_ga": 1, "w_gb": 1, "w_out": 2, "w_ff1": 8}
        cvsem = {nm: [es.enter_context(nc.semaphore(f"cv_{nm}_{i}")) for i in range(n)] for nm, n in CVPARTS.items()}

        def cvt_piece(prog_, nm, i, gate=()):
            srcv = WSRC[nm].rearrange("k (a c) -> (k a) c", c=1024)
            dstv = WBF[nm].rearrange("k (a c) -> (k a) c", c=1024)
            step = srcv.shape[0] // CVPARTS[nm]
            a, b = i * step, (i + 1) * step
            prog_.add("pool", lambda e: e.dma_start(out=dstv[a:b, :], in_=srcv[a:b, :]), list(gate), [("cvt", nm, i)], dsem=cvsem[nm][i])

        def bank():
            b = bankc["i"] % 8
            bankc["i"] += 1
            return b

        with ExitStack() as e1:
            def sb1(name, shape, dt=F32):
                return e1.enter_context(nc.sbuf_tensor(name, list(shape), dt))

            def sem1(name):
                return e1.enter_context(nc.semaphore(name))

            prog = Prog()
            sems = {k: sem1("s1_" + k) for k in ("pe", "act", "dve", "pool")}
            wsem = [sem1(f"s1_w{i}") for i in range(NSLOT)]
            ldsem = [sem1(f"s1_ld{i}") for i in range(3)]
            sB = sb1("sB", [128, 5, 512])
            sC = sb1("sC", [128, 7, 512])
            tmps = {}

            def T(name):
                if name not in tmps:
                    tmps[name] = sb1("t_" + name, [128, 512])
                return tmps[name]

            def ap_of(x):
                return T(x)[:] if isinstance(x, str) else x[0]

            def key_of(x):
                return ("t", x) if isinstance(x, str) else x[1]

            def tt(out, a, b, op, eng="dve"):
                o, x, y = ap_of(out), ap_of(a), ap_of(b)
                prog.add(eng, lambda e: e.tensor_tensor(out=o, in0=x, in1=y, op=op),
                         [key_of(a), key_of(b)], [key_of(out)])

            def ts(out, a, s1, s2, op0, op1=None, eng="dve", extra=()):
                o, x = ap_of(out), ap_of(a)
                if op1 is None:
                    prog.add(eng, lambda e: e.tensor_scalar(out=o, in0=x, scalar1=s1, scalar2=None, op0=op0),
                             [key_of(a)] + list(extra), [key_of(out)])
                else:
                    prog.add(eng, lambda e: e.tensor_scalar(out=o, in0=x, scalar1=s1, scalar2=s2, op0=op0, op1=op1),
                             [key_of(a)] + list(extra), [key_of(out)])

            def act(out, a, func, scale=1.0, bias=0.0, extra=()):
                o, x = ap_of(out), ap_of(a)
                prog.add("act", lambda e: e.activation(out=o, in_=x, func=func, bias=bias, scale=scale),
                         [key_of(a)] + list(extra), [key_of(out)])

            prog.add("sp", lambda e: e.dma_start(out=par[:], in_=par_d), [], ["par"], dsem=ldsem[0])
            prog.add("sp", lambda e: e.dma_start(out=sB[:], in_=ssmB_d), [], ["sB"], dsem=ldsem[1])
            prog.add("sp", lambda e: e.dma_start(out=sC[:], in_=ssmC_d), [], ["sC"], dsem=ldsem[2])
            prog.add("dve", lambda e: e.memset(onesb[:], 1.0), [], ["onesb"])
            prog.add("dve", lambda e: e.memset(onesf[:], 1.0), [], ["onesf"])
            prog.add("dve", lambda e: e.memset(zerob[:], 0.0), [], ["zerob"])

            prog.add("act", lambda e: e.activation(out=cab[:], in_=pcol("cvec"), func=AF.Silu),
                     ["par"], ["cab"])
            w_ada_r = w_ada.rearrange("(kt p) c -> p kt c", p=128)
            mb = bank()
            for ch in range(8):
                s = ring["i"] % NSLOT
                ring["i"] += 1
                wv = wslot[s][:, 0:8192].rearrange("p (k c) -> p k c", k=16)
                src = w_ada_r[:, :, ch * 512:(ch + 1) * 512]
                prog.add("pool", lambda e, wv=wv, src=src: e.dma_start(out=wv, in_=src),
                         [], [("w", s)], dsem=wsem[s])
                for ti in range(4):
                    j = ch * 4 + ti

                    def fn(e, wv=wv, ti=ti, j=j):
                        ins = None
                        for kt in range(16):
                            ins = e.matmul(ps[:, mb, j:j + 1], lhsT=wv[:, kt, ti * 128:(ti + 1) * 128],
                                           rhs=cab[:, kt:kt + 1], start=(kt == 0), stop=(kt == 15))
                        return ins
                    prog.add("pe", fn, [("w", s), "cab"], [("ps", mb)])
            for i_ in range(CVPARTS["w_in"]):
                cvt_piece(prog, "w_in", i_)
            prog.add("dve", lambda e: e.tensor_tensor(out=modv[:, 0:32], in0=ps[:, mb, 0:32], in1=pcol("bada", 0, 32), op=ALU.add),
                     [("ps", mb), "par"], ["modv"])
            prog.add("dve", lambda e: e.scalar_tensor_tensor(out=g1s[:], in0=modv[:, 16:32], scalar=1.0, in1=pcol("n1g"),
                                                             op0=ALU.add, op1=ALU.mult), ["modv", "par"], ["g1s"])

            def cmul(outr, outi, ar, ai, br, bi):
                tt("c1", ar, br, ALU.mult)
                tt("c2", ai, bi, ALU.mult)
                tt(outr, "c1", "c2", ALU.subtract)
                tt("c1", ar, bi, ALU.mult)
                tt("c2", ai, br, ALU.mult)
                tt(outi, "c1", "c2", ALU.add)

            def lam_tables(are, aim, ldt):
                act("s0", ldt, AF.Exp)
                tt("s1", are, "s0", ALU.mult)
                act("s1", "s1", AF.Exp)
                tt("s2", aim, "s0", ALU.mult)
                for outn, shift in (("li", 0.0), ("lr", PI / 2)):
                    ts("s3", "s2", shift, None, ALU.add)
                    ts("s4", "s2", shift, None, ALU.add)
                    for k in range(8):
                        thr = (2 * k + 1) * PI
                        ts("s5", "s4", thr, -2 * PI, ALU.is_gt, ALU.mult)
                        tt("s3", "s3", "s5", ALU.add)
                    act("s3", "s3", AF.Sin)
                    tt(outn, "s1", "s3", ALU.mult)
                tt("s0", are, are, ALU.mult)
                tt("s1", aim, aim, ALU.mult)
                tt("s0", "s0", "s1", ALU.add)
                o, x = T("s1")[:], T("s0")[:]
                prog.add("dve", lambda e: e.reciprocal(out=o, in_=x), [("t", "s0")], [("t", "s1")])
                ts("s0", "lr", -1.0, None, ALU.add)
                tt("s2", "s0", are, ALU.mult)
                tt("s3", "li", aim, ALU.mult)
                tt("s2", "s2", "s3", ALU.add)
                tt("fr", "s2", "s1", ALU.mult)
                tt("s2", "li", are, ALU.mult)
                tt("s3", "s0", aim, ALU.mult)
                tt("s2", "s2", "s3", ALU.subtract)
                tt("fi", "s2", "s1", ALU.mult)

            def sBk(i):
                return (sB[:, i, :], "sB")

            def sCk(i):
                return (sC[:, i, :], "sC")

            lam_tables(sBk(0), sBk(1), sBk(2))
            cmul("br", "bi", "fr", "fi", sBk(3), sBk(4))
            cmul("l2r", "l2i", "lr", "li", "lr", "li")
            cmul("l3r", "l3i", "l2r", "l2i", "lr", "li")
            for b in range(4):
                if b == 3:
                    vr, vi = "br", "bi"
                else:
                    pw = {0: ("l3r", "l3i"), 1: ("l2r", "l2i"), 2: ("lr", "li")}[b]
                    cmul("wr", "wi", pw[0], pw[1], "br", "bi")
                    vr, vi = "wr", "wi"
                for plane, vn in ((0, vr), (1, vi)):
                    for j in range(2):
                        o = W2[:, :, b, plane, j * 64:(j + 1) * 64]
                        x = T(vn)[:].rearrange("p (c q) -> p c q", c=8)
                        m = pcol("mB", j, j + 1)
                        prog.add("dve", lambda e, o=o, x=x, m=m: e.tensor_scalar(out=o, in0=x, scalar1=m, scalar2=None,
                                                                                  op0=ALU.mult),
                                 [("t", vn), "par"], ["W2"])

            lam_tables(sCk(0), sCk(1), sCk(2))
            cmul("l2r", "l2i", "lr", "li", "lr", "li")
            cmul("l3r", "l3i", "l2r", "l2i", "lr", "li")
            cmul("l4r", "l4i", "l2r", "l2i", "l2r", "l2i")
            cmul("l8r", "l8i", "l4r", "l4i", "l4r", "l4i")
            cmul("l12r", "l12i", "l8r", "l8i", "l4r", "l4i")
            cmul("l16r", "l16i", "l8r", "l8i", "l8r", "l8i")
            cmul("l32r", "l32i", "l16r", "l16i", "l16r", "l16i")
            cmul("l64r", "l64i", "l32r", "l32i", "l32r", "l32i")
            cmul("l128r", "l128i", "l64r", "l64i", "l64r", "l64i")
            for i, nm in enumerate(("l4", "l8", "l12", "l16", "l32", "l64", "l128")):
                xr = T(nm + "r")[:].rearrange("p (a h) -> p a h", h=16)[:, :, 0]
                xi = T(nm + "i")[:].rearrange("p (a h) -> p a h", h=16)[:, :, 0]
                for o, x, sg, kn in ((CA[:, i, :, 0], xr, 1.0, nm + "r"), (CA[:, i, :, 1], xr, 1.0, nm + "r"),
                                     (CB[:, i, :, 0], xi, -1.0, nm + "i"), (CB[:, i, :, 1], xi, 1.0, nm + "i")):
                    prog.add("dve", lambda e, o=o, x=x, sg=sg: e.tensor_scalar(out=o, in0=x, scalar1=sg, scalar2=None,
                                                                              op0=ALU.mult), [("t", kn)], ["LC"])
            pws = [("lr", "li"), ("l2r", "l2i"), ("l3r", "l3i"), ("l4r", "l4i")]
            for b in range(4):
                cmul("wr", "wi", pws[b][0], pws[b][1], sCk(3), sCk(4))
                for plane, vn, sgn in ((0, "wr", 1.0), (1, "wi", -1.0)):
                    for j in range(2):
                        o = W4[:, :, b, plane, j * 16:(j + 1) * 16]
                        x = T(vn)[:].rearrange("p (a h) -> p a h", h=16)
                        m = pcol("mC", j, j + 1)
                        prog.add("dve", lambda e, o=o, x=x, m=m, sgn=sgn: e.tensor_scalar(
                            out=o, in0=x, scalar1=m, scalar2=sgn, op0=ALU.mult, op1=ALU.mult),
                            [("t", vn), "par"], ["W4"])
            cmul("br", "bi", "fr", "fi", sCk(5), sCk(6))
            Lm = sb1("Lm", [128, 2, 32, 32])
            Rm = sb1("Rm", [128, 2, 32, 32])
            ITf = sb1("ITf", [128, 8, 128])
            for plane, (src, sgn) in enumerate(((sCk(3), 1.0), (sCk(4), -1.0))):
                for j in range(2):
                    o = Rm[:, plane, :, j * 16:(j + 1) * 16]
                    x = src[0].rearrange("p (a h) -> p a h", h=16)
                    m = pcol("mC", j, j + 1)
                    prog.add("dve", lambda e, o=o, x=x, m=m, sgn=sgn: e.tensor_scalar(
                        out=o, in0=x, scalar1=m, scalar2=sgn, op0=ALU.mult, op1=ALU.mult), ["sC", "par"], ["Rm"])
            prog.add("dve", lambda e: e.memset(ITf[:], 0.0), [], ["ITf"])
            prog.add("dve", lambda e: e.memset(IT[:], 0.0), [], ["IT"])
            ident = pcol("ident")
            for lag in range(4):
                if lag == 0:
                    vr, vi = "br", "bi"
                else:
                    cmul("wr", "wi", pws[lag - 1][0], pws[lag - 1][1], "br", "bi")
                    vr, vi = "wr", "wi"
                for plane, vn in ((0, vr), (1, vi)):
                    for j in range(2):
                        o = Lm[:, plane, :, j * 16:(j + 1) * 16]
                        x = T(vn)[:].rearrange("p (a h) -> p a h", h=16)
                        m = pcol("mC", j, j + 1)
                        prog.add("dve", lambda e, o=o, x=x, m=m: e.tensor_scalar(out=o, in0=x, scalar1=m, scalar2=None,
                                                                                  op0=ALU.mult),
                                 [("t", vn), "par"], ["Lm"])
                bk = bank()
                for pt in range(32):
                    ct, q = pt // 4, pt % 4
                    o = ps[32 * q:32 * q + 32, bk, ct * 32:ct * 32 + 32]

                    def fn(e, o=o, pt=pt, q=q):
                        e.matmul(o, lhsT=Lm[:, 0, pt, :], rhs=Rm[:, 0, pt, :], start=True, stop=False,
                                 tile_position=(0, 32 * q))
                        return e.matmul(o, lhsT=Lm[:, 1, pt, :], rhs=Rm[:, 1, pt, :], start=False, stop=True,
                                        tile_position=(0, 32 * q))
                    prog.add("pe", fn, ["Lm", "Rm"], [("ps", bk)])
                for q in range(4):
                    x = ps[32 * q:32 * q + 32, bk, 0:256].rearrange("p (c k) -> p c k", c=8)
                    if lag == 0:
                        o = ITf[32 * q:32 * q + 32, :, 32 * q:32 * q + 32]
                        prog.add("dve", lambda e, o=o, x=x: e.tensor_copy(out=o, in_=x), [("ps", bk)], ["ITf"])
                    else:
                        o = IT[32 * q:32 * q + 32, :, lag, 32 * q:32 * q + 32]
                        prog.add("dve", lambda e, o=o, x=x: e.tensor_copy(out=o, in_=x), [("ps", bk)], ["IT"])
                if lag == 0:
                    for ct in range(8):
                        o = ITf[:, ct, :]
                        d = pcol("dsk", ct, ct + 1)
                        prog.add("dve", lambda e, o=o, d=d: e.scalar_tensor_tensor(out=o, in0=ident, scalar=d, in1=o,
                                                                                  op0=ALU.mult, op1=ALU.add),
                                 ["par", "ITf"], ["ITf"])
                    prog.add("dve", lambda e: e.tensor_copy(out=IT[:, :, 0, :], in_=ITf[:]), ["ITf"], ["IT"])
            prog.add("sp", lambda e: None, ["IT", "W4", "W2", "LC", "g1s", "g2s", "modv", "onesb", "onesf", "zerob"], [])
            with nc.Block() as block:
                prog.emit(block, sems)
            nc.all_engine_barrier()

        with ExitStack() as e2:
            def sb2(name, shape, dt=F32):
                return e2.enter_context(nc.sbuf_tensor(name, list(shape), dt))

            def sem2(name):
                return e2.enter_context(nc.semaphore(name))

            prog = Prog()
            sems = {k: sem2("s2_" + k) for k in ("pe", "act", "dve", "pool")}
            wsem = [sem2(f"s2_w{i}") for i in range(NSLOT)]
            wsemH = [sem2(f"s2_wh{i}") for i in range(NSLOT)]
            hsem = sem2("s2_h")
            osem = sem2("s2_o")
            h = sb2("h", [128, 16, NT])
            u = sb2("u", [128, 16, NT], BF16)
            v = sb2("v", [128, 8, 32 + NT], BF16)
            vssm = sb2("vssm", [128, 8, NT], BF16)
            R16 = sb2("R16", [128, 16, NT], BF16)
            sAB = sb2("sAB", [128, 8, NT], BF16)
            rstd = sb2("rstd", [128, NT])
            mu = sb2("mu", [128, NT])
            sq = [sb2(f"sq{i}", [128, NT], BF16) for i in range(2)]
            tf = [sb2(f"tf{i}", [128, NT]) for i in range(2)]
            X = sb2("X", [128, 32, MC + 1, 2])
            G = sb2("G", [128, 32, NCH + 1, 2])
            X2 = sb2("X2", [128, 32, MC + 1, 2])
            ZR = sb2("ZR", [128, 32, MC], BF16)
            ZI = sb2("ZI", [128, 32, MC], BF16)
            P1 = sb2("P1", [128, 32, NCH, 2])
            P2 = sb2("P2", [128, 32, NCH, 2])
            uh = sb2("uh", [128, 16, 32], BF16)
            sgh = sb2("sgh", [128, 8, 32], BF16)
            ring["i"] = 0
            NDG = 4
            DG = sb2("DG", [128, NDG, 128], BF16)
            cnt = {"sq": 0, "tf": 0, "dg": 0}

            shift1, gate1 = modv[:, 0:16], modv[:, 32:48]
            shift2, gate2 = modv[:, 48:64], modv[:, 80:96]

            mode = {"bf": False}

            def wload(W, kt0, nkt, c0, ncols):
                s = ring["i"] % NSLOT
                ring["i"] += 1
                wv = wslot[s][:, 0:nkt * ncols].rearrange("p (k c) -> p k c", k=nkt)
                nm = WNAME.get(id(W))
                if mode["bf"] and nm is not None:
                    src = WBF[nm].rearrange("(kt p) c -> p kt c", p=128)[:, kt0:kt0 + nkt, c0:c0 + ncols]
                    prog.add("sp", lambda e: e.dma_start(out=wv, in_=src), cvt_keys[nm], [("w", s)], dsem=wsemH[s])
                else:
                    src = W.rearrange("(kt p) c -> p kt c", p=128)[:, kt0:kt0 + nkt, c0:c0 + ncols]
                    prog.add("pool", lambda e: e.dma_start(out=wv, in_=src), [], [("w", s)], dsem=wsem[s])
                return wv, ("w", s)

            cvt_keys = {nm: [("cvt", nm, i) for i in range(n)] for nm, n in CVPARTS.items()}
            for i_ in range(CVPARTS["w_in"]):
                prog.add("pool", lambda e: None, [], [("cvt", "w_in", i_)], dsem=cvsem["w_in"][i_])
            cvt_plan = [(nm, i_) for nm in ("w_co", "w_gb", "w_ga", "w_out", "w_ff1") for i_ in range(CVPARTS[nm])]

            def cvt_next(gate=()):
                if cvt_plan:
                    nm, i_ = cvt_plan.pop(0)
                    cvt_piece(prog, nm, i_, gate)

            def mmg(out, pairs, reads, writes):
                def fn(e):
                    ins = None
                    n = len(pairs)
                    for i, (l, r) in enumerate(pairs):
                        ins = e.matmul(out, lhsT=l, rhs=r, start=(i == 0), stop=(i == n - 1))
                    return ins
                prog.add("pe", fn, reads, writes)

            def mod_load(chs):
                return [(wload(w_ada, 0, 16, ch * 512, 512), ch) for ch in chs]

            def mod_compute(loaded):
                for (wv, wk), ch in loaded:
                    b = bank()
                    for ti in range(4):
                        def fn(e, wv=wv, ti=ti, b=b):
                            ins = None
                            for kt in range(16):
                                ins = e.matmul(ps[:, b, ti:ti + 1], lhsT=wv[:, kt, ti * 128:(ti + 1) * 128],
                                               rhs=cab[:, kt:kt + 1], start=(kt == 0), stop=(kt == 15))
                            return ins
                        prog.add("pe", fn, [wk], [("ps", b)])
                    j0 = ch * 4
                    prog.add("dve", lambda e, b=b, j0=j0: e.tensor_tensor(out=modv[:, j0:j0 + 4], in0=ps[:, b, 0:4],
                                                                          in1=pcol("bada", j0, j0 + 4), op=ALU.add),
                             [("ps", b)], ["modv2"])
                    if ch == 23:
                        prog.add("dve", lambda e: e.scalar_tensor_tensor(out=g2s[:], in0=modv[:, 64:80], scalar=1.0, in1=pcol("n2g"),
                                                                         op0=ALU.add, op1=ALU.mult), ["modv2"], ["modv2"])

            def rmsnorm_stats():
                b = bank()
                for kt in range(16):
                    i = cnt["sq"] % 2
                    cnt["sq"] += 1
                    sqa = sq[i]
                    prog.add("act", lambda e, sqa=sqa, kt=kt: e.activation(out=sqa[:], in_=h[:, kt, :], func=AF.Square),
                             [("h", kt)], [("sq", i)])
                    prog.add("pe", lambda e, sqa=sqa, kt=kt: e.matmul(ps[:, b, :], lhsT=onesb[:], rhs=sqa[:],
                                                                      start=(kt == 0), stop=(kt == 15)),
                             [("sq", i)], [("ps", b)])
                prog.add("act", lambda e: e.activation(out=rstd[:], in_=ps[:, b, :], func=AF.Sqrt, bias=EPS, scale=1.0 / D),
                         [("ps", b)], ["rstd"])
                prog.add("dve", lambda e: e.reciprocal(out=rstd[:], in_=rstd[:]), ["rstd"], ["rstd"])

            def modulate(gs, sh, extra):
                for kt in range(16):
                    i = cnt["tf"] % 2
                    cnt["tf"] += 1
                    t = tf[i]
                    prog.add("dve", lambda e, t=t, kt=kt: e.tensor_tensor(out=t[:], in0=h[:, kt, :], in1=rstd[:], op=ALU.mult),
                             [("h", kt), "rstd"], [("tf", i)])
                    prog.add("act", lambda e, t=t, kt=kt: e.activation(out=u[:, kt, :], in_=t[:], func=AF.Identity,
                                                                       bias=sh[:, kt:kt + 1], scale=gs[:, kt:kt + 1]),
                             [("tf", i)] + extra, [("u", kt)])

            ukeys = [("u", kt) for kt in range(16)]

            SCAN_ENG = "dve"

            def cmuladd(dst, ci, src, n, kd, ks):
                shp = [128, 32, n, 2]
                ca = CA[:, ci, :, :].unsqueeze(2).to_broadcast(shp)
                cb = CB[:, ci, :, :].unsqueeze(2).to_broadcast(shp)
                p1, p2 = P1[:, :, 0:n, :], P2[:, :, 0:n, :]
                srcsw = src[:, :, :, ::-1]
                prog.add(SCAN_ENG, lambda e: e.tensor_tensor(out=p1, in0=src, in1=ca, op=ALU.mult), [ks], ["P1"])
                prog.add(SCAN_ENG, lambda e: e.tensor_tensor(out=p2, in0=srcsw, in1=cb, op=ALU.mult), [ks], ["P2"])
                prog.add(SCAN_ENG, lambda e: e.tensor_tensor(out=p1, in0=p1, in1=p2, op=ALU.add), ["P1", "P2"], ["P1"])
                prog.add(SCAN_ENG, lambda e: e.tensor_tensor(out=dst, in0=dst, in1=p1, op=ALU.add), ["P1", kd], [kd])

            def Xc(a, n=NCH, step=4):
                return X[:, :, 1 + a:2 + a + step * (n - 1):step, :]

            def ssm_q(sb_i, XB=None, xk="X"):
                XB = X if XB is None else XB
                t0 = sb_i * SUB
                for half in range(2):
                    banks = [bank() for _ in range(4)]
                    for cl in range(4):
                        ct = half * 4 + cl
                        for q in range(4):
                            for plane in range(2):
                                c0 = (cl * 2 + plane) * MC
                                o = ps[:, banks[q], c0:c0 + MC]

                                def fn(e, o=o, ct=ct, q=q, plane=plane):
                                    ins = None
                                    for b in range(4):
                                        ins = e.matmul(o, lhsT=W2[32 * q:32 * q + 32, ct, b, plane, :],
                                                       rhs=vssm[32 * q:32 * q + 32, ct, t0 + b * MC:t0 + (b + 1) * MC],
                                                       start=(b == 0), stop=(b == 3), tile_position=(32 * q, 0))
                                    return ins
                                prog.add("pe", fn, [("vssm", ct, sb_i)], [("ps", banks[q])])
                    for q in range(4):
                        src = ps[:, banks[q], 0:8 * MC].rearrange("p (c l m) -> p c l m", c=4, l=2)
                        for plane in range(2):
                            o = XB[:, half * 16 + q:half * 16 + 16:4, 1:MC + 1, plane]
                            x = src[:, :, plane, :]
                            prog.add("act", lambda e, o=o, x=x: e.activation(out=o, in_=x, func=AF.Identity),
                                     [("ps", banks[q])], [xk])

            def prefix_scan(XB, xk, last):
                for a in range(1, 4):
                    cmuladd(XB[:, :, 1 + a:2 + a + 4 * (NCH - 1):4, :], 0, XB[:, :, a:1 + a + 4 * (NCH - 1):4, :], NCH, xk, xk)
                cmuladd(XB[:, :, 8:33:8, :], 3, XB[:, :, 4:29:8, :], 4, xk, xk)
                cmuladd(XB[:, :, 16:33:16, :], 4, XB[:, :, 8:25:16, :], 2, xk, xk)
                cmuladd(XB[:, :, 32:33, :], 5, XB[:, :, 16:17, :], 1, xk, xk)
                cmuladd(XB[:, :, 32:33, :], 6, G[:, :, 0:1, :], 1, xk, "G")
                if last:
                    fl = pcol("flag")
                    prog.add(SCAN_ENG, lambda e: e.tensor_scalar(out=G[:, :, 0:1, :], in0=XB[:, :, 32:33, :], scalar1=fl,
                                                                 scalar2=None, op0=ALU.mult), [xk], ["G"])
                    prog.add(SCAN_ENG, lambda e: e.tensor_copy(out=X[:, :, 0:1, :], in_=G[:, :, 0:1, :]), ["G"], ["X"])
                else:
                    prog.add(SCAN_ENG, lambda e: e.tensor_copy(out=G[:, :, 0:1, :], in_=XB[:, :, 32:33, :]), [xk], ["G"])

            def ssm_scan(prefix, last_prefix_sub):
                for a in range(1, 4):
                    cmuladd(Xc(a), 0, Xc(a - 1), NCH, "X", "X")
                if prefix:
                    raise AssertionError("use prefix_scan")
                prog.add(SCAN_ENG, lambda e: e.tensor_copy(out=G[:, :, 1:NCH + 1, :], in_=X[:, :, 4:4 * NCH + 1:4, :]),
                         ["X"], ["G"])
                for k in range(4):
                    d = 1 << k
                    cmuladd(G[:, :, d:NCH + 1, :], 3 + k, G[:, :, 0:NCH + 1 - d, :], NCH + 1 - d, "G", "G")
                for a in range(3):
                    cmuladd(Xc(a), a, G[:, :, 0:NCH, :], NCH, "X", "G")
                prog.add(SCAN_ENG, lambda e: e.tensor_copy(out=Xc(3), in_=G[:, :, 1:NCH + 1, :]), ["G"], ["X"])
                for Z, pl in ((ZR, 0), (ZI, 1)):
                    prog.add(SCAN_ENG, lambda e, Z=Z, pl=pl: e.tensor_copy(out=Z[:], in_=X[:, :, 0:MC, pl]), ["X"], ["Z"])
                prog.add(SCAN_ENG, lambda e: e.tensor_copy(out=G[:, :, 0:1, :], in_=G[:, :, NCH:NCH + 1, :]), ["G"], ["G"])
                prog.add(SCAN_ENG, lambda e: e.tensor_copy(out=X[:, :, 0:1, :], in_=G[:, :, 0:1, :]), ["G", "Z"], ["X"])

            def ssm_y(sb_i):
                t0 = sb_i * SUB
                for ct in range(8):
                    b_ = bank()
                    Y = ps[:, b_, 0:SUB]

                    def fn(e, ct=ct, b_=b_):
                        e.matmul(ps[:, b_, 0:SUB], lhsT=zerob[:], rhs=vssm[:, ct, t0:t0 + SUB], start=True, stop=False)
                        for b in range(4):
                            for b2 in range(b + 1):
                                e.matmul(ps[:, b_, b * MC:(b + 1) * MC], lhsT=IT[:, ct, b - b2, :],
                                         rhs=vssm[:, ct, t0 + b2 * MC:t0 + (b2 + 1) * MC], start=False, stop=False)
                        ins = None
                        for q in range(4):
                            pt = ct * 4 + q
                            for b in range(4):
                                for plane, Z in ((0, ZR), (1, ZI)):
                                    last = (b == 3 and plane == 1)
                                    ins = e.matmul(ps[32 * q:32 * q + 32, b_, b * MC:(b + 1) * MC],
                                                   lhsT=W4[:, pt, b, plane, :], rhs=Z[:, pt, :], start=False, stop=last,
                                                   tile_position=(0, 32 * q))
                        return ins
                    prog.add("pe", fn, [("vssm", ct, sb_i), "Z"], [("ps", b_)])
                    o = vssm[:, ct, t0:t0 + SUB].rearrange("p (m b) -> p b m", b=4)
                    Yv = Y.rearrange("p (b m) -> p b m", b=4)
                    prog.add("act", lambda e, o=o, Yv=Yv: e.activation(out=o, in_=Yv, func=AF.Gelu_apprx_tanh),
                             [("ps", b_)], [("vssm", ct, sb_i)])

            def win_tile(wv, wk, ti, evac):
                b = bank()
                wkl = list(wk) if isinstance(wk, list) else [wk]
                mmg(ps[:, b, :], [(wv[:, kt, ti * 128:(ti + 1) * 128], u[:, kt, :]) for kt in range(16)],
                    wkl + ukeys, [("ps", b)])
                evac(b)

            def pre_vssm():
                return [wload(w_in, 0, 16, ch * 512, 512) for ch in (4, 5)]

            def do_vssm(pre):
                ct = 0
                for wv, wk, ntl in pre:
                    for ti in range(ntl):
                        def evac(b, ct=ct):
                            o = vssm[:, ct, :].rearrange("p (s b m) -> p s b m", s=NSUB, b=4)
                            x = ps[:, b, :].rearrange("p (s m b) -> p s b m", s=NSUB, b=4)
                            prog.add("act", lambda e, o=o, x=x: e.activation(out=o, in_=x, func=AF.Identity),
                                     [("ps", b)], [("vssm", ct, s_) for s_ in range(NSUB)])
                        win_tile(wv, wk, ti, evac)
                        ct += 1

            RK = [("R", j) for j in range(16)]
            SK = [("sAB", j) for j in range(8)]
            VK = [("v", c) for c in range(8)] + [("vh", c) for c in range(8)]

            def load_resident():
                w_in_r = w_in.rearrange("(kt p) c -> p kt c", p=128)
                sabw = sAB[:].rearrange("p c t -> p (c t)").rearrange("p (k c) -> p k c", k=16)
                vw = v[:].rearrange("p c t -> p (c t)")[:, 0:4096].rearrange("p (k c) -> p k c", k=16)
                res = []
                for i, (dst, c0, nc_, keys, ntl) in enumerate(((R16[:], 2048, 512, RK, 4), (sabw, 2560, 256, SK, 2),
                                                              (vw, 2816, 256, VK, 2))):
                    sem = sem2(f"s2_res{i}")
                    src = w_in_r[:, :, c0:c0 + nc_]
                    prog.add("pool", lambda e, dst=dst, src=src: e.dma_start(out=dst, in_=src), [], keys, dsem=sem)
                    res.append((dst, keys, ntl))
                return res

            def do_glu(first):
                fl = pcol("flag")
                for ch in (2, 3, 0, 1):
                    wv, wk = wload(w_in, 0, 16, ch * 512, 512)
                    for ti in range(4):
                        ct = (ch % 2) * 4 + ti
                        if ch >= 2:
                            def evac(b, ct=ct):
                                prog.add("act", lambda e: e.activation(out=sAB[:, ct, :], in_=ps[:, b, :], func=AF.Sigmoid),
                                         [("ps", b)], [("sAB", ct)])
                        else:
                            def evac(b, ct=ct):
                                prog.add("dve", lambda e: e.tensor_tensor(out=v[:, ct, 32:32 + NT], in0=ps[:, b, :],
                                                                          in1=sAB[:, ct, :], op=ALU.mult),
                                         [("ps", b), ("sAB", ct)], [("v", ct)])
                        win_tile(wv, wk, ti, evac)
                        if first:
                            b2 = bank()
                            mmg(ps[:, b2, 0:32], [(wv[:, kt, ti * 128:(ti + 1) * 128], uh[:, kt, :]) for kt in range(16)],
                                [wk, "uh"], [("ps", b2)])
                            if ch >= 2:
                                prog.add("act", lambda e, ct=ct, b2=b2: e.activation(out=sgh[:, ct, :], in_=ps[:, b2, 0:32], func=AF.Sigmoid),
                                         [("ps", b2)], [("sgh", ct)])
                            else:
                                prog.add("dve", lambda e, ct=ct, b2=b2: e.scalar_tensor_tensor(out=v[:, ct, 0:32], in0=ps[:, b2, 0:32], scalar=fl,
                                                                                             in1=sgh[:, ct, :], op0=ALU.mult, op1=ALU.mult),
                                         [("ps", b2), ("sgh", ct)], [("vh", ct)])

            def halo_copy(ct, use_flag):
                if use_flag:
                    fl = pcol("flag")
                    prog.add("dve", lambda e: e.tensor_scalar(out=v[:, ct, 0:32], in0=v[:, ct, NT:NT + 32], scalar1=fl,
                                                              scalar2=None, op0=ALU.mult), [("v", ct)], [("vh", ct)])
                else:
                    prog.add("pool", lambda e: e.tensor_copy(out=v[:, ct, 0:32], in_=v[:, ct, NT:NT + 32]),
                             [("v", ct)], [("vh", ct)])

            def conv_ct(ct):
                wdw = pcol("wdw")
                ident = pcol("ident")
                b = bank()
                for k in range(31):
                    i = cnt["dg"] % NDG
                    cnt["dg"] += 1
                    wk_ = wdw[:, ct * 31 + k:ct * 31 + k + 1]
                    prog.add("act", lambda e, i=i, wk_=wk_: e.activation(out=DG[:, i, :], in_=ident, func=AF.Identity, scale=wk_),
                             [], [("dg", i)])
                    src = v[:, ct, 2 + k:2 + k + NT]
                    prog.add("pe", lambda e, i=i, src=src, k=k: e.matmul(ps[:, b, :], lhsT=DG[:, i, :], rhs=src,
                                                                        start=(k == 0), stop=(k == 30)),
                             [("dg", i), ("v", ct), ("vh", ct)], [("ps", b)])
                bd = pcol("bdw", ct, ct + 1)
                prog.add("act", lambda e: e.activation(out=sAB[:, ct, :], in_=ps[:, b, :], func=AF.Identity, bias=bd),
                         [("ps", b)], [("sAB", ct)])
                halo_copy(ct, False)

            def do_ln():
                b1, b2 = bank(), bank()
                for ct in range(8):
                    rk = [("sAB", ct)]
                    x = sAB[:, ct, :]
                    prog.add("pe", lambda e, x=x, ct=ct: e.matmul(ps[:, b1, :], lhsT=onesb[:], rhs=x, start=(ct == 0), stop=(ct == 7)),
                             rk, [("ps", b1)])
                    i = cnt["sq"] % 2
                    cnt["sq"] += 1
                    t = sq[i]
                    prog.add("act", lambda e, x=x, t=t: e.activation(out=t[:], in_=x, func=AF.Square), rk, [("sq", i)])
                    prog.add("pe", lambda e, t=t, ct=ct: e.matmul(ps[:, b2, :], lhsT=onesb[:], rhs=t[:], start=(ct == 0), stop=(ct == 7)),
                             [("sq", i)], [("ps", b2)])
                prog.add("dve", lambda e: e.tensor_scalar(out=mu[:], in0=ps[:, b1, :], scalar1=1.0 / 1024, scalar2=None, op0=ALU.mult),
                         [("ps", b1)], ["mu"])
                prog.add("dve", lambda e: e.tensor_tensor(out=rstd[:], in0=mu[:], in1=mu[:], op=ALU.mult), ["mu"], ["rstd"])
                prog.add("dve", lambda e: e.scalar_tensor_tensor(out=rstd[:], in0=ps[:, b2, :], scalar=1.0 / 1024, in1=rstd[:],
                                                                 op0=ALU.mult, op1=ALU.subtract), [("ps", b2), "rstd"], ["rstd"])
                prog.add("act", lambda e: e.activation(out=rstd[:], in_=rstd[:], func=AF.Sqrt, bias=EPS, scale=1.0), ["rstd"], ["rstd"])
                prog.add("dve", lambda e: e.reciprocal(out=rstd[:], in_=rstd[:]), ["rstd"], ["rstd"])
                for ct in range(8):
                    rk = [("sAB", ct)]
                    i = cnt["tf"] % 2
                    cnt["tf"] += 1
                    t = tf[i]
                    x = sAB[:, ct, :]
                    prog.add("dve", lambda e, x=x, t=t: e.tensor_tensor(out=t[:], in0=x, in1=mu[:], op=ALU.subtract),
                             rk + ["mu"], [("tf", i)])
                    prog.add("dve", lambda e, t=t: e.tensor_tensor(out=t[:], in0=t[:], in1=rstd[:], op=ALU.mult),
                             [("tf", i), "rstd"], [("tf", i)])
                    lg, lb = pcol("lng", ct, ct + 1), pcol("lnb", ct, ct + 1)
                    prog.add("act", lambda e, t=t, ct=ct, lg=lg, lb=lb: e.activation(out=v[:, ct, 32:32 + NT], in_=t[:], func=AF.Silu,
                                                                                    bias=lb, scale=lg),
                             [("tf", i), ("vh", ct)], [("v", ct)])

            def merge_p1(chs, pre):
                for ch in chs:
                    wv, wk = pre
                    for ti in range(4):
                        j = (ch - 6) * 4 + ti

                        def evac(b, j=j):
                            prog.add("act", lambda e: e.activation(out=R16[:, j, :], in_=ps[:, b, :], func=AF.Sigmoid),
                                     [("ps", b)], [("R", j)])
                        win_tile(wv, wk, ti, evac)

            def merge_p23():
                for ch in range(2):
                    wv, wk = wload(w_co, 0, 8, ch * 1024, 1024)
                    for ti in range(8):
                        j = ch * 8 + ti
                        b = bank()
                        mmg(ps[:, b, :], [(wv[:, kt, ti * 128:(ti + 1) * 128], v[:, kt, 32:32 + NT]) for kt in range(8)],
                            [wk] + [("v", kt) for kt in range(8)], [("ps", b)])
                        prog.add("dve", lambda e, j=j, b=b: e.tensor_tensor(out=R16[:, j, :], in0=ps[:, b, :], in1=R16[:, j, :],
                                                                            op=ALU.mult), [("ps", b), ("R", j)], [("R", j)])
                skeys = [("vssm", kt, s_) for kt in range(8) for s_ in range(NSUB)]
                for jb in range(4):
                    wv, wk = wload(w_in, 0, 16, (10 + jb) * 512, 512)
                    for ti in range(4):
                        def evac(b, ti=ti):
                            prog.add("act", lambda e: e.activation(out=sAB[:, ti, :], in_=ps[:, b, :], func=AF.Sigmoid),
                                     [("ps", b)], [("sAB", ti)])
                        win_tile(wv, wk, ti, evac)
                    for which, W in ((0, w_gb), (1, w_ga)):
                        wv, wk = wload(W, 0, 8, jb * 512, 512)
                        for ti in range(4):
                            j = jb * 4 + ti
                            b = bank()
                            mmg(ps[:, b, :], [(wv[:, kt, ti * 128:(ti + 1) * 128], vssm[:, kt, :]) for kt in range(8)],
                                [wk] + skeys, [("ps", b)])
                            kB = ("sAB", 4 + ti)
                            if which == 0:
                                prog.add("act", lambda e, ti=ti, b=b: e.activation(out=sAB[:, 4 + ti, :], in_=ps[:, b, :], func=AF.Sigmoid),
                                         [("ps", b)], [kB])
                            else:
                                prog.add("dve", lambda e, ti=ti, b=b: e.tensor_tensor(out=sAB[:, 4 + ti, :], in0=ps[:, b, :], in1=sAB[:, 4 + ti, :],
                                                                                      op=ALU.mult), [("ps", b), kB], [kB])
                                prog.add("dve", lambda e, ti=ti: e.tensor_tensor(out=sAB[:, 4 + ti, :], in0=sAB[:, 4 + ti, :], in1=sAB[:, ti, :],
                                                                                 op=ALU.mult), [kB, ("sAB", ti)], [kB])
                                prog.add("dve", lambda e, ti=ti, j=j: e.tensor_tensor(out=R16[:, j, :], in0=R16[:, j, :], in1=sAB[:, 4 + ti, :],
                                                                                      op=ALU.add), [kB, ("R", j)], [("R", j)])

            def do_wout():
                for ch in range(4):
                    wv, wk = wload(w_out, 0, 16, ch * 512, 512)
                    for ti in range(4):
                        j = ch * 4 + ti
                        b = bank()
                        mmg(ps[:, b, :], [(wv[:, kt, ti * 128:(ti + 1) * 128], R16[:, kt, :]) for kt in range(16)],
                            [wk] + [("R", kt) for kt in range(16)], [("ps", b)])
                        prog.add("dve", lambda e, j=j, b=b: e.scalar_tensor_tensor(out=h[:, j, :], in0=ps[:, b, :], scalar=gate1[:, j:j + 1],
                                                                                   in1=h[:, j, :], op0=ALU.mult, op1=ALU.add),
                                 [("ps", b), ("h", j), "modv2"], [("h", j)])

            def do_ffn():
                for hb in range(4):
                    for c4 in range(4):
                        wv, wk = wload(w_ff1, 0, 16, (hb * 4 + c4) * 512, 512)
                        for ti in range(4):
                            i_ = c4 * 4 + ti

                            def evac(b, i_=i_):
                                k = cnt["sq"] % 2
                                cnt["sq"] += 1
                                s_ = sq[k]
                                prog.add("act", lambda e: e.activation(out=s_[:], in_=ps[:, b, :], func=AF.Relu), [("ps", b)], [("sq", k)])
                                prog.add("dve", lambda e: e.tensor_tensor(out=R16[:, i_, :], in0=ps[:, b, :], in1=s_[:], op=ALU.mult),
                                         [("ps", b), ("sq", k)], [("R", i_)])
                            win_tile(wv, wk, ti, evac)
                    for cb in range(4):
                        wv, wk = wload(w_ff2, hb * 16, 16, cb * 512, 512)
                        for ti in range(4):
                            j = cb * 4 + ti
                            b = bank()
                            mmg(ps[:, b, :], [(wv[:, kt, ti * 128:(ti + 1) * 128], R16[:, kt, :]) for kt in range(16)],
                                [wk] + [("R", kt) for kt in range(16)], [("ps", b)])
                            prog.add("dve", lambda e, j=j, b=b: e.scalar_tensor_tensor(out=h[:, j, :], in0=ps[:, b, :], scalar=gate2[:, j:j + 1],
                                                                                       in1=h[:, j, :], op0=ALU.mult, op1=ALU.add),
                                     [("ps", b), ("h", j)], [("h", j)])

            hkeys = [("h", kt) for kt in range(16)]
            prog.add("dve", lambda e: e.memset(G[:], 0.0), [], ["G"])
            prog.add("dve", lambda e: e.memset(X[:], 0.0), [], ["X"])
            prog.add("dve", lambda e: e.memset(X2[:], 0.0), [], ["X2"])
            prog.add("dve", lambda e: e.memset(v[:], 0.0), [], [("v", c) for c in range(8)] + [("vh", c) for c in range(8)])

            steps = [("p", i) for i in range(NTILE)] + [("m", i) for i in range(NTILE)]
            resw = load_resident()
            for kind, ti_ in steps:
                srcT = xpT if kind == "p" else xT
                src = srcT.rearrange("(kt p) t -> p kt t", p=128)[:, :, ti_ * NT:(ti_ + 1) * NT]
                prog.add("sp", lambda e, src=src: e.dma_start(out=h[:], in_=src), [], hkeys, dsem=hsem)
                rmsnorm_stats()
                modulate(g1s, shift1, [])
                if kind == "p":
                    ml = mod_load([8 + 2 * ti_, 9 + 2 * ti_])
                    do_vssm(resw)
                    for s_i in range(NSUB):
                        XB, xk = (X, "X") if s_i % 2 == 0 else (X2, "X2")
                        ssm_q(s_i, XB, xk)
                        cvt_next([xk])
                        prefix_scan(XB, xk, ti_ == NTILE - 1 and s_i == NSUB - 1)
                    mod_compute(ml)
                    if ti_ == NTILE - 1:
                        prog.add("act", lambda e: e.activation(out=uh[:], in_=u[:, :, NT - 32:NT], func=AF.Identity), ukeys, ["uh"])
                    continue
                while cvt_plan:
                    cvt_next()
                mode["bf"] = True
                do_glu(ti_ == 0)
                do_vssm([(wv_, wk_, 4) for wv_, wk_ in pre_vssm()])
                for s_i in range(NSUB):
                    p1w = wload(w_in, 0, 16, (6 + s_i) * 512, 512)
                    ssm_q(s_i)
                    ssm_scan(False, False)
                    conv_ct(2 * s_i)
                    conv_ct(2 * s_i + 1)
                    merge_p1([6 + s_i], p1w)
                    if ti_ == 0:
                        mod_compute(mod_load([16 + 2 * s_i, 17 + 2 * s_i]))
                    ssm_y(s_i)
                do_ln()
                merge_p23()
                do_wout()
                rmsnorm_stats()
                modulate(g2s, shift2, ["modv2"])
                do_ffn()
                rmsnorm_stats()
                fgc = pcol("fg")
                for kt in range(16):
                    prog.add("dve", lambda e, kt=kt: e.scalar_tensor_tensor(out=h[:, kt, :], in0=h[:, kt, :], scalar=fgc[:, kt:kt + 1],
                                                                            in1=rstd[:], op0=ALU.mult, op1=ALU.mult),
                             [("h", kt), "rstd"], [("h", kt)])
                dst = outT.rearrange("(kt p) t -> p kt t", p=128)[:, :, ti_ * NT:(ti_ + 1) * NT]
                prog.add("sp", lambda e, dst=dst: e.dma_start(out=dst, in_=h[:]), hkeys, [("out", ti_)], dsem=osem)
            prog.add("sp", lambda e: None, [("out", i) for i in range(NTILE)], [])
            with nc.Block() as block:
                prog.emit(block, sems)
    return nc


_NC = None


def _prep(x, c, w_ada, b_ada, norm1_g, w_in, w_dw, b_dw, ln_g, ln_b, w_conv_out, a_re, a_im, log_dt, b_re, b_im,
          c_re, c_im, d_skip, w_glu_a, w_glu_b, w_out, norm2_g, w_ff1, w_ff2, final_g):
    f = np.float32
    shared = {
        "w_ada": np.ascontiguousarray(w_ada[0], f), "w_in": np.ascontiguousarray(w_in[0], f),
        "w_conv_out": np.ascontiguousarray(w_conv_out[0], f), "w_glu_a": np.ascontiguousarray(w_glu_a[0], f),
        "w_glu_b": np.ascontiguousarray(w_glu_b[0], f), "w_out": np.ascontiguousarray(w_out[0], f),
        "w_ff1": np.ascontiguousarray(w_ff1[0], f), "w_ff2": np.ascontiguousarray(w_ff2[0], f),
    }
    are, aim, ldt = a_re[0], a_im[0], log_dt[0]
    bre, bim, cre, cim = b_re[0], b_im[0], c_re[0], c_im[0]
    def Bl_a(a):
        t = a.reshape(8, 8, 64).transpose(1, 0, 2)
        return np.broadcast_to(t[:, None], (8, 16, 8, 64)).reshape(128, 512)
    ldB = np.broadcast_to(ldt.reshape(8, 8).T[:, None, :, None], (8, 16, 8, 64)).reshape(128, 512)
    def Bl_b(b):
        return b.reshape(8, 8, 64, 16).transpose(1, 3, 0, 2).reshape(128, 512)
    ssmB = np.stack([Bl_a(are), Bl_a(aim), ldB, Bl_b(bre), Bl_b(bim)], axis=1).astype(f)
    def Cl_a(a):
        t = a.reshape(32, 2, 64).transpose(1, 2, 0)
        return np.broadcast_to(t[..., None], (2, 64, 32, 16)).reshape(128, 512)
    ldC = np.broadcast_to(ldt.reshape(32, 2).T[:, None, :, None], (2, 64, 32, 16)).reshape(128, 512)
    def Cl_c(cc):
        return cc.reshape(32, 2, 16, 64).transpose(1, 3, 0, 2).reshape(128, 512)
    def Cl_b(b):
        return b.reshape(32, 2, 64, 16).transpose(1, 2, 0, 3).reshape(128, 512)
    ssmC = np.stack([Cl_a(are), Cl_a(aim), ldC, Cl_c(cre), Cl_c(cim), Cl_b(bre), Cl_b(bim)], axis=1).astype(f)
    shared["ssmB"] = np.ascontiguousarray(ssmB)
    shared["ssmC"] = np.ascontiguousarray(ssmC)

    def fm(vec, n):
        return np.asarray(vec, f).reshape(n, 128).T
    par_base = np.zeros((128, NPAR), f)
    def put(name, arr):
        o, w = PC[name]
        par_base[:, o:o + w] = arr
    put("bada", fm(b_ada[0], 96))
    put("n1g", fm(norm1_g[0], 16)); put("n2g", fm(norm2_g[0], 16)); put("fg", fm(final_g, 16))
    put("wdw", np.asarray(w_dw[0], f).reshape(31, 8, 128).transpose(2, 1, 0).reshape(128, 248))
    put("bdw", fm(b_dw[0], 8)); put("lng", fm(ln_g[0], 8)); put("lnb", fm(ln_b[0], 8)); put("dsk", fm(d_skip[0], 8))
    pidx = np.arange(128)
    mB = np.stack([((pidx // 16) % 2 == j) for j in range(2)], axis=1).astype(f)
    mC = np.stack([((pidx // 64) == j) for j in range(2)], axis=1).astype(f)
    put("mB", mB); put("mC", mC); put("ident", np.eye(128, dtype=f))
    in_maps = []
    zeros = np.zeros((D, NTOK), f)
    for core in range(NCORE):
        b, s = core // 2, core % 2
        m = dict(shared)
        m["xT"] = np.ascontiguousarray(x[b, s * NTOK:(s + 1) * NTOK, :].T)
        m["xpT"] = np.ascontiguousarray(x[b, 0:NTOK, :].T) if s == 1 else zeros
        p = par_base.copy()
        o, w = PC["cvec"]; p[:, o:o + w] = fm(c[b], 16)
        o, w = PC["flag"]; p[:, o:o + w] = float(s)
        m["par"] = p
        in_maps.append(m)
    return in_maps


def kernel(**inputs):
    global _NC
    inputs = {k: np.asarray(v) for k, v in inputs.items()}
    in_maps = _prep(**inputs)
    if _NC is None:
        _NC = build()
    res = run_bass_kernel_spmd(_NC, in_maps, core_ids=list(range(NCORE)))
    out = np.empty((4, 4096, D), np.float32)
    for core in range(NCORE):
        b, s = core // 2, core % 2
        out[b, s * NTOK:(s + 1) * NTOK, :] = res.results[core]["outT"].T
    return out
```

```python
import math
import numpy as np
from contextlib import ExitStack
import concourse.bass as bass
import concourse.mybir as mybir
from concourse.bass_utils import run_bass_kernel_spmd

F32, BF16 = mybir.dt.float32, mybir.dt.bfloat16
AF = mybir.ActivationFunctionType
ALU = mybir.AluOpType

NCORE = 8
D = 2048
NTOK = 2048
NT = 512
NTILE = NTOK // NT
SUB = 128
NSUB = NT // SUB
MC = SUB // 4
NCH = MC // 4
NSLOT = 2
EPS = 1e-6
PI = math.pi

PC = {}
_off = 0
for _n, _w in [("cvec", 16), ("bada", 96), ("n1g", 16), ("n2g", 16), ("fg", 16), ("wdw", 248),
               ("bdw", 8), ("lng", 8), ("lnb", 8), ("dsk", 8), ("flag", 1), ("mB", 2), ("mC", 2),
               ("ident", 128)]:
    PC[_n] = (_off, _w)
    _off += _w
NPAR = _off


class Op:
    __slots__ = ("eng", "fn", "deps", "needs", "val", "dsem", "dval")


class Prog:
    ENG = ("pe", "act", "dve", "pool", "sp")

    def __init__(self):
        self.ops = {e: [] for e in self.ENG}
        self.lastw = {}
        self.readers = {}
        self.dmacount = {}

    def add(self, eng, fn, reads=(), writes=(), dsem=None):
        op = Op()
        op.eng, op.fn, op.needs, op.val, op.dsem, op.dval = eng, fn, False, None, dsem, None
        if dsem is not None:
            self.dmacount[id(dsem)] = self.dmacount.get(id(dsem), 0) + 16
            op.dval = self.dmacount[id(dsem)]
        deps, seen = [], set()
        cand = []
        for k in reads:
            w = self.lastw.get(k)
            if w is not None:
                cand.append(w)
        for k in writes:
            w = self.lastw.get(k)
            if w is not None:
                cand.append(w)
            cand.extend(self.readers.get(k, ()))
        for d in cand:
            if id(d) in seen:
                continue
            seen.add(id(d))
            if d.dsem is None and d.eng == eng and eng == "pe":
                continue
            deps.append(d)
        op.deps = deps
        for k in reads:
            self.readers.setdefault(k, []).append(op)
        for k in writes:
            self.lastw[k] = op
            self.readers[k] = []
        self.ops[eng].append(op)
        return op

    def emit(self, block, sems):
        for e in self.ENG:
            for op in self.ops[e]:
                for d in op.deps:
                    if d.dsem is None:
                        d.needs = True
        for e in self.ENG:
            n = 0
            for op in self.ops[e]:
                if op.dsem is None and op.needs:
                    n += 1
                    op.val = n

        def run(engname, eobj):
            known = {}
            for op in self.ops[engname]:
                need = {}
                for d in op.deps:
                    if d.dsem is not None:
                        sem, val = d.dsem, d.dval
                    else:
                        sem, val = sems[d.eng], d.val
                    if need.get(id(sem), (None, 0))[1] < val:
                        need[id(sem)] = (sem, val)
                for sid, (sem, val) in need.items():
                    if known.get(sid, 0) >= val:
                        continue
                    eobj.wait_ge(sem, val)
                    known[sid] = val
                ins = op.fn(eobj)
                if ins is None:
                    continue
                if op.dsem is not None:
                    ins.then_inc(op.dsem, 16)
                elif op.needs:
                    ins.then_inc(sems[engname], 1)

        block.tensor(lambda e: run("pe", e))
        block.scalar(lambda e: run("act", e))
        block.vector(lambda e: run("dve", e))
        block.gpsimd(lambda e: run("pool", e))
        block.sync(lambda e: run("sp", e))


def build():
    nc = bass.Bass("TRN2", target_bir_lowering=False)

    def din(name, shape):
        return nc.dram_tensor(name, list(shape), F32, kind="ExternalInput").ap()

    xT = din("xT", [D, NTOK])
    xpT = din("xpT", [D, NTOK])
    par_d = din("par", [128, NPAR])
    ssmB_d = din("ssmB", [128, 5, 512])
    ssmC_d = din("ssmC", [128, 7, 512])
    w_ada = din("w_ada", [D, 6 * D])
    w_in = din("w_in", [D, 7168])
    w_co = din("w_conv_out", [1024, D])
    w_ga = din("w_glu_a", [1024, D])
    w_gb = din("w_glu_b", [1024, D])
    w_out = din("w_out", [D, D])
    w_ff1 = din("w_ff1", [D, 4 * D])
    w_ff2 = din("w_ff2", [4 * D, D])
    outT = nc.dram_tensor("outT", [D, NTOK], F32, kind="ExternalOutput").ap()
    WSRC = {"w_in": w_in, "w_co": w_co, "w_ga": w_ga, "w_gb": w_gb, "w_out": w_out, "w_ff1": w_ff1}
    WBF = {k: nc.dram_tensor("bf_" + k, list(a.shape), BF16, kind="Internal").ap() for k, a in WSRC.items()}
    WNAME = {id(a): k for k, a in WSRC.items()}

    es = ExitStack()
    with es:
        def sb(name, shape, dt=F32):
            return es.enter_context(nc.sbuf_tensor(name, list(shape), dt))

        par = sb("par_sb", [128, NPAR])
        modv = sb("modv", [128, 96])
        g1s = sb("g1s", [128, 16])
        g2s = sb("g2s", [128, 16])
        W2 = sb("W2", [128, 8, 4, 2, 128], BF16)
        W4 = sb("W4", [128, 32, 4, 2, 32], BF16)
        IT = sb("IT", [128, 8, 4, 128], BF16)
        CA = sb("CA", [128, 7, 32, 2])
        CB = sb("CB", [128, 7, 32, 2])
        cab = sb("cab", [128, 16], BF16)
        onesb = sb("onesb", [128, 128], BF16)
        onesf = sb("onesf", [128, 128])
        zerob = sb("zerob", [128, 128], BF16)
        wslot = [sb(f"wslot{i}", [128, 8192], BF16) for i in range(NSLOT)]
        ps = es.enter_context(nc.psum_tensor("ps", [128, 8, 512], F32))

        def pcol(name, a=0, b=None):
            o, w = PC[name]
            b = w if b is None else b
            return par[:, o + a:o + b]

        ring = {"i": 0}
        bankc = {"i": 0}
        CVPARTS = {"w_in": 7, "w_co": 1, "w_ga": 1, "w_gb": 1, "w_out": 2, "w_ff1": 8}
        cvsem = {nm: [es.enter_context(nc.semaphore(f"cv_{nm}_{i}")) for i in range(n)] for nm, n in CVPARTS.items()}

        def cvt_piece(prog_, nm, i, gate=()):
            srcv = WSRC[nm].rearrange("k (a c) -> (k a) c", c=1024)
            dstv = WBF[nm].rearrange("k (a c) -> (k a) c", c=1024)
            step = srcv.shape[0] // CVPARTS[nm]
            a, b = i * step, (i + 1) * step
            prog_.add("pool", lambda e: e.dma_start(out=dstv[a:b, :], in_=srcv[a:b, :]), list(gate), [("cvt", nm, i)], dsem=cvsem[nm][i])

        def bank():
            b = bankc["i"] % 8
            bankc["i"] += 1
            return b

        with ExitStack() as e1:
            def sb1(name, shape, dt=F32):
                return e1.enter_context(nc.sbuf_tensor(name, list(shape), dt))

            def sem1(name):
                return e1.enter_context(nc.semaphore(name))

            prog = Prog()
            sems = {k: sem1("s1_" + k) for k in ("pe", "act", "dve", "pool")}
            wsem = [sem1(f"s1_w{i}") for i in range(NSLOT)]
            ldsem = [sem1(f"s1_ld{i}") for i in range(3)]
            sB = sb1("sB", [128, 5, 512])
            sC = sb1("sC", [128, 7, 512])
            tmps = {}

            def T(name):
                if name not in tmps:
                    tmps[name] = sb1("t_" + name, [128, 512])
                return tmps[name]

            def ap_of(x):
                return T(x)[:] if isinstance(x, str) else x[0]

            def key_of(x):
                return ("t", x) if isinstance(x, str) else x[1]

            def tt(out, a, b, op, eng="dve"):
                o, x, y = ap_of(out), ap_of(a), ap_of(b)
                prog.add(eng, lambda e: e.tensor_tensor(out=o, in0=x, in1=y, op=op),
                         [key_of(a), key_of(b)], [key_of(out)])

            def ts(out, a, s1, s2, op0, op1=None, eng="dve", extra=()):
                o, x = ap_of(out), ap_of(a)
                if op1 is None:
                    prog.add(eng, lambda e: e.tensor_scalar(out=o, in0=x, scalar1=s1, scalar2=None, op0=op0),
                             [key_of(a)] + list(extra), [key_of(out)])
                else:
                    prog.add(eng, lambda e: e.tensor_scalar(out=o, in0=x, scalar1=s1, scalar2=s2, op0=op0, op1=op1),
                             [key_of(a)] + list(extra), [key_of(out)])

            def act(out, a, func, scale=1.0, bias=0.0, extra=()):
                o, x = ap_of(out), ap_of(a)
                prog.add("act", lambda e: e.activation(out=o, in_=x, func=func, bias=bias, scale=scale),
                         [key_of(a)] + list(extra), [key_of(out)])

            prog.add("sp", lambda e: e.dma_start(out=par[:], in_=par_d), [], ["par"], dsem=ldsem[0])
            prog.add("sp", lambda e: e.dma_start(out=sB[:], in_=ssmB_d), [], ["sB"], dsem=ldsem[1])
            prog.add("sp", lambda e: e.dma_start(out=sC[:], in_=ssmC_d), [], ["sC"], dsem=ldsem[2])
            prog.add("dve", lambda e: e.memset(onesb[:], 1.0), [], ["onesb"])
            prog.add("dve", lambda e: e.memset(onesf[:], 1.0), [], ["onesf"])
            prog.add("dve", lambda e: e.memset(zerob[:], 0.0), [], ["zerob"])

            prog.add("act", lambda e: e.activation(out=cab[:], in_=pcol("cvec"), func=AF.Silu),
                     ["par"], ["cab"])
            w_ada_r = w_ada.rearrange("(kt p) c -> p kt c", p=128)
            mb = bank()
            for ch in range(8):
                s = ring["i"] % NSLOT
                ring["i"] += 1
                wv = wslot[s][:, 0:8192].rearrange("p (k c) -> p k c", k=16)
                src = w_ada_r[:, :, ch * 512:(ch + 1) * 512]
                prog.add("pool", lambda e, wv=wv, src=src: e.dma_start(out=wv, in_=src),
                         [], [("w", s)], dsem=wsem[s])
                for ti in range(4):
                    j = ch * 4 + ti

                    def fn(e, wv=wv, ti=ti, j=j):
                        ins = None
                        for kt in range(16):
                            ins = e.matmul(ps[:, mb, j:j + 1], lhsT=wv[:, kt, ti * 128:(ti + 1) * 128],
                                           rhs=cab[:, kt:kt + 1], start=(kt == 0), stop=(kt == 15))
                        return ins
                    prog.add("pe", fn, [("w", s), "cab"], [("ps", mb)])
            for i_ in range(CVPARTS["w_in"]):
                cvt_piece(prog, "w_in", i_)
            prog.add("dve", lambda e: e.tensor_tensor(out=modv[:, 0:32], in0=ps[:, mb, 0:32], in1=pcol("bada", 0, 32), op=ALU.add),
                     [("ps", mb), "par"], ["modv"])
            prog.add("dve", lambda e: e.scalar_tensor_tensor(out=g1s[:], in0=modv[:, 16:32], scalar=1.0, in1=pcol("n1g"),
                                                             op0=ALU.add, op1=ALU.mult), ["modv", "par"], ["g1s"])

            def cmul(outr, outi, ar, ai, br, bi):
                tt("c1", ar, br, ALU.mult)
                tt("c2", ai, bi, ALU.mult)
                tt(outr, "c1", "c2", ALU.subtract)
                tt("c1", ar, bi, ALU.mult)
                tt("c2", ai, br, ALU.mult)
                tt(outi, "c1", "c2", ALU.add)

            def lam_tables(are, aim, ldt):
                act("s0", ldt, AF.Exp)
                tt("s1", are, "s0", ALU.mult)
                act("s1", "s1", AF.Exp)
                tt("s2", aim, "s0", ALU.mult)
                for outn, shift in (("li", 0.0), ("lr", PI / 2)):
                    ts("s3", "s2", shift, None, ALU.add)
                    ts("s4", "s2", shift, None, ALU.add)
                    for k in range(8):
                        thr = (2 * k + 1) * PI
                        ts("s5", "s4", thr, -2 * PI, ALU.is_gt, ALU.mult)
                        tt("s3", "s3", "s5", ALU.add)
                    act("s3", "s3", AF.Sin)
                    tt(outn, "s1", "s3", ALU.mult)
                tt("s0", are, are, ALU.mult)
                tt("s1", aim, aim, ALU.mult)
                tt("s0", "s0", "s1", ALU.add)
                o, x = T("s1")[:], T("s0")[:]
                prog.add("dve", lambda e: e.reciprocal(out=o, in_=x), [("t", "s0")], [("t", "s1")])
                ts("s0", "lr", -1.0, None, ALU.add)
                tt("s2", "s0", are, ALU.mult)
                tt("s3", "li", aim, ALU.mult)
                tt("s2", "s2", "s3", ALU.add)
                tt("fr", "s2", "s1", ALU.mult)
                tt("s2", "li", are, ALU.mult)
                tt("s3", "s0", aim, ALU.mult)
                tt("s2", "s2", "s3", ALU.subtract)
                tt("fi", "s2", "s1", ALU.mult)

            def sBk(i):
                return (sB[:, i, :], "sB")

            def sCk(i):
                return (sC[:, i, :], "sC")

            lam_tables(sBk(0), sBk(1), sBk(2))
            cmul("br", "bi", "fr", "fi", sBk(3), sBk(4))
            cmul("l2r", "l2i", "lr", "li", "lr", "li")
            cmul("l3r", "l3i", "l2r", "l2i", "lr", "li")
            for b in range(4):
                if b == 3:
                    vr, vi = "br", "bi"
                else:
                    pw = {0: ("l3r", "l3i"), 1: ("l2r", "l2i"), 2: ("lr", "li")}[b]
                    cmul("wr", "wi", pw[0], pw[1], "br", "bi")
                    vr, vi = "wr", "wi"
                for plane, vn in ((0, vr), (1, vi)):
                    for j in range(2):
                        o = W2[:, :, b, plane, j * 64:(j + 1) * 64]
                        x = T(vn)[:].rearrange("p (c q) -> p c q", c=8)
                        m = pcol("mB", j, j + 1)
                        prog.add("dve", lambda e, o=o, x=x, m=m: e.tensor_scalar(out=o, in0=x, scalar1=m, scalar2=None,
                                                                                  op0=ALU.mult),
                                 [("t", vn), "par"], ["W2"])

            lam_tables(sCk(0), sCk(1), sCk(2))
            cmul("l2r", "l2i", "lr", "li", "lr", "li")
            cmul("l3r", "l3i", "l2r", "l2i", "lr", "li")
            cmul("l4r", "l4i", "l2r", "l2i", "l2r", "l2i")
            cmul("l8r", "l8i", "l4r", "l4i", "l4r", "l4i")
            cmul("l12r", "l12i", "l8r", "l8i", "l4r", "l4i")
            cmul("l16r", "l16i", "l8r", "l8i", "l8r", "l8i")
            cmul("l32r", "l32i", "l16r", "l16i", "l16r", "l16i")
            cmul("l64r", "l64i", "l32r", "l32i", "l32r", "l32i")
            cmul("l128r", "l128i", "l64r", "l64i", "l64r", "l64i")
            for i, nm in enumerate(("l4", "l8", "l12", "l16", "l32", "l64", "l128")):
                xr = T(nm + "r")[:].rearrange("p (a h) -> p a h", h=16)[:, :, 0]
                xi = T(nm + "i")[:].rearrange("p (a h) -> p a h", h=16)[:, :, 0]
                for o, x, sg, kn in ((CA[:, i, :, 0], xr, 1.0, nm + "r"), (CA[:, i, :, 1], xr, 1.0, nm + "r"),
                                     (CB[:, i, :, 0], xi, -1.0, nm + "i"), (CB[:, i, :, 1], xi, 1.0, nm + "i")):
                    prog.add("dve", lambda e, o=o, x=x, sg=sg: e.tensor_scalar(out=o, in0=x, scalar1=sg, scalar2=None,
                                                                              op0=ALU.mult), [("t", kn)], ["LC"])
            pws = [("lr", "li"), ("l2r", "l2i"), ("l3r", "l3i"), ("l4r", "l4i")]
            for b in range(4):
                cmul("wr", "wi", pws[b][0], pws[b][1], sCk(3), sCk(4))
                for plane, vn, sgn in ((0, "wr", 1.0), (1, "wi", -1.0)):
                    for j in range(2):
                        o = W4[:, :, b, plane, j * 16:(j + 1) * 16]
                        x = T(vn)[:].rearrange("p (a h) -> p a h", h=16)
                        m = pcol("mC", j, j + 1)
                        prog.add("dve", lambda e, o=o, x=x, m=m, sgn=sgn: e.tensor_scalar(
                            out=o, in0=x, scalar1=m, scalar2=sgn, op0=ALU.mult, op1=ALU.mult),
                            [("t", vn), "par"], ["W4"])
            cmul("br", "bi", "fr", "fi", sCk(5), sCk(6))
            Lm = sb1("Lm", [128, 2, 32, 32])
            Rm = sb1("Rm", [128, 2, 32, 32])
            ITf = sb1("ITf", [128, 8, 128])
            for plane, (src, sgn) in enumerate(((sCk(3), 1.0), (sCk(4), -1.0))):
                for j in range(2):
                    o = Rm[:, plane, :, j * 16:(j + 1) * 16]
                    x = src[0].rearrange("p (a h) -> p a h", h=16)
                    m = pcol("mC", j, j + 1)
                    prog.add("dve", lambda e, o=o, x=x, m=m, sgn=sgn: e.tensor_scalar(
                        out=o, in0=x, scalar1=m, scalar2=sgn, op0=ALU.mult, op1=ALU.mult), ["sC", "par"], ["Rm"])
            prog.add("dve", lambda e: e.memset(ITf[:], 0.0), [], ["ITf"])
            prog.add("dve", lambda e: e.memset(IT[:], 0.0), [], ["IT"])
            ident = pcol("ident")
            for lag in range(4):
                if lag == 0:
                    vr, vi = "br", "bi"
                else:
                    cmul("wr", "wi", pws[lag - 1][0], pws[lag - 1][1], "br", "bi")
                    vr, vi = "wr", "wi"
                for plane, vn in ((0, vr), (1, vi)):
                    for j in range(2):
                        o = Lm[:, plane, :, j * 16:(j + 1) * 16]
                        x = T(vn)[:].rearrange("p (a h) -> p a h", h=16)
                        m = pcol("mC", j, j + 1)
                        prog.add("dve", lambda e, o=o, x=x, m=m: e.tensor_scalar(out=o, in0=x, scalar1=m, scalar2=None,
                                                                                  op0=ALU.mult),
                                 [("t", vn), "par"], ["Lm"])
                bk = bank()
                for pt in range(32):
                    ct, q = pt // 4, pt % 4
                    o = ps[32 * q:32 * q + 32, bk, ct * 32:ct * 32 + 32]

                    def fn(e, o=o, pt=pt, q=q):
                        e.matmul(o, lhsT=Lm[:, 0, pt, :], rhs=Rm[:, 0, pt, :], start=True, stop=False,
                                 tile_position=(0, 32 * q))
                        return e.matmul(o, lhsT=Lm[:, 1, pt, :], rhs=Rm[:, 1, pt, :], start=False, stop=True,
                                        tile_position=(0, 32 * q))
                    prog.add("pe", fn, ["Lm", "Rm"], [("ps", bk)])
                for q in range(4):
                    x = ps[32 * q:32 * q + 32, bk, 0:256].rearrange("p (c k) -> p c k", c=8)
                    if lag == 0:
                        o = ITf[32 * q:32 * q + 32, :, 32 * q:32 * q + 32]
                        prog.add("dve", lambda e, o=o, x=x: e.tensor_copy(out=o, in_=x), [("ps", bk)], ["ITf"])
                    else:
                        o = IT[32 * q:32 * q + 32, :, lag, 32 * q:32 * q + 32]
                        prog.add("dve", lambda e, o=o, x=x: e.tensor_copy(out=o, in_=x), [("ps", bk)], ["IT"])
                if lag == 0:
                    for ct in range(8):
                        o = ITf[:, ct, :]
                        d = pcol("dsk", ct, ct + 1)
                        prog.add("dve", lambda e, o=o, d=d: e.scalar_tensor_tensor(out=o, in0=ident, scalar=d, in1=o,
                                                                                  op0=ALU.mult, op1=ALU.add),
                                 ["par", "ITf"], ["ITf"])
                    prog.add("dve", lambda e: e.tensor_copy(out=IT[:, :, 0, :], in_=ITf[:]), ["ITf"], ["IT"])
            prog.add("sp", lambda e: None, ["IT", "W4", "W2", "LC", "g1s", "g2s", "modv", "onesb", "onesf", "zerob"], [])
            with nc.Block() as block:
                prog.emit(block, sems)
            nc.all_engine_barrier()

        with ExitStack() as e2:
            def sb2(name, shape, dt=F32):
                return e2.enter_context(nc.sbuf_tensor(name, list(shape), dt))

            def sem2(name):
                return e2.enter_context(nc.semaphore(name))

            prog = Prog()
            sems = {k: sem2("s2_" + k) for k in ("pe", "act", "dve", "pool")}
            wsem = [sem2(f"s2_w{i}") for i in range(NSLOT)]
            wsemH = [sem2(f"s2_wh{i}") for i in range(NSLOT)]
            hsem = [sem2(f"s2_h{i}") for i in range(16)]
            osem = [sem2(f"s2_o{i}") for i in range(16)]
            h = sb2("h", [128, 16, NT])
            u = sb2("u", [128, 16, NT], BF16)
            v = sb2("v", [128, 8, 32 + NT], BF16)
            vssm = sb2("vssm", [128, 8, NT], BF16)
            R16 = sb2("R16", [128, 16, NT], BF16)
            sAB = sb2("sAB", [128, 8, NT], BF16)
            rstd = sb2("rstd", [128, NT])
            mu = sb2("mu", [128, NT])
            sq = [sb2(f"sq{i}", [128, NT], BF16) for i in range(2)]
            tf = [sb2(f"tf{i}", [128, NT]) for i in range(2)]
            X = sb2("X", [128, 32, MC + 1, 2])
            G = sb2("G", [128, 32, NCH + 1, 2])
            X2 = sb2("X2", [128, 32, MC + 1, 2])
            ZR = sb2("ZR", [128, 32, MC], BF16)
            ZI = sb2("ZI", [128, 32, MC], BF16)
            P1 = sb2("P1", [128, 32, NCH, 2])
            P2 = sb2("P2", [128, 32, NCH, 2])
            uh = sb2("uh", [128, 16, 32], BF16)
            sgh = sb2("sgh", [128, 8, 32], BF16)
            ring["i"] = 0
            NDG = 4
            DG = sb2("DG", [128, NDG, 128], BF16)
            cnt = {"sq": 0, "tf": 0, "dg": 0}

            shift1, gate1 = modv[:, 0:16], modv[:, 32:48]
            shift2, gate2 = modv[:, 48:64], modv[:, 80:96]

            mode = {"bf": False}

            def wload(W, kt0, nkt, c0, ncols):
                s = ring["i"] % NSLOT
                ring["i"] += 1
                wv = wslot[s][:, 0:nkt * ncols].rearrange("p (k c) -> p k c", k=nkt)
                nm = WNAME.get(id(W))
                if mode["bf"] and nm is not None:
                    src = WBF[nm].rearrange("(kt p) c -> p kt c", p=128)[:, kt0:kt0 + nkt, c0:c0 + ncols]
                    prog.add("sp", lambda e: e.dma_start(out=wv, in_=src), cvt_keys[nm], [("w", s)], dsem=wsemH[s])
                else:
                    src = W.rearrange("(kt p) c -> p kt c", p=128)[:, kt0:kt0 + nkt, c0:c0 + ncols]
                    prog.add("pool", lambda e: e.dma_start(out=wv, in_=src), [], [("w", s)], dsem=wsem[s])
                return wv, ("w", s)

            cvt_keys = {nm: [("cvt", nm, i) for i in range(n)] for nm, n in CVPARTS.items()}
            for i_ in range(CVPARTS["w_in"]):
                prog.add("pool", lambda e: None, [], [("cvt", "w_in", i_)], dsem=cvsem["w_in"][i_])
            cvt_plan = [(nm, i_) for nm in ("w_co", "w_gb", "w_ga", "w_out", "w_ff1") for i_ in range(CVPARTS[nm])]

            def cvt_next(gate=()):
                if cvt_plan:
                    nm, i_ = cvt_plan.pop(0)
                    cvt_piece(prog, nm, i_, gate)

            def mmg(out, pairs, reads, writes):
                def fn(e):
                    ins = None
                    n = len(pairs)
                    for i, (l, r) in enumerate(pairs):
                        ins = e.matmul(out, lhsT=l, rhs=r, start=(i == 0), stop=(i == n - 1))
                    return ins
                prog.add("pe", fn, reads, writes)

            def mod_load(chs):
                return [(wload(w_ada, 0, 16, ch * 512, 512), ch) for ch in chs]

            def mod_compute(loaded):
                for (wv, wk), ch in loaded:
                    b = bank()
                    for ti in range(4):
                        def fn(e, wv=wv, ti=ti, b=b):
                            ins = None
                            for kt in range(16):
                                ins = e.matmul(ps[:, b, ti:ti + 1], lhsT=wv[:, kt, ti * 128:(ti + 1) * 128],
                                               rhs=cab[:, kt:kt + 1], start=(kt == 0), stop=(kt == 15))
                            return ins
                        prog.add("pe", fn, [wk], [("ps", b)])
                    j0 = ch * 4
                    prog.add("dve", lambda e, b=b, j0=j0: e.tensor_tensor(out=modv[:, j0:j0 + 4], in0=ps[:, b, 0:4],
                                                                          in1=pcol("bada", j0, j0 + 4), op=ALU.add),
                             [("ps", b)], ["modv2"])
                    if ch == 23:
                        prog.add("dve", lambda e: e.scalar_tensor_tensor(out=g2s[:], in0=modv[:, 64:80], scalar=1.0, in1=pcol("n2g"),
                                                                         op0=ALU.add, op1=ALU.mult), ["modv2"], ["modv2"])

            def rmsnorm_stats():
                b = bank()
                for kt in range(16):
                    i = cnt["sq"] % 2
                    cnt["sq"] += 1
                    sqa = sq[i]
                    prog.add("act", lambda e, sqa=sqa, kt=kt: e.activation(out=sqa[:], in_=h[:, kt, :], func=AF.Square),
                             [("h", kt)], [("sq", i)])
                    prog.add("pe", lambda e, sqa=sqa, kt=kt: e.matmul(ps[:, b, :], lhsT=onesb[:], rhs=sqa[:],
                                                                      start=(kt == 0), stop=(kt == 15)),
                             [("sq", i)], [("ps", b)])
                prog.add("act", lambda e: e.activation(out=rstd[:], in_=ps[:, b, :], func=AF.Sqrt, bias=EPS, scale=1.0 / D),
                         [("ps", b)], ["rstd"])
                prog.add("dve", lambda e: e.reciprocal(out=rstd[:], in_=rstd[:]), ["rstd"], ["rstd"])

            def modulate(gs, sh, extra):
                for kt in range(16):
                    i = cnt["tf"] % 2
                    cnt["tf"] += 1
                    t = tf[i]
                    prog.add("dve", lambda e, t=t, kt=kt: e.tensor_tensor(out=t[:], in0=h[:, kt, :], in1=rstd[:], op=ALU.mult),
                             [("h", kt), "rstd"], [("tf", i)])
                    prog.add("act", lambda e, t=t, kt=kt: e.activation(out=u[:, kt, :], in_=t[:], func=AF.Identity,
                                                                       bias=sh[:, kt:kt + 1], scale=gs[:, kt:kt + 1]),
                             [("tf", i)] + extra, [("u", kt)])

            ukeys = [("u", kt) for kt in range(16)]

            SCAN_ENG = "dve"

            def cmuladd(dst, ci, src, n, kd, ks):
                shp = [128, 32, n, 2]
                ca = CA[:, ci, :, :].unsqueeze(2).to_broadcast(shp)
                cb = CB[:, ci, :, :].unsqueeze(2).to_broadcast(shp)
                p1, p2 = P1[:, :, 0:n, :], P2[:, :, 0:n, :]
                srcsw = src[:, :, :, ::-1]
                prog.add(SCAN_ENG, lambda e: e.tensor_tensor(out=p1, in0=src, in1=ca, op=ALU.mult), [ks], ["P1"])
                prog.add(SCAN_ENG, lambda e: e.tensor_tensor(out=p2, in0=srcsw, in1=cb, op=ALU.mult), [ks], ["P2"])
                prog.add(SCAN_ENG, lambda e: e.tensor_tensor(out=p1, in0=p1, in1=p2, op=ALU.add), ["P1", "P2"], ["P1"])
                prog.add(SCAN_ENG, lambda e: e.tensor_tensor(out=dst, in0=dst, in1=p1, op=ALU.add), ["P1", kd], [kd])

            def Xc(a, n=NCH, step=4):
                return X[:, :, 1 + a:2 + a + step * (n - 1):step, :]

            def ssm_q(sb_i, XB=None, xk="X"):
                XB = X if XB is None else XB
                t0 = sb_i * SUB
                for half in range(2):
                    banks = [bank() for _ in range(4)]
                    for cl in range(4):
                        ct = half * 4 + cl
                        for q in range(4):
                            for plane in range(2):
                                c0 = (cl * 2 + plane) * MC
                                o = ps[:, banks[q], c0:c0 + MC]

                                def fn(e, o=o, ct=ct, q=q, plane=plane):
                                    ins = None
                                    for b in range(4):
                                        ins = e.matmul(o, lhsT=W2[32 * q:32 * q + 32, ct, b, plane, :],
                                                       rhs=vssm[32 * q:32 * q + 32, ct, t0 + b * MC:t0 + (b + 1) * MC],
                                                       start=(b == 0), stop=(b == 3), tile_position=(32 * q, 0))
                                    return ins
                                prog.add("pe", fn, [("vssm", ct, sb_i)], [("ps", banks[q])])
                    for q in range(4):
                        src = ps[:, banks[q], 0:8 * MC].rearrange("p (c l m) -> p c l m", c=4, l=2)
                        for plane in range(2):
                            o = XB[:, half * 16 + q:half * 16 + 16:4, 1:MC + 1, plane]
                            x = src[:, :, plane, :]
                            prog.add("act", lambda e, o=o, x=x: e.activation(out=o, in_=x, func=AF.Identity),
                                     [("ps", banks[q])], [xk])

            def prefix_scan(XB, xk, last):
                for a in range(1, 4):
                    cmuladd(XB[:, :, 1 + a:2 + a + 4 * (NCH - 1):4, :], 0, XB[:, :, a:1 + a + 4 * (NCH - 1):4, :], NCH, xk, xk)
                cmuladd(XB[:, :, 8:33:8, :], 3, XB[:, :, 4:29:8, :], 4, xk, xk)
                cmuladd(XB[:, :, 16:33:16, :], 4, XB[:, :, 8:25:16, :], 2, xk, xk)
                cmuladd(XB[:, :, 32:33, :], 5, XB[:, :, 16:17, :], 1, xk, xk)
                cmuladd(XB[:, :, 32:33, :], 6, G[:, :, 0:1, :], 1, xk, "G")
                if last:
                    fl = pcol("flag")
                    prog.add(SCAN_ENG, lambda e: e.tensor_scalar(out=G[:, :, 0:1, :], in0=XB[:, :, 32:33, :], scalar1=fl,
                                                                 scalar2=None, op0=ALU.mult), [xk], ["G"])
                    prog.add(SCAN_ENG, lambda e: e.tensor_copy(out=X[:, :, 0:1, :], in_=G[:, :, 0:1, :]), ["G"], ["X"])
                else:
                    prog.add(SCAN_ENG, lambda e: e.tensor_copy(out=G[:, :, 0:1, :], in_=XB[:, :, 32:33, :]), [xk], ["G"])

            def ssm_scan(prefix, last_prefix_sub):
                for a in range(1, 4):
                    cmuladd(Xc(a), 0, Xc(a - 1), NCH, "X", "X")
                if prefix:
                    raise AssertionError("use prefix_scan")
                prog.add(SCAN_ENG, lambda e: e.tensor_copy(out=G[:, :, 1:NCH + 1, :], in_=X[:, :, 4:4 * NCH + 1:4, :]),
                         ["X"], ["G"])
                for k in range(4):
                    d = 1 << k
                    cmuladd(G[:, :, d:NCH + 1, :], 3 + k, G[:, :, 0:NCH + 1 - d, :], NCH + 1 - d, "G", "G")
                for a in range(3):
                    cmuladd(Xc(a), a, G[:, :, 0:NCH, :], NCH, "X", "G")
                prog.add(SCAN_ENG, lambda e: e.tensor_copy(out=Xc(3), in_=G[:, :, 1:NCH + 1, :]), ["G"], ["X"])
                for Z, pl in ((ZR, 0), (ZI, 1)):
                    prog.add(SCAN_ENG, lambda e, Z=Z, pl=pl: e.tensor_copy(out=Z[:], in_=X[:, :, 0:MC, pl]), ["X"], ["Z"])
                prog.add(SCAN_ENG, lambda e: e.tensor_copy(out=G[:, :, 0:1, :], in_=G[:, :, NCH:NCH + 1, :]), ["G"], ["G"])
                prog.add(SCAN_ENG, lambda e: e.tensor_copy(out=X[:, :, 0:1, :], in_=G[:, :, 0:1, :]), ["G", "Z"], ["X"])

            def ssm_y(sb_i):
                t0 = sb_i * SUB
                for ct in range(8):
                    b_ = bank()
                    Y = ps[:, b_, 0:SUB]

                    def fn(e, ct=ct, b_=b_):
                        e.matmul(ps[:, b_, 0:SUB], lhsT=zerob[:], rhs=vssm[:, ct, t0:t0 + SUB], start=True, stop=False)
                        for b in range(4):
                            for b2 in range(b + 1):
                                e.matmul(ps[:, b_, b * MC:(b + 1) * MC], lhsT=IT[:, ct, b - b2, :],
                                         rhs=vssm[:, ct, t0 + b2 * MC:t0 + (b2 + 1) * MC], start=False, stop=False)
                        ins = None
                        for q in range(4):
                            pt = ct * 4 + q
                            for b in range(4):
                                for plane, Z in ((0, ZR), (1, ZI)):
                                    last = (b == 3 and plane == 1)
                                    ins = e.matmul(ps[32 * q:32 * q + 32, b_, b * MC:(b + 1) * MC],
                                                   lhsT=W4[:, pt, b, plane, :], rhs=Z[:, pt, :], start=False, stop=last,
                                                   tile_position=(0, 32 * q))
                        return ins
                    prog.add("pe", fn, [("vssm", ct, sb_i), "Z"], [("ps", b_)])
                    o = vssm[:, ct, t0:t0 + SUB].rearrange("p (m b) -> p b m", b=4)
                    Yv = Y.rearrange("p (b m) -> p b m", b=4)
                    prog.add("act", lambda e, o=o, Yv=Yv: e.activation(out=o, in_=Yv, func=AF.Gelu_apprx_tanh),
                             [("ps", b_)], [("vssm", ct, sb_i)])

            def win_tile(wv, wk, ti, evac):
                b = bank()
                wkl = list(wk) if isinstance(wk, list) else [wk]
                mmg(ps[:, b, :], [(wv[:, kt, ti * 128:(ti + 1) * 128], u[:, kt, :]) for kt in range(16)],
                    wkl + ukeys, [("ps", b)])
                evac(b)

            def pre_vssm():
                return [wload(w_in, 0, 16, ch * 512, 512) for ch in (4, 5)]

            def do_vssm(pre):
                ct = 0
                for wv, wk, ntl in pre:
                    for ti in range(ntl):
                        def evac(b, ct=ct):
                            o = vssm[:, ct, :].rearrange("p (s b m) -> p s b m", s=NSUB, b=4)
                            x = ps[:, b, :].rearrange("p (s m b) -> p s b m", s=NSUB, b=4)
                            prog.add("act", lambda e, o=o, x=x: e.activation(out=o, in_=x, func=AF.Identity),
                                     [("ps", b)], [("vssm", ct, s_) for s_ in range(NSUB)])
                        win_tile(wv, wk, ti, evac)
                        ct += 1

            RK = [("R", j) for j in range(16)]
            SK = [("sAB", j) for j in range(8)]
            VK = [("v", c) for c in range(8)] + [("vh", c) for c in range(8)]

            def load_resident():
                w_in_r = w_in.rearrange("(kt p) c -> p kt c", p=128)
                sabw = sAB[:].rearrange("p c t -> p (c t)").rearrange("p (k c) -> p k c", k=16)
                vw = v[:].rearrange("p c t -> p (c t)")[:, 0:4096].rearrange("p (k c) -> p k c", k=16)
                res = []
                for i, (dst, c0, nc_, keys, ntl) in enumerate(((R16[:], 2048, 512, RK, 4), (sabw, 2560, 256, SK, 2),
                                                              (vw, 2816, 256, VK, 2))):
                    sem = sem2(f"s2_res{i}")
                    src = w_in_r[:, :, c0:c0 + nc_]
                    prog.add("pool", lambda e, dst=dst, src=src: e.dma_start(out=dst, in_=src), [], keys, dsem=sem)
                    res.append((dst, keys, ntl))
                return res

            def do_glu(first):
                fl = pcol("flag")
                for ch in (2, 3, 0, 1):
                    wv, wk = wload(w_in, 0, 16, ch * 512, 512)
                    for ti in range(4):
                        ct = (ch % 2) * 4 + ti
                        if ch >= 2:
                            def evac(b, ct=ct):
                                prog.add("act", lambda e: e.activation(out=sAB[:, ct, :], in_=ps[:, b, :], func=AF.Sigmoid),
                                         [("ps", b)], [("sAB", ct)])
                        else:
                            def evac(b, ct=ct):
                                prog.add("dve", lambda e: e.tensor_tensor(out=v[:, ct, 32:32 + NT], in0=ps[:, b, :],
                                                                          in1=sAB[:, ct, :], op=ALU.mult),
                                         [("ps", b), ("sAB", ct)], [("v", ct)])
                        win_tile(wv, wk, ti, evac)
                        if first:
                            b2 = bank()
                            mmg(ps[:, b2, 0:32], [(wv[:, kt, ti * 128:(ti + 1) * 128], uh[:, kt, :]) for kt in range(16)],
                                [wk, "uh"], [("ps", b2)])
                            if ch >= 2:
                                prog.add("act", lambda e, ct=ct, b2=b2: e.activation(out=sgh[:, ct, :], in_=ps[:, b2, 0:32], func=AF.Sigmoid),
                                         [("ps", b2)], [("sgh", ct)])
                            else:
                                prog.add("dve", lambda e, ct=ct, b2=b2: e.scalar_tensor_tensor(out=v[:, ct, 0:32], in0=ps[:, b2, 0:32], scalar=fl,
                                                                                             in1=sgh[:, ct, :], op0=ALU.mult, op1=ALU.mult),
                                         [("ps", b2), ("sgh", ct)], [("vh", ct)])

            def halo_copy(ct, use_flag):
                if use_flag:
                    fl = pcol("flag")
                    prog.add("dve", lambda e: e.tensor_scalar(out=v[:, ct, 0:32], in0=v[:, ct, NT:NT + 32], scalar1=fl,
                                                              scalar2=None, op0=ALU.mult), [("v", ct)], [("vh", ct)])
                else:
                    prog.add("pool", lambda e: e.tensor_copy(out=v[:, ct, 0:32], in_=v[:, ct, NT:NT + 32]),
                             [("v", ct)], [("vh", ct)])

            def conv_ct(ct):
                wdw = pcol("wdw")
                ident = pcol("ident")
                b = bank()
                for k in range(31):
                    i = cnt["dg"] % NDG
                    cnt["dg"] += 1
                    wk_ = wdw[:, ct * 31 + k:ct * 31 + k + 1]
                    prog.add("act", lambda e, i=i, wk_=wk_: e.activation(out=DG[:, i, :], in_=ident, func=AF.Identity, scale=wk_),
                             [], [("dg", i)])
                    src = v[:, ct, 2 + k:2 + k + NT]
                    prog.add("pe", lambda e, i=i, src=src, k=k: e.matmul(ps[:, b, :], lhsT=DG[:, i, :], rhs=src,
                                                                        start=(k == 0), stop=(k == 30)),
                             [("dg", i), ("v", ct), ("vh", ct)], [("ps", b)])
                bd = pcol("bdw", ct, ct + 1)
                prog.add("act", lambda e: e.activation(out=sAB[:, ct, :], in_=ps[:, b, :], func=AF.Identity, bias=bd),
                         [("ps", b)], [("sAB", ct)])
                halo_copy(ct, False)

            def do_ln():
                b1, b2 = bank(), bank()
                for ct in range(8):
                    rk = [("sAB", ct)]
                    x = sAB[:, ct, :]
                    prog.add("pe", lambda e, x=x, ct=ct: e.matmul(ps[:, b1, :], lhsT=onesb[:], rhs=x, start=(ct == 0), stop=(ct == 7)),
                             rk, [("ps", b1)])
                    i = cnt["sq"] % 2
                    cnt["sq"] += 1
                    t = sq[i]
                    prog.add("act", lambda e, x=x, t=t: e.activation(out=t[:], in_=x, func=AF.Square), rk, [("sq", i)])
                    prog.add("pe", lambda e, t=t, ct=ct: e.matmul(ps[:, b2, :], lhsT=onesb[:], rhs=t[:], start=(ct == 0), stop=(ct == 7)),
                             [("sq", i)], [("ps", b2)])
                prog.add("dve", lambda e: e.tensor_scalar(out=mu[:], in0=ps[:, b1, :], scalar1=1.0 / 1024, scalar2=None, op0=ALU.mult),
                         [("ps", b1)], ["mu"])
                prog.add("dve", lambda e: e.tensor_tensor(out=rstd[:], in0=mu[:], in1=mu[:], op=ALU.mult), ["mu"], ["rstd"])
                prog.add("dve", lambda e: e.scalar_tensor_tensor(out=rstd[:], in0=ps[:, b2, :], scalar=1.0 / 1024, in1=rstd[:],
                                                                 op0=ALU.mult, op1=ALU.subtract), [("ps", b2), "rstd"], ["rstd"])
                prog.add("act", lambda e: e.activation(out=rstd[:], in_=rstd[:], func=AF.Sqrt, bias=EPS, scale=1.0), ["rstd"], ["rstd"])
                prog.add("dve", lambda e: e.reciprocal(out=rstd[:], in_=rstd[:]), ["rstd"], ["rstd"])
                for ct in range(8):
                    rk = [("sAB", ct)]
                    i = cnt["tf"] % 2
                    cnt["tf"] += 1
                    t = tf[i]
                    x = sAB[:, ct, :]
                    prog.add("dve", lambda e, x=x, t=t: e.tensor_tensor(out=t[:], in0=x, in1=mu[:], op=ALU.subtract),
                             rk + ["mu"], [("tf", i)])
                    prog.add("dve", lambda e, t=t: e.tensor_tensor(out=t[:], in0=t[:], in1=rstd[:], op=ALU.mult),
                             [("tf", i), "rstd"], [("tf", i)])
                    lg, lb = pcol("lng", ct, ct + 1), pcol("lnb", ct, ct + 1)
                    prog.add("act", lambda e, t=t, ct=ct, lg=lg, lb=lb: e.activation(out=v[:, ct, 32:32 + NT], in_=t[:], func=AF.Silu,
                                                                                    bias=lb, scale=lg),
                             [("tf", i), ("vh", ct)], [("v", ct)])

            def merge_p1(chs, pre):
                for ch in chs:
                    wv, wk = pre
                    for ti in range(4):
                        j = (ch - 6) * 4 + ti

                        def evac(b, j=j):
                            prog.add("act", lambda e: e.activation(out=R16[:, j, :], in_=ps[:, b, :], func=AF.Sigmoid),
                                     [("ps", b)], [("R", j)])
                        win_tile(wv, wk, ti, evac)

            def merge_p23():
                for ch in range(2):
                    wv, wk = wload(w_co, 0, 8, ch * 1024, 1024)
                    for ti in range(8):
                        j = ch * 8 + ti
                        b = bank()
                        mmg(ps[:, b, :], [(wv[:, kt, ti * 128:(ti + 1) * 128], v[:, kt, 32:32 + NT]) for kt in range(8)],
                            [wk] + [("v", kt) for kt in range(8)], [("ps", b)])
                        prog.add("dve", lambda e, j=j, b=b: e.tensor_tensor(out=R16[:, j, :], in0=ps[:, b, :], in1=R16[:, j, :],
                                                                            op=ALU.mult), [("ps", b), ("R", j)], [("R", j)])
                skeys = [("vssm", kt, s_) for kt in range(8) for s_ in range(NSUB)]
                for jb in range(4):
                    wv, wk = wload(w_in, 0, 16, (10 + jb) * 512, 512)
                    for ti in range(4):
                        def evac(b, ti=ti):
                            prog.add("act", lambda e: e.activation(out=sAB[:, ti, :], in_=ps[:, b, :], func=AF.Sigmoid),
                                     [("ps", b)], [("sAB", ti)])
                        win_tile(wv, wk, ti, evac)
                    for which, W in ((0, w_gb), (1, w_ga)):
                        wv, wk = wload(W, 0, 8, jb * 512, 512)
                        for ti in range(4):
                            j = jb * 4 + ti
                            b = bank()
                            mmg(ps[:, b, :], [(wv[:, kt, ti * 128:(ti + 1) * 128], vssm[:, kt, :]) for kt in range(8)],
                                [wk] + skeys, [("ps", b)])
                            kB = ("sAB", 4 + ti)
                            if which == 0:
                                prog.add("act", lambda e, ti=ti, b=b: e.activation(out=sAB[:, 4 + ti, :], in_=ps[:, b, :], func=AF.Sigmoid),
                                         [("ps", b)], [kB])
                            else:
                                prog.add("dve", lambda e, ti=ti, b=b: e.tensor_tensor(out=sAB[:, 4 + ti, :], in0=ps[:, b, :], in1=sAB[:, 4 + ti, :],
                                                                                      op=ALU.mult), [("ps", b), kB], [kB])
                                prog.add("dve", lambda e, ti=ti: e.tensor_tensor(out=sAB[:, 4 + ti, :], in0=sAB[:, 4 + ti, :], in1=sAB[:, ti, :],
                                                                                 op=ALU.mult), [kB, ("sAB", ti)], [kB])
                                prog.add("dve", lambda e, ti=ti, j=j: e.tensor_tensor(out=R16[:, j, :], in0=R16[:, j, :], in1=sAB[:, 4 + ti, :],
                                                                                      op=ALU.add), [kB, ("R", j)], [("R", j)])

            def do_wout():
                for ch in range(4):
                    wv, wk = wload(w_out, 0, 16, ch * 512, 512)
                    for ti in range(4):
                        j = ch * 4 + ti
                        b = bank()
                        mmg(ps[:, b, :], [(wv[:, kt, ti * 128:(ti + 1) * 128], R16[:, kt, :]) for kt in range(16)],
                            [wk] + [("R", kt) for kt in range(16)], [("ps", b)])
                        prog.add("dve", lambda e, j=j, b=b: e.scalar_tensor_tensor(out=h[:, j, :], in0=ps[:, b, :], scalar=gate1[:, j:j + 1],
                                                                                   in1=h[:, j, :], op0=ALU.mult, op1=ALU.add),
                                 [("ps", b), ("h", j), "modv2"], [("h", j)])

            def do_ffn():
                for hb in range(4):
                    for c4 in range(4):
                        wv, wk = wload(w_ff1, 0, 16, (hb * 4 + c4) * 512, 512)
                        for ti in range(4):
                            i_ = c4 * 4 + ti

                            def evac(b, i_=i_):
                                k = cnt["sq"] % 2
                                cnt["sq"] += 1
                                s_ = sq[k]
                                prog.add("act", lambda e: e.activation(out=s_[:], in_=ps[:, b, :], func=AF.Relu), [("ps", b)], [("sq", k)])
                                prog.add("dve", lambda e: e.tensor_tensor(out=R16[:, i_, :], in0=ps[:, b, :], in1=s_[:], op=ALU.mult),
                                         [("ps", b), ("sq", k)], [("R", i_)])
                            win_tile(wv, wk, ti, evac)
                    for cb in range(4):
                        wv, wk = wload(w_ff2, hb * 16, 16, cb * 512, 512)
                        for ti in range(4):
                            j = cb * 4 + ti
                            b = bank()
                            mmg(ps[:, b, :], [(wv[:, kt, ti * 128:(ti + 1) * 128], R16[:, kt, :]) for kt in range(16)],
                                [wk] + [("R", kt) for kt in range(16)], [("ps", b)])
                            prog.add("dve", lambda e, j=j, b=b: e.scalar_tensor_tensor(out=h[:, j, :], in0=ps[:, b, :], scalar=gate2[:, j:j + 1],
                                                                                       in1=h[:, j, :], op0=ALU.mult, op1=ALU.add),
                                     [("ps", b), ("h", j)], [("h", j)])

            hkeys = [("h", kt) for kt in range(16)]
            prog.add("dve", lambda e: e.memset(G[:], 0.0), [], ["G"])
            prog.add("dve", lambda e: e.memset(X[:], 0.0), [], ["X"])
            prog.add("dve", lambda e: e.memset(X2[:], 0.0), [], ["X2"])
            prog.add("dve", lambda e: e.memset(v[:], 0.0), [], [("v", c) for c in range(8)] + [("vh", c) for c in range(8)])

            steps = [("p", i) for i in range(NTILE)] + [("m", i) for i in range(NTILE)]
            resw = load_resident()
            for kind, ti_ in steps:
                srcT = xpT if kind == "p" else xT
                src = srcT.rearrange("(kt p) t -> p kt t", p=128)[:, :, ti_ * NT:(ti_ + 1) * NT]
                for kt in range(16):
                    prog.add("sp", lambda e, kt=kt, src=src: e.dma_start(out=h[:, kt, :], in_=src[:, kt, :]), [], [("h", kt)],
                             dsem=hsem[kt])
                rmsnorm_stats()
                modulate(g1s, shift1, [])
                if kind == "p":
                    ml = mod_load([8 + 2 * ti_, 9 + 2 * ti_])
                    do_vssm(resw)
                    for s_i in range(NSUB):
                        XB, xk = (X, "X") if s_i % 2 == 0 else (X2, "X2")
                        ssm_q(s_i, XB, xk)
                        cvt_next([xk])
                        prefix_scan(XB, xk, ti_ == NTILE - 1 and s_i == NSUB - 1)
                    mod_compute(ml)
                    if ti_ == NTILE - 1:
                        prog.add("act", lambda e: e.activation(out=uh[:], in_=u[:, :, NT - 32:NT], func=AF.Identity), ukeys, ["uh"])
                    continue
                while cvt_plan:
                    cvt_next()
                mode["bf"] = True
                do_glu(ti_ == 0)
                do_vssm([(wv_, wk_, 4) for wv_, wk_ in pre_vssm()])
                for s_i in range(NSUB):
                    p1w = wload(w_in, 0, 16, (6 + s_i) * 512, 512)
                    ssm_q(s_i)
                    ssm_scan(False, False)
                    conv_ct(2 * s_i)
                    conv_ct(2 * s_i + 1)
                    merge_p1([6 + s_i], p1w)
                    if ti_ == 0:
                        mod_compute(mod_load([16 + 2 * s_i, 17 + 2 * s_i]))
                    ssm_y(s_i)
                do_ln()
                merge_p23()
                do_wout()
                rmsnorm_stats()
                modulate(g2s, shift2, ["modv2"])
                do_ffn()
                rmsnorm_stats()
                fgc = pcol("fg")
                dst = outT.rearrange("(kt p) t -> p kt t", p=128)[:, :, ti_ * NT:(ti_ + 1) * NT]
                for kt in range(16):
                    prog.add("dve", lambda e, kt=kt: e.scalar_tensor_tensor(out=h[:, kt, :], in0=h[:, kt, :], scalar=fgc[:, kt:kt + 1],
                                                                            in1=rstd[:], op0=ALU.mult, op1=ALU.mult),
                             [("h", kt), "rstd"], [("h", kt)])
                    prog.add("sp", lambda e, kt=kt, dst=dst: e.dma_start(out=dst[:, kt, :], in_=h[:, kt, :]), [("h", kt)],
                             [("out", ti_, kt)], dsem=osem[kt])
            prog.add("sp", lambda e: None, [("out", i, kt) for i in range(NTILE) for kt in range(16)], [])
            with nc.Block() as block:
                prog.emit(block, sems)
    return nc


_NC = None


def _prep(x, c, w_ada, b_ada, norm1_g, w_in, w_dw, b_dw, ln_g, ln_b, w_conv_out, a_re, a_im, log_dt, b_re, b_im,
          c_re, c_im, d_skip, w_glu_a, w_glu_b, w_out, norm2_g, w_ff1, w_ff2, final_g):
    f = np.float32
    shared = {
        "w_ada": np.ascontiguousarray(w_ada[0], f), "w_in": np.ascontiguousarray(w_in[0], f),
        "w_conv_out": np.ascontiguousarray(w_conv_out[0], f), "w_glu_a": np.ascontiguousarray(w_glu_a[0], f),
        "w_glu_b": np.ascontiguousarray(w_glu_b[0], f), "w_out": np.ascontiguousarray(w_out[0], f),
        "w_ff1": np.ascontiguousarray(w_ff1[0], f), "w_ff2": np.ascontiguousarray(w_ff2[0], f),
    }
    are, aim, ldt = a_re[0], a_im[0], log_dt[0]
    bre, bim, cre, cim = b_re[0], b_im[0], c_re[0], c_im[0]
    def Bl_a(a):
        t = a.reshape(8, 8, 64).transpose(1, 0, 2)
        return np.broadcast_to(t[:, None], (8, 16, 8, 64)).reshape(128, 512)
    ldB = np.broadcast_to(ldt.reshape(8, 8).T[:, None, :, None], (8, 16, 8, 64)).reshape(128, 512)
    def Bl_b(b):
        return b.reshape(8, 8, 64, 16).transpose(1, 3, 0, 2).reshape(128, 512)
    ssmB = np.stack([Bl_a(are), Bl_a(aim), ldB, Bl_b(bre), Bl_b(bim)], axis=1).astype(f)
    def Cl_a(a):
        t = a.reshape(32, 2, 64).transpose(1, 2, 0)
        return np.broadcast_to(t[..., None], (2, 64, 32, 16)).reshape(128, 512)
    ldC = np.broadcast_to(ldt.reshape(32, 2).T[:, None, :, None], (2, 64, 32, 16)).reshape(128, 512)
    def Cl_c(cc):
        return cc.reshape(32, 2, 16, 64).transpose(1, 3, 0, 2).reshape(128, 512)
    def Cl_b(b):
        return b.reshape(32, 2, 64, 16).transpose(1, 2, 0, 3).reshape(128, 512)
    ssmC = np.stack([Cl_a(are), Cl_a(aim), ldC, Cl_c(cre), Cl_c(cim), Cl_b(bre), Cl_b(bim)], axis=1).astype(f)
    shared["ssmB"] = np.ascontiguousarray(ssmB)
    shared["ssmC"] = np.ascontiguousarray(ssmC)

    def fm(vec, n):
        return np.asarray(vec, f).reshape(n, 128).T
    par_base = np.zeros((128, NPAR), f)
    def put(name, arr):
        o, w = PC[name]
        par_base[:, o:o + w] = arr
    put("bada", fm(b_ada[0], 96))
    put("n1g", fm(norm1_g[0], 16)); put("n2g", fm(norm2_g[0], 16)); put("fg", fm(final_g, 16))
    put("wdw", np.asarray(w_dw[0], f).reshape(31, 8, 128).transpose(2, 1, 0).reshape(128, 248))
    put("bdw", fm(b_dw[0], 8)); put("lng", fm(ln_g[0], 8)); put("lnb", fm(ln_b[0], 8)); put("dsk", fm(d_skip[0], 8))
    pidx = np.arange(128)
    mB = np.stack([((pidx // 16) % 2 == j) for j in range(2)], axis=1).astype(f)
    mC = np.stack([((pidx // 64) == j) for j in range(2)], axis=1).astype(f)
    put("mB", mB); put("mC", mC); put("ident", np.eye(128, dtype=f))
    in_maps = []
    zeros = np.zeros((D, NTOK), f)
    for core in range(NCORE):
        b, s = core // 2, core % 2
        m = dict(shared)
        m["xT"] = np.ascontiguousarray(x[b, s * NTOK:(s + 1) * NTOK, :].T)
        m["xpT"] = np.ascontiguousarray(x[b, 0:NTOK, :].T) if s == 1 else zeros
        p = par_base.copy()
        o, w = PC["cvec"]; p[:, o:o + w] = fm(c[b], 16)
        o, w = PC["flag"]; p[:, o:o + w] = float(s)
        m["par"] = p
        in_maps.append(m)
    return in_maps


def kernel(**inputs):
    global _NC
    inputs = {k: np.asarray(v) for k, v in inputs.items()}
    in_maps = _prep(**inputs)
    if _NC is None:
        _NC = build()
    res = run_bass_kernel_spmd(_NC, in_maps, core_ids=list(range(NCORE)))
    out = np.empty((4, 4096, D), np.float32)
    for core in range(NCORE):
        b, s = core // 2, core % 2
        out[b, s * NTOK:(s + 1) * NTOK, :] = res.results[core]["outT"].T
    return out
```
